# Optimizing a Trainium2 kernel written in Bass

```python
import math
import jax, jax.numpy as jnp
from jax import lax
import numpy as np


D_MODEL = 1024
BATCH = 8
SEQ = 8192
DEPTH = 2

CTX_LEN = 256
GRID_W = 64
D_MIX = D_MODEL
CONV_CH = D_MIX // 2
CONV_K = 31
QK_NOPE = 64
QK_ROPE = 32
V_DIM = 64
MLA_HEADS = (D_MIX - CONV_CH) // V_DIM
QK_DIM = QK_NOPE + QK_ROPE
Q_LORA = 384
KV_LORA = 256
ROPE_THETA = 10000.0
SHORT_K = 3
FFN_DIM = 2816
FFN_K = 3
Q_BLOCK = 128
EPS = 1e-6
N_MOD = 6
EVEN_IN = 2 * CONV_CH + Q_LORA + KV_LORA + QK_ROPE
ODD_IN = 3 * D_MIX
SM_SCALE = QK_DIM ** -0.5

kernel_name = "hybrid_conformer_mla_shortconv_dit"


def rms_norm(x, g):
    xf = x.astype(jnp.float32)
    y = xf * lax.rsqrt(jnp.mean(xf * xf, axis=-1, keepdims=True) + EPS)
    return (y * g.astype(jnp.float32)).astype(x.dtype)


def layer_norm(x, g, b):
    xf = x.astype(jnp.float32)
    mu = jnp.mean(xf, axis=-1, keepdims=True)
    xc = xf - mu
    y = xc * lax.rsqrt(jnp.mean(xc * xc, axis=-1, keepdims=True) + EPS)
    return (y * g.astype(jnp.float32) + b.astype(jnp.float32)).astype(x.dtype)


def modulate(h, shift, scale):
    return h * (1 + scale) + shift


def dwconv(x, w, b):
    y = lax.conv_general_dilated(
        x, w[:, None, :].astype(x.dtype), window_strides=(1,), padding='SAME',
        dimension_numbers=('NWC', 'WIO', 'NWC'), feature_group_count=x.shape[-1])
    return y + b.astype(x.dtype)


def axial_tables(n_tokens):
    rows = n_tokens // GRID_W
    row = jnp.broadcast_to(jnp.arange(rows)[:, None], (rows, GRID_W)).reshape(-1)
    col = jnp.broadcast_to(jnp.arange(GRID_W)[None, :], (rows, GRID_W)).reshape(-1)
    half = QK_ROPE // 2
    inv = ROPE_THETA ** (-jnp.arange(0, half, 2, dtype=jnp.float32) / half)

    def cos_sin(pos):
        ang = pos.astype(jnp.float32)[:, None] * inv[None, :]
        ang = jnp.concatenate([ang, ang], axis=-1)
        return jnp.cos(ang), jnp.sin(ang)

    return cos_sin(row), cos_sin(col)


def rotate_half(v):
    v1, v2 = jnp.split(v, 2, axis=-1)
    return jnp.concatenate([-v2, v1], axis=-1)


def apply_axial_rope(x, tables):
    (cr, sr), (cc, sc) = tables
    half = QK_ROPE // 2

    def rot(v, cs, sn):
        vf = v.astype(jnp.float32)
        out = vf * cs[None, :, None, :] + rotate_half(vf) * sn[None, :, None, :]
        return out.astype(v.dtype)

    return jnp.concatenate([rot(x[..., :half], cr, sr), rot(x[..., half:], cc, sc)], axis=-1)


def split_even(proj):
    o1 = 2 * CONV_CH
    o2 = o1 + Q_LORA
    o3 = o2 + KV_LORA
    return proj[..., :o1], proj[..., o1:o2], proj[..., o2:o3], proj[..., o3:]


def conformer_conv(glu_in, conv_w, conv_b, ln_g, ln_b):
    val, gate = jnp.split(glu_in, 2, axis=-1)
    u = dwconv(val * jax.nn.sigmoid(gate), conv_w, conv_b)
    return jax.nn.silu(layer_norm(u, ln_g, ln_b))


def mla_q(cq, qa_g, w_uq, q_g, tables):
    b, l = cq.shape[:2]
    q = (rms_norm(cq, qa_g) @ w_uq).reshape(b, l, MLA_HEADS, QK_DIM)
    q = rms_norm(q, q_g)
    if tables is not None:
        q = jnp.concatenate([q[..., :QK_NOPE], apply_axial_rope(q[..., QK_NOPE:], tables)], axis=-1)
    return q


def mla_kv(ckv, kr, kva_g, w_ukv, k_g, tables):
    b, l = ckv.shape[:2]
    kv = (rms_norm(ckv, kva_g) @ w_ukv).reshape(b, l, MLA_HEADS, QK_NOPE + V_DIM)
    k_nope, v = kv[..., :QK_NOPE], kv[..., QK_NOPE:]
    k_rope = jnp.broadcast_to(kr[:, :, None, :], (b, l, MLA_HEADS, QK_ROPE))
    k = rms_norm(jnp.concatenate([k_nope, k_rope], axis=-1), k_g)
    if tables is not None:
        k = jnp.concatenate([k[..., :QK_NOPE], apply_axial_rope(k[..., QK_NOPE:], tables)], axis=-1)
    return k, v


def attend_blocks(q, k, v):
    b, s, h, dq = q.shape
    nb = s // Q_BLOCK
    qb = q.reshape(b, nb, Q_BLOCK, h, dq).transpose(1, 0, 2, 3, 4)

    def one_block(qi):
        sc = jnp.einsum('bqhd,bkhd->bhqk', qi, k, preferred_element_type=jnp.float32) * SM_SCALE
        p = jax.nn.softmax(sc, axis=-1).astype(v.dtype)
        return jnp.einsum('bhqk,bkhd->bqhd', p, v)

    o = lax.map(one_block, qb)
    return o.transpose(1, 0, 2, 3, 4).reshape(b, s, h * v.shape[-1])


def short_conv_mixer(h, w_in, conv_w, conv_b, w_out):
    bg, cg, u = jnp.split(h @ w_in, 3, axis=-1)
    return (bg * dwconv(cg * u, conv_w, conv_b)) @ w_out


def conv_ffn(h, w_up, conv_w, conv_b, w_down):
    gate, val = jnp.split(h @ w_up, 2, axis=-1)
    return (jax.nn.silu(dwconv(gate, conv_w, conv_b)) * val) @ w_down


def setup_inputs(seed: int = 0) -> dict:
    key = jax.random.key(seed)
    ks = iter(jax.random.split(key, 40))
    f32 = jnp.float32
    D = D_MODEL
    ne = (DEPTH + 1) // 2
    no = DEPTH // 2

    def nrm(shape, scale):
        return scale * jax.random.normal(next(ks), shape, f32)

    def gain(shape):
        return 1.0 + 0.02 * jax.random.normal(next(ks), shape, f32)

    return {
        "x": nrm((BATCH, SEQ, D), 1.0),
        "c": nrm((BATCH, D), 1.0),
        "ctx": nrm((BATCH, CTX_LEN, D), 1.0),
        "c_ctx": nrm((D,), 1.0),
        "ada_w": nrm((DEPTH, D, N_MOD * D), 0.5 * D ** -0.5),
        "ada_b": nrm((DEPTH, N_MOD * D), 0.01),
        "norm_mix_g": gain((DEPTH, D)),
        "norm_ffn_g": gain((DEPTH, D)),
        "ffn_w_up": nrm((DEPTH, D, 2 * FFN_DIM), D ** -0.5),
        "ffn_conv_w": nrm((DEPTH, FFN_K, FFN_DIM), FFN_K ** -0.5),
        "ffn_conv_b": nrm((DEPTH, FFN_DIM), 0.01),
        "ffn_w_down": nrm((DEPTH, FFN_DIM, D), FFN_DIM ** -0.5),
        "ev_w_in": nrm((ne, D, EVEN_IN), D ** -0.5),
        "ev_conv_w": nrm((ne, CONV_K, CONV_CH), CONV_K ** -0.5),
        "ev_conv_b": nrm((ne, CONV_CH), 0.01),
        "ev_ln_g": gain((ne, CONV_CH)),
        "ev_ln_b": nrm((ne, CONV_CH), 0.01),
        "ev_qa_norm_g": gain((ne, Q_LORA)),
        "ev_w_uq": nrm((ne, Q_LORA, MLA_HEADS * QK_DIM), Q_LORA ** -0.5),
        "ev_kva_norm_g": gain((ne, KV_LORA)),
        "ev_w_ukv": nrm((ne, KV_LORA, MLA_HEADS * (QK_NOPE + V_DIM)), KV_LORA ** -0.5),
        "ev_q_norm_g": gain((ne, QK_DIM)),
        "ev_k_norm_g": gain((ne, QK_DIM)),
        "ev_w_out": nrm((ne, D_MIX, D), D_MIX ** -0.5),
        "od_w_in": nrm((no, D, ODD_IN), D ** -0.5),
        "od_conv_w": nrm((no, SHORT_K, D_MIX), SHORT_K ** -0.5),
        "od_conv_b": nrm((no, D_MIX), 0.01),
        "od_w_out": nrm((no, D_MIX, D), D_MIX ** -0.5),
    }


def reference(x, c, ctx, c_ctx, ada_w, ada_b, norm_mix_g, norm_ffn_g, ffn_w_up, ffn_conv_w, ffn_conv_b,
              ffn_w_down, ev_w_in, ev_conv_w, ev_conv_b, ev_ln_g, ev_ln_b, ev_qa_norm_g, ev_w_uq,
              ev_kva_norm_g, ev_w_ukv, ev_q_norm_g, ev_k_norm_g, ev_w_out, od_w_in, od_conv_w, od_conv_b,
              od_w_out):
    tables = axial_tables(x.shape[1])
    silu_c = jax.nn.silu(c)
    silu_cc = jax.nn.silu(c_ctx)
    xc = ctx

    for layer in range(DEPTH):
        i = layer // 2
        is_even = layer % 2 == 0
        update_ctx = any(j % 2 == 0 for j in range(layer + 1, DEPTH))

        mod_lat = (silu_c @ ada_w[layer] + ada_b[layer])[:, None, :]
        sh_m, sc_m, g_m, sh_f, sc_f, g_f = jnp.split(mod_lat, N_MOD, axis=-1)
        if is_even or update_ctx:
            mod_ctx = (silu_cc @ ada_w[layer] + ada_b[layer])[None, None, :]
            csh_m, csc_m, cg_m, csh_f, csc_f, cg_f = jnp.split(mod_ctx, N_MOD, axis=-1)
            hc = modulate(rms_norm(xc, norm_mix_g[layer]), csh_m, csc_m)

        h = modulate(rms_norm(x, norm_mix_g[layer]), sh_m, sc_m)

        if is_even:
            glu_c, cq_c, ckv_c, kr_c = split_even(hc @ ev_w_in[i])
            k_c, v_c = mla_kv(ckv_c, kr_c, ev_kva_norm_g[i], ev_w_ukv[i], ev_k_norm_g[i], None)

            glu, cq, ckv, kr = split_even(h @ ev_w_in[i])
            a = conformer_conv(glu, ev_conv_w[i], ev_conv_b[i], ev_ln_g[i], ev_ln_b[i])
            q = mla_q(cq, ev_qa_norm_g[i], ev_w_uq[i], ev_q_norm_g[i], tables)
            k, v = mla_kv(ckv, kr, ev_kva_norm_g[i], ev_w_ukv[i], ev_k_norm_g[i], tables)
            k_all = jnp.concatenate([k_c, k], axis=1)
            v_all = jnp.concatenate([v_c, v], axis=1)
            att = attend_blocks(q, k_all, v_all)
            x = x + g_m * (jnp.concatenate([a, att], axis=-1) @ ev_w_out[i])

            if update_ctx:
                a_c = conformer_conv(glu_c, ev_conv_w[i], ev_conv_b[i], ev_ln_g[i], ev_ln_b[i])
                q_c = mla_q(cq_c, ev_qa_norm_g[i], ev_w_uq[i], ev_q_norm_g[i], None)
                att_c = attend_blocks(q_c, k_c, v_c)
                xc = xc + cg_m * (jnp.concatenate([a_c, att_c], axis=-1) @ ev_w_out[i])
        else:
            x = x + g_m * short_conv_mixer(h, od_w_in[i], od_conv_w[i], od_conv_b[i], od_w_out[i])
            if update_ctx:
                xc = xc + cg_m * short_conv_mixer(hc, od_w_in[i], od_conv_w[i], od_conv_b[i], od_w_out[i])

        hf = modulate(rms_norm(x, norm_ffn_g[layer]), sh_f, sc_f)
        x = x + g_f * conv_ffn(hf, ffn_w_up[layer], ffn_conv_w[layer], ffn_conv_b[layer], ffn_w_down[layer])
        if update_ctx:
            hfc = modulate(rms_norm(xc, norm_ffn_g[layer]), csh_f, csc_f)
            xc = xc + cg_f * conv_ffn(hfc, ffn_w_up[layer], ffn_conv_w[layer], ffn_conv_b[layer], ffn_w_down[layer])

    return x
```

```python
import contextlib
import math
import numpy as np
import ml_dtypes
import concourse.bass as bass
import concourse.mybir as mybir
from concourse.bass_utils import run_bass_kernel_spmd

F32 = mybir.dt.float32
BF16 = mybir.dt.bfloat16
AF = mybir.ActivationFunctionType
ALU = mybir.AluOpType
AX = mybir.AxisListType

D = 1024
CTX = 256
CONV_CH = 512
CONV_K = 31
NH = 8
QK = 96
VD = 64
QL = 384
KVL = 256
EVEN_IN = 1696
FFN = 2816
EPS = 1e-6
SM_SCALE = QK ** -0.5
WSTR = 510


class Eng:
    def __init__(self, name, h, sem):
        self.name = name
        self.h = h
        self.sem = sem
        self.n = 0
        self.seen = {}


class SemC:
    def __init__(self, sem):
        self.sem = sem
        self.n = 0


class View:
    __slots__ = ("buf", "ap")

    def __init__(self, buf, ap):
        self.buf = buf
        self.ap = ap


class Buf:
    def __init__(self, K, t):
        self.K = K
        self.t = t
        self.cw = {}
        self.cr = {}
        self.pw = {}
        self.pr = {}
        self.has_read = False
        self.ld = None
        self.st = None

    def __getitem__(self, key):
        return View(self, self.t[key])

    def v(self, ap):
        return View(self, ap)


class Kern:
    def __init__(self, nc):
        self.nc = nc
        self.es = contextlib.ExitStack()
        mk = lambda n, h: Eng(n, h, self.es.enter_context(nc.semaphore("sem_" + n)))
        self.pe = mk("pe", nc.tensor)
        self.act = mk("act", nc.scalar)
        self.dve = mk("dve", nc.vector)
        self.pool = mk("pool", nc.gpsimd)
        self.sp = mk("sp", nc.sync)
        self.engs = [self.pe, self.act, self.dve, self.pool, self.sp]
        self.free_sems = [SemC(self.es.enter_context(nc.semaphore("dsem%d" % i))) for i in range(88)]
        self.live = []
        self.setup_sem = self.free_sems.pop()
        self.uid = 0

    def sb(self, st, shape, dt, name=None):
        self.uid += 1
        t = st.enter_context(self.nc.sbuf_tensor("%s_%d" % (name or "sb", self.uid), list(shape), dt))
        return Buf(self, t)

    def ps(self, st, shape, dt, name=None):
        self.uid += 1
        t = st.enter_context(self.nc.psum_tensor("%s_%d" % (name or "ps", self.uid), list(shape), dt))
        return Buf(self, t)

    def _need(self, e, sem, val):
        key = id(sem)
        if e.seen.get(key, 0) >= val:
            return
        e.h.wait_ge(sem, val)
        e.seen[key] = val

    def _dep(self, e, b, tag, idx, raw):
        if tag == "LD":
            self._need(e, b.ld.sem, idx)
        elif tag == "ST":
            self._need(e, b.st.sem, idx)
        else:
            if tag is e and not raw:
                return
            self._need(e, tag.sem, idx + 1)

    def _rdeps(self, e, b):
        for tag, idx in b.cw.items():
            self._dep(e, b, tag, idx, True)

    def _wdeps(self, e, b, also_cur=False):
        if b.has_read:
            b.pr, b.pw, b.cr, b.cw, b.has_read = b.cr, b.cw, {}, {}, False
        for tag, idx in b.pr.items():
            self._dep(e, b, tag, idx, False)
        for tag, idx in b.pw.items():
            self._dep(e, b, tag, idx, False)
        if also_cur:
            for tag, idx in b.cw.items():
                self._dep(e, b, tag, idx, False)

    def issue(self, e, reads, writes, fn, sig=True, wait_cur=False):
        rb = []
        for v in reads:
            if isinstance(v, View) and v.buf not in rb:
                rb.append(v.buf)
        wb = []
        for v in writes:
            if v.buf not in wb:
                wb.append(v.buf)
        for b in rb:
            self._rdeps(e, b)
        for b in wb:
            self._wdeps(e, b, also_cur=wait_cur)
        ins = fn()
        idx = e.n
        if sig:
            ins.then_inc(e.sem, 1)
            e.n += 1
        for b in rb:
            if b not in wb:
                b.cr[e] = idx
                b.has_read = True
        for b in wb:
            b.cw[e] = idx
        return ins

    def _getsem(self, b, which):
        s = getattr(b, which)
        if s is None:
            s = self.free_sems.pop()
            setattr(b, which, s)
            if b not in self.live:
                self.live.append(b)
        return s

    def dma(self, q, out, in_, **kw):
        nc = self.nc
        if isinstance(out, View):
            b = out.buf
            self._wdeps(q, b, also_cur=True)
            s = self._getsem(b, "ld")
            q.h.dma_start(out=out.ap, in_=in_, **kw).then_inc(s.sem, 16)
            s.n += 16
            b.cw["LD"] = s.n
        else:
            b = in_.buf
            self._rdeps(q, b)
            s = self._getsem(b, "st")
            q.h.dma_start(out=out, in_=in_.ap, **kw).then_inc(s.sem, 16)
            s.n += 16
            b.cr["ST"] = s.n
            b.has_read = True

    def cload(self, out, in_, q=None, **kw):
        q = q or self.sp
        kw.setdefault("allow_slow_non_contiguous", True)
        s = self.setup_sem
        q.h.dma_start(out=out.ap, in_=in_, **kw).then_inc(s.sem, 16)
        s.n += 16

    def setup_done(self):
        for e in self.engs:
            self._need(e, self.setup_sem.sem, self.setup_sem.n)

    def barrier(self):
        for e in self.engs:
            for f in self.engs:
                if f is not e and f.n > 0:
                    self._need(e, f.sem, f.n)
            for b in self.live:
                for s in (b.ld, b.st):
                    if s is not None and s.n > 0:
                        self._need(e, s.sem, s.n)
            self._need(e, self.setup_sem.sem, self.setup_sem.n)
        for b in self.live:
            for w in ("ld", "st"):
                s = getattr(b, w)
                if s is not None:
                    self.free_sems.append(s)
                    setattr(b, w, None)
            b.cw, b.cr, b.pw, b.pr, b.has_read = {}, {}, {}, {}, False
        self.live = []

    @staticmethod
    def _a(v):
        return v.ap if isinstance(v, View) else v

    def mm(self, out, lhsT, rhs, start, stop, sig=None):
        if sig is None:
            sig = stop
        return self.issue(self.pe, [lhsT, rhs], [out],
                          lambda: self.nc.tensor.matmul(out.ap, lhsT.ap, rhs.ap, start=start, stop=stop), sig)

    def tr(self, out, in_, ident, sig=True):
        return self.issue(self.pe, [in_, ident], [out],
                          lambda: self.nc.tensor.transpose(out.ap, in_.ap, ident.ap), sig)

    def actf(self, out, in_, func, scale=None, bias=None, accum=None):
        kw = {}
        rd = [in_]
        wr = [out]
        if scale is not None:
            kw["scale"] = self._a(scale)
            rd.append(scale)
        if bias is not None:
            kw["bias"] = self._a(bias)
            rd.append(bias)
        if accum is not None:
            kw["accum_out"] = accum.ap
            wr.append(accum)
        return self.issue(self.act, rd, wr, lambda: self.nc.scalar.activation(out.ap, in_.ap, func, **kw))

    def _ve(self, e):
        return self.nc.vector if e is self.dve else self.nc.gpsimd

    def tt(self, e, out, in0, in1, op):
        return self.issue(e, [in0, in1], [out], lambda: self._ve(e).tensor_tensor(out.ap, in0.ap, in1.ap, op))

    def ts(self, e, out, in0, s1, s2, op0, op1=None):
        rd = [in0, s1, s2]
        if op1 is None:
            return self.issue(e, rd, [out], lambda: self._ve(e).tensor_scalar(out.ap, in0.ap, self._a(s1), None, op0))
        return self.issue(e, rd, [out],
                          lambda: self._ve(e).tensor_scalar(out.ap, in0.ap, self._a(s1), self._a(s2), op0, op1))

    def stt(self, out, in0, s, in1, op0, op1):
        return self.issue(self.dve, [in0, s, in1], [out],
                          lambda: self.nc.vector.scalar_tensor_tensor(out.ap, in0.ap, self._a(s), in1.ap, op0, op1))

    def cp(self, e, out, in_):
        if e is self.act:
            return self.issue(e, [in_], [out], lambda: self.nc.scalar.copy(out.ap, in_.ap))
        return self.issue(e, [in_], [out], lambda: self._ve(e).tensor_copy(out.ap, in_.ap))

    def recip(self, out, in_):
        return self.issue(self.dve, [in_], [out], lambda: self.nc.vector.reciprocal(out.ap, in_.ap))

    def red(self, out, in_, op=None):
        return self.issue(self.dve, [in_], [out],
                          lambda: self.nc.vector.tensor_reduce(out.ap, in_.ap, AX.X, op or ALU.add))

    def memset(self, e, out, val, wait_cur=False):
        return self.issue(e, [], [out], lambda: self._ve(e).memset(out.ap, val), wait_cur=wait_cur)

    def rsqrt(self, out, in_, mul, eps):
        self.ts(self.dve, out, in_, mul, eps, ALU.mult, ALU.add)
        self.actf(out, out, AF.Sqrt)
        self.recip(out, out)


def bc(ap, shape):
    return ap.broadcast_to(list(shape))


def load_cast(K, st, dst, src, nk, F, colscale=None, rowscale=None, stg=None):
    CH = 1024
    i = 0
    for k in range(nk):
        for c0 in range(0, F, CH):
            w = min(CH, F - c0)
            s = stg[i % len(stg)]
            K.dma(K.sp, s[:, 0:w], src[k * 128:(k + 1) * 128, c0:c0 + w])
            e = K.dve if i % 2 == 0 else K.pool
            d = dst[:, k, c0:c0 + w]
            if colscale is not None:
                K.tt(e, d, s[:, 0:w], colscale[:, c0:c0 + w], ALU.mult)
            elif rowscale is not None:
                K.ts(e, d, s[:, 0:w], rowscale[:, k:k + 1], None, ALU.mult)
            else:
                K.cp(e, d, s[:, 0:w])
            i += 1


def colvec(ap1d, nk):
    return ap1d.rearrange("(k p o) -> p k o", p=128, o=1)


def norm_block(K, A, x_rows, tiles, gs, sh, ident, hT, hb, xt, ss, rstd, pT, do_norm_now=True):
    nt = len(tiles)
    for (i, p_lo, p_hi, tok_lo, w) in tiles:
        x = xt[i % len(xt)]
        if p_lo > 0 or p_hi < 128:
            K.memset(K.dve, x[:, :], 0.0)
        K.dma(K.sp, x[p_lo:p_hi, :], x_rows[tok_lo:tok_lo + (p_hi - p_lo), :])
        K.actf(A["junk"][:, :], x[:, :], AF.Square, accum=ss[:, i:i + 1])
    K.rsqrt(rstd[:, 0:nt], ss[:, 0:nt], 1.0 / D, EPS)
    for (i, p_lo, p_hi, tok_lo, w) in tiles:
        x = xt[i % len(xt)]
        K.ts(K.dve, hb[i][:, :], x[:, :], rstd[:, i:i + 1], None, ALU.mult)


def transposes_block(K, tiles, gs, sh, ident, hT, hb, pT, zero_cols):
    for (i, p_lo, p_hi, tok_lo, w) in tiles:
        p = pT[i % len(pT)]
        for k in range(8):
            K.tr(p[:, k, :], hb[i][:, k * 128:(k + 1) * 128], ident[:, :], sig=(k == 7))
        for k in range(8):
            K.actf(hT[:, k, i * 128:i * 128 + w], p[:, k, 0:w], AF.Identity,
                   scale=gs[:, k:k + 1], bias=sh[:, k:k + 1])
    for c in zero_cols:
        K.memset(K.dve, hT[:, :, c:c + 1], 0.0, wait_cur=True)


def win_blocks(S):
    blocks = []
    j = 0
    while WSTR * j < S:
        nvalid = min(WSTR, S - WSTR * j)
        t0 = WSTR * j - 1
        Wd = nvalid + 2
        tiles = []
        for i in range((Wd + 127) // 128):
            a = t0 + 128 * i
            tok_lo = max(a, 0)
            tok_hi = min(a + 128, S, t0 + Wd)
            w = min(128, Wd - 128 * i)
            tiles.append((i, tok_lo - a, tok_hi - a, tok_lo, w))
        zero_cols = []
        if t0 < 0:
            zero_cols.append(0)
        if t0 + Wd - 1 >= S:
            zero_cols.append(Wd - 1)
        blocks.append(dict(j=j, t0=t0, Wd=Wd, nvalid=nvalid, tiles=tiles, zero_cols=zero_cols))
        j += 1
    return blocks


def load_mod_cols(K, st, modrow, sec_shift, sec_scale, normg, name):
    gs = K.sb(st, [128, 8], F32, name + "gs")
    sh = K.sb(st, [128, 8], F32, name + "sh")
    ng = K.sb(st, [128, 8], F32, name + "ng")
    K.cload(gs.v(gs.t[:, :].rearrange("p (k o) -> p k o", o=1)), colvec(modrow[sec_scale * D:(sec_scale + 1) * D], 8))
    K.cload(sh.v(sh.t[:, :].rearrange("p (k o) -> p k o", o=1)), colvec(modrow[sec_shift * D:(sec_shift + 1) * D], 8))
    K.cload(ng.v(ng.t[:, :].rearrange("p (k o) -> p k o", o=1)), colvec(normg, 8))
    K.setup_done()
    K.stt(gs[:, :], gs[:, :], 1.0, ng[:, :], ALU.add, ALU.mult)
    return gs, sh


def phase_mod(K, I, modv, modc):
    nc = K.nc
    with contextlib.ExitStack() as st:
        cT = K.sb(st, [128, 8], F32, "cT")
        ccT = K.sb(st, [128, 8], F32, "ccT")
        ab = K.sb(st, [1, 2 * 6 * D], F32, "ab")
        stg = [K.sb(st, [128, 8, 512], F32, "adastg") for _ in range(2)]
        row = [K.sb(st, [1, 512], F32, "modrow") for _ in range(2)]
        pm = [K.ps(st, [128, 512], F32, "pm") for _ in range(2)]
        K.cload(cT.v(cT.t[:, :].rearrange("p (k o) -> p k o", o=1)), colvec(I["c"], 8))
        K.cload(ccT.v(ccT.t[:, :].rearrange("p (k o) -> p k o", o=1)), colvec(I["c_ctx"], 8))
        K.cload(ab[:, :], I["ada_b"].rearrange("(o l) f -> o (l f)", o=1))
        K.setup_done()
        K.actf(cT[:, :], cT[:, :], AF.Silu)
        K.actf(ccT[:, :], ccT[:, :], AF.Silu)
        it = 0
        for layer in range(2):
            for g in range(12):
                s = stg[it % 2]
                K.dma(K.sp, s[:, :, :], I["ada_w"][layer, :, g * 512:(g + 1) * 512].rearrange("(k p) f -> p k f", p=128))
                srcs = [(cT, modv[layer:layer + 1, g * 512:(g + 1) * 512])]
                if layer == 0 and g < 4:
                    srcs.append((ccT, modc[0:1, g * 512:(g + 1) * 512]))
                for (vec, dst) in srcs:
                    p = pm[it % 2]
                    r = row[it % 2]
                    for k in range(8):
                        K.mm(p[0:1, :], vec[:, k:k + 1], s[:, k, :], start=(k == 0), stop=(k == 7))
                    K.tt(K.dve, r[:, :], p[0:1, :], ab[:, layer * 6 * D + g * 512: layer * 6 * D + (g + 1) * 512], ALU.add)
                    K.dma(K.pool, dst, r[:, :])
                    it += 1
        K.barrier()


def phase_ffn(K, I, layer, S, x_in, x_out, modrow):
    nc = K.nc
    NM = FFN // 128
    with contextlib.ExitStack() as st:
        wup = K.sb(st, [128, 8, 2 * FFN], BF16, "wup")
        wdn = K.sb(st, [128, NM, D], BF16, "wdn")
        ident = K.sb(st, [128, 128], BF16, "ident")
        cw = K.sb(st, [128, NM, 3], F32, "cw")
        cb = K.sb(st, [128, NM], F32, "cb")
        K.cload(ident[:, :], I["ident"])
        for kk in range(3):
            K.cload(cw.v(cw.t[:, :, kk:kk + 1]), colvec(I["ffn_conv_w"][layer, kk, :], NM), allow_slow_non_contiguous=True)
        K.cload(cb.v(cb.t[:, :].rearrange("p (k o) -> p k o", o=1)), colvec(I["ffn_conv_b"][layer, :], NM))
        gs, sh = load_mod_cols(K, st, modrow, 3, 4, I["norm_ffn_g"][layer, :], "f")
        with contextlib.ExitStack() as st2:
            G = K.sb(st2, [128, D], F32, "G")
            stg = [K.sb(st2, [128, 1024], F32, "wstg") for _ in range(3)]
            K.cload(G[:, :], modrow[5 * D:6 * D].partition_broadcast(128))
            K.setup_done()
            load_cast(K, st2, wup, I["ffn_w_up"][layer], 8, 2 * FFN, stg=stg)
            load_cast(K, st2, wdn, I["ffn_w_down"][layer], NM, D, colscale=G, stg=stg)
            K.barrier()
        hid = K.sb(st, [128, NM, 512], BF16, "hid")
        hT = K.sb(st, [128, 8, 512], BF16, "hT")
        hb = [K.sb(st, [128, D], BF16, "hb") for _ in range(4)]
        xt = [K.sb(st, [128, D], F32, "xt") for _ in range(4)]
        xr = [K.sb(st, [128, D], F32, "xr") for _ in range(2)]
        acc = [K.sb(st, [128, 512], F32, "acc") for _ in range(2)]
        sl = [K.sb(st, [128, 512], BF16, "sl") for _ in range(2)]
        A = {"junk": K.sb(st, [128, D], BF16, "junk")}
        ss = K.sb(st, [128, 4], F32, "ss")
        rstd = K.sb(st, [128, 4], F32, "rstd")
        pT = [K.ps(st, [128, 8, 128], BF16, "pT") for _ in range(2)]
        pg = [K.ps(st, [128, 512], F32, "pg") for _ in range(2)]
        pv = [K.ps(st, [128, 512], F32, "pv") for _ in range(2)]
        po = [K.ps(st, [128, 512], F32, "po") for _ in range(2)]
        blocks = win_blocks(S)

        def pre_norm(b):
            norm_block(K, A, x_in, b["tiles"], gs, sh, ident, hT, hb, xt, ss, rstd, pT)

        def pre_tr(b):
            transposes_block(K, b["tiles"], gs, sh, ident, hT, hb, pT, b["zero_cols"])

        def up(b):
            Wd = b["Wd"]
            n = Wd - 2
            for m in range(NM):
                g_, v_ = pg[m % 2], pv[m % 2]
                for k in range(8):
                    K.mm(g_[:, 0:Wd], wup[:, k, m * 128:(m + 1) * 128], hT[:, k, 0:Wd], start=(k == 0), stop=(k == 7))
                for k in range(8):
                    K.mm(v_[:, 0:Wd], wup[:, k, FFN + m * 128:FFN + (m + 1) * 128], hT[:, k, 0:Wd], start=(k == 0), stop=(k == 7))
                a = acc[m % 2]
                K.ts(K.dve, a[:, 0:n], g_[:, 0:n], cw[:, m, 0:1], None, ALU.mult)
                K.stt(a[:, 0:n], g_[:, 1:n + 1], cw[:, m, 1:2], a[:, 0:n], ALU.mult, ALU.add)
                K.stt(a[:, 0:n], g_[:, 2:n + 2], cw[:, m, 2:3], a[:, 0:n], ALU.mult, ALU.add)
                s_ = sl[m % 2]
                K.actf(s_[:, 0:n], a[:, 0:n], AF.Silu, bias=cb[:, m:m + 1])
                K.tt(K.dve, hid[:, m, 0:n], s_[:, 0:n], v_[:, 1:n + 1], ALU.mult)

        def down(b, cnt):
            nv = b["nvalid"]
            tok0 = WSTR * b["j"]
            for i2 in range((nv + 127) // 128):
                r = min(128, nv - 128 * i2)
                x = xr[cnt[0] % 2]
                cnt[0] += 1
                K.dma(K.sp, x[0:r, :], x_in[tok0 + 128 * i2: tok0 + 128 * i2 + r, :])
                for n_ in range(2):
                    for k in range(NM):
                        K.mm(po[n_][0:r, :], hid[:, k, 128 * i2:128 * i2 + r], wdn[:, k, n_ * 512:(n_ + 1) * 512],
                             start=(k == 0), stop=(k == NM - 1))
                for n_ in range(2):
                    K.tt(K.dve, x[0:r, n_ * 512:(n_ + 1) * 512], x[0:r, n_ * 512:(n_ + 1) * 512], po[n_][0:r, :], ALU.add)
                K.dma(K.pool, x_out[tok0 + 128 * i2: tok0 + 128 * i2 + r, :], x[0:r, :])

        cnt = [0]
        pre_norm(blocks[0])
        pre_tr(blocks[0])
        for bi, b in enumerate(blocks):
            nxt = blocks[bi + 1] if bi + 1 < len(blocks) else None
            if nxt:
                pre_norm(nxt)
            up(b)
            if nxt:
                pre_tr(nxt)
            down(b, cnt)
        K.barrier()


def phase_sconv(K, I, S, x_in, x_out, modrow):
    nc = K.nc
    with contextlib.ExitStack() as st:
        win = K.sb(st, [128, 8, 3 * D], BF16, "odwin")
        wout = K.sb(st, [128, 8, D], BF16, "odwout")
        ident = K.sb(st, [128, 128], BF16, "ident")
        cw = K.sb(st, [128, 8, 3], F32, "cw")
        cb = K.sb(st, [128, 8], F32, "cb")
        K.cload(ident[:, :], I["ident"])
        for kk in range(3):
            K.cload(cw.v(cw.t[:, :, kk:kk + 1]), colvec(I["od_conv_w"][0, kk, :], 8), allow_slow_non_contiguous=True)
        K.cload(cb.v(cb.t[:, :].rearrange("p (k o) -> p k o", o=1)), colvec(I["od_conv_b"][0, :], 8))
        gs, sh = load_mod_cols(K, st, modrow, 0, 1, I["norm_mix_g"][1, :], "m1")
        with contextlib.ExitStack() as st2:
            G = K.sb(st2, [128, D], F32, "G")
            stg = [K.sb(st2, [128, 1024], F32, "wstg") for _ in range(3)]
            K.cload(G[:, :], modrow[2 * D:3 * D].partition_broadcast(128))
            K.setup_done()
            load_cast(K, st2, win, I["od_w_in"][0], 8, 3 * D, stg=stg)
            load_cast(K, st2, wout, I["od_w_out"][0], 8, D, colscale=G, stg=stg)
            K.barrier()
        yT = K.sb(st, [128, 8, 512], BF16, "yT")
        hT = K.sb(st, [128, 8, 512], BF16, "hT")
        hb = [K.sb(st, [128, D], BF16, "hb") for _ in range(4)]
        xt = [K.sb(st, [128, D], F32, "xt") for _ in range(4)]
        xr = [K.sb(st, [128, D], F32, "xr") for _ in range(2)]
        acc = [K.sb(st, [128, 512], F32, "acc") for _ in range(2)]
        us = [K.sb(st, [128, 512], F32, "us") for _ in range(2)]
        A = {"junk": K.sb(st, [128, D], BF16, "junk")}
        ss = K.sb(st, [128, 4], F32, "ss")
        rstd = K.sb(st, [128, 4], F32, "rstd")
        pT = [K.ps(st, [128, 8, 128], BF16, "pT") for _ in range(1)]
        pb = [K.ps(st, [128, 512], F32, "pb") for _ in range(2)]
        pc = [K.ps(st, [128, 512], F32, "pc") for _ in range(2)]
        pu = [K.ps(st, [128, 512], F32, "pu") for _ in range(1)]
        po = [K.ps(st, [128, 512], F32, "po") for _ in range(2)]
        blocks = win_blocks(S)

        def mix(b):
            Wd = b["Wd"]
            n = Wd - 2
            for m in range(8):
                b_, c_, u_ = pb[m % 2], pc[m % 2], pu[0]
                for (dst, off) in ((u_, 2 * D), (c_, D), (b_, 0)):
                    for k in range(8):
                        K.mm(dst[:, 0:Wd], win[:, k, off + m * 128: off + (m + 1) * 128], hT[:, k, 0:Wd],
                             start=(k == 0), stop=(k == 7))
                u = us[m % 2]
                K.cp(K.act, u[:, 0:Wd], u_[:, 0:Wd])
                K.tt(K.dve, u[:, 0:Wd], u[:, 0:Wd], c_[:, 0:Wd], ALU.mult)
                a = acc[m % 2]
                K.ts(K.dve, a[:, 0:n], u[:, 0:n], cw[:, m, 0:1], cb[:, m:m + 1], ALU.mult, ALU.add)
                K.stt(a[:, 0:n], u[:, 1:n + 1], cw[:, m, 1:2], a[:, 0:n], ALU.mult, ALU.add)
                K.stt(a[:, 0:n], u[:, 2:n + 2], cw[:, m, 2:3], a[:, 0:n], ALU.mult, ALU.add)
                K.tt(K.dve, yT[:, m, 0:n], a[:, 0:n], b_[:, 1:n + 1], ALU.mult)

        def outp(b, cnt):
            nv = b["nvalid"]
            tok0 = WSTR * b["j"]
            for i2 in range((nv + 127) // 128):
                r = min(128, nv - 128 * i2)
                x = xr[cnt[0] % 2]
                cnt[0] += 1
                K.dma(K.sp, x[0:r, :], x_in[tok0 + 128 * i2: tok0 + 128 * i2 + r, :])
                for n_ in range(2):
                    for k in range(8):
                        K.mm(po[n_][0:r, :], yT[:, k, 128 * i2:128 * i2 + r], wout[:, k, n_ * 512:(n_ + 1) * 512],
                             start=(k == 0), stop=(k == 7))
                for n_ in range(2):
                    K.tt(K.dve, x[0:r, n_ * 512:(n_ + 1) * 512], x[0:r, n_ * 512:(n_ + 1) * 512], po[n_][0:r, :], ALU.add)
                K.dma(K.pool, x_out[tok0 + 128 * i2: tok0 + 128 * i2 + r, :], x[0:r, :])

        cnt = [0]
        norm_block(K, A, x_in, blocks[0]["tiles"], gs, sh, ident, hT, hb, xt, ss, rstd, pT)
        transposes_block(K, blocks[0]["tiles"], gs, sh, ident, hT, hb, pT, blocks[0]["zero_cols"])
        for bi, b in enumerate(blocks):
            nxt = blocks[bi + 1] if bi + 1 < len(blocks) else None
            if nxt:
                norm_block(K, A, x_in, nxt["tiles"], gs, sh, ident, hT, hb, xt, ss, rstd, pT)
            mix(b)
            if nxt:
                transposes_block(K, nxt["tiles"], gs, sh, ident, hT, hb, pT, nxt["zero_cols"])
            outp(b, cnt)
        K.barrier()


def phase_l0a(K, I, S, modrow, modc, uT, qT, kT, vE):
    nc = K.nc
    NK = S + CTX
    with contextlib.ExitStack() as st:
        win = K.sb(st, [128, 8, EVEN_IN], BF16, "evwin")
        wuq = K.sb(st, [128, 3, NH * QK], BF16, "wuq")
        wukv = K.sb(st, [128, 2, NH * 128], BF16, "wukv")
        ident = K.sb(st, [128, 128], BF16, "ident")
        QG = K.sb(st, [128, QK], F32, "QG")
        KG = K.sb(st, [128, QK], F32, "KG")
        rope = K.sb(st, [128, S // 128, 64], F32, "rope")
        qag = K.sb(st, [128, 3], F32, "qag")
        kvag = K.sb(st, [128, 2], F32, "kvag")
        K.cload(ident[:, :], I["ident"])
        K.cload(QG[:, :], I["ev_q_norm_g"][0, :].partition_broadcast(128))
        K.cload(KG[:, :], I["ev_k_norm_g"][0, :].partition_broadcast(128))
        K.cload(rope[:, :, :], I["rope"].rearrange("(i p) c -> p i c", p=128))
        K.cload(qag.v(qag.t[:, :].rearrange("p (k o) -> p k o", o=1)), colvec(I["ev_qa_norm_g"][0, :], 3))
        K.cload(kvag.v(kvag.t[:, :].rearrange("p (k o) -> p k o", o=1)), colvec(I["ev_kva_norm_g"][0, :], 2))
        gs, sh = load_mod_cols(K, st, modrow, 0, 1, I["norm_mix_g"][0, :], "m0")
        gsc, shc = load_mod_cols(K, st, modc, 0, 1, I["norm_mix_g"][0, :], "m0c")
        with contextlib.ExitStack() as st2:
            stg = [K.sb(st2, [128, 1024], F32, "wstg") for _ in range(3)]
            load_cast(K, st2, win, I["ev_w_in"][0], 8, EVEN_IN, stg=stg)
            load_cast(K, st2, wuq, I["ev_w_uq"][0], 3, NH * QK, rowscale=qag, stg=stg)
            load_cast(K, st2, wukv, I["ev_w_ukv"][0], 2, NH * 128, rowscale=kvag, stg=stg)
            K.barrier()
        hT = K.sb(st, [128, 8, 512], BF16, "hT")
        hb = [K.sb(st, [128, D], BF16, "hb") for _ in range(4)]
        xt = [K.sb(st, [128, D], F32, "xt") for _ in range(4)]
        A = {"junk": K.sb(st, [128, D], BF16, "junk")}
        ss = K.sb(st, [128, 4], F32, "ss")
        rstd = K.sb(st, [128, 4], F32, "rstd")
        th = [K.sb(st, [128, 512], F32, "th") for _ in range(2)]
        ust = [K.sb(st, [128, 512], F32, "ust") for _ in range(2)]
        ccT = K.sb(st, [128, 5, 512], BF16, "ccT")
        krs = K.sb(st, [128, 32], F32, "krs")
        st2t = K.sb(st, [128, 4], F32, "st2")
        r2 = K.sb(st, [128, 2], F32, "r2")
        qf = K.sb(st, [128, NH, QK], F32, "qf")
        kvf = K.sb(st, [128, NH, 128], F32, "kvf")
        sq = K.sb(st, [128, NH, QK], F32, "sq")
        ssh = K.sb(st, [128, 2 * NH], F32, "ssh")
        rh = K.sb(st, [128, 2 * NH], F32, "rh")
        R = K.sb(st, [128, NH, 32], F32, "R")
        T1 = K.sb(st, [128, NH, 32], F32, "T1")
        U = K.sb(st, [128, NH, 32], F32, "U")
        qb = [K.sb(st, [128, NH, QK], BF16, "qb") for _ in range(2)]
        kb = [K.sb(st, [128, NH, QK], BF16, "kb") for _ in range(2)]
        vb = [K.sb(st, [128, NH, VD + 1], BF16, "vb") for _ in range(2)]
        qTs = [K.sb(st, [128, NH, 512], BF16, "qTs") for _ in range(2)]
        kTs = [K.sb(st, [128, NH, 512], BF16, "kTs") for _ in range(2)]
        for v_ in vb:
            K.memset(K.dve, v_[:, :, VD:VD + 1], 1.0)
        pT = [K.ps(st, [128, 8, 128], BF16, "pT")]
        pA = [K.ps(st, [128, 512], F32, "pA") for _ in range(2)]
        pQ = [K.ps(st, [128, 512], F32, "pQ") for _ in range(2)]
        pTq = K.ps(st, [128, 8, 128], BF16, "pTq")
        pTk = K.ps(st, [128, 8, 128], BF16, "pTk")
        blks = [dict(ctx=True, x=I["ctx"], t0=0, nt=CTX // 128, key0=0)]
        for j in range(S // 512):
            blks.append(dict(ctx=False, x=I["x"], t0=512 * j, nt=4, key0=CTX + 512 * j))
        tcount = 0
        for bi, b in enumerate(blks):
            nt = b["nt"]
            Wd = nt * 128
            isctx = b["ctx"]
            g_, s_ = (gsc, shc) if isctx else (gs, sh)
            tiles = [(i, 0, 128, b["t0"] + 128 * i, 128) for i in range(nt)]
            norm_block(K, A, b["x"], tiles, g_, s_, ident, hT, hb, xt, ss, rstd, pT)
            transposes_block(K, tiles, g_, s_, ident, hT, hb, pT, [])
            pi = 0
            if not isctx:
                for m in range(4):
                    pv_, pg_ = pA[0], pA[1]
                    for k in range(8):
                        K.mm(pv_[:, 0:Wd], win[:, k, m * 128:(m + 1) * 128], hT[:, k, 0:Wd], start=(k == 0), stop=(k == 7))
                    for k in range(8):
                        K.mm(pg_[:, 0:Wd], win[:, k, 512 + m * 128:512 + (m + 1) * 128], hT[:, k, 0:Wd], start=(k == 0), stop=(k == 7))
                    t_ = th[m % 2]
                    K.actf(t_[:, :], pg_[:, :], AF.Tanh, scale=0.5)
                    u_ = ust[m % 2]
                    K.stt(u_[:, :], t_[:, :], 1.0, pv_[:, :], ALU.add, ALU.mult)
                    K.dma(K.pool, uT[m * 128:(m + 1) * 128, b["t0"]:b["t0"] + 512], u_[:, :])
            for m in range(5):
                if isctx and m < 3:
                    continue
                p_ = pA[m % 2]
                for k in range(8):
                    K.mm(p_[:, 0:Wd], win[:, k, 1024 + m * 128:1024 + (m + 1) * 128], hT[:, k, 0:Wd], start=(k == 0), stop=(k == 7))
                K.cp(K.dve, ccT[:, m, 0:Wd], p_[:, 0:Wd])
            qs, ks = qTs[bi % 2], kTs[bi % 2]
            for i in range(nt):
                cs = slice(i * 128, (i + 1) * 128)
                p1, p2 = pA[0], pA[1]
                for k in range(8):
                    K.mm(p1[:, :], hT[:, k, cs], win[:, k, 1024:1536], start=(k == 0), stop=(k == 7))
                for k in range(8):
                    K.mm(p2[:, 0:160], hT[:, k, cs], win[:, k, 1536:1696], start=(k == 0), stop=(k == 7))
                J = A["junk"]
                K.actf(J[:, 0:384], p1[:, 0:384], AF.Square, accum=st2t[:, 0:1])
                K.actf(J[:, 0:128], p1[:, 384:512], AF.Square, accum=st2t[:, 1:2])
                K.actf(J[:, 0:128], p2[:, 0:128], AF.Square, accum=st2t[:, 2:3])
                K.actf(krs[:, :], p2[:, 128:160], AF.Identity)
                K.actf(J[:, 0:32], p2[:, 128:160], AF.Square, accum=st2t[:, 3:4])
                K.tt(K.dve, st2t[:, 1:2], st2t[:, 1:2], st2t[:, 2:3], ALU.add)
                K.ts(K.dve, r2[:, 0:1], st2t[:, 0:1], 1.0 / QL, EPS, ALU.mult, ALU.add)
                K.ts(K.dve, r2[:, 1:2], st2t[:, 1:2], 1.0 / KVL, EPS, ALU.mult, ALU.add)
                K.actf(r2[:, :], r2[:, :], AF.Sqrt)
                K.recip(r2[:, :], r2[:, :])
                qfl = qf.v(qf.t[:, :, :].rearrange("p h d -> p (h d)"))
                kvfl = kvf.v(kvf.t[:, :, :].rearrange("p h d -> p (h d)"))
                if not isctx:
                    pq0, pq1 = pQ[0], pQ[1]
                    for k in range(3):
                        K.mm(pq0[:, :], ccT[:, k, cs], wuq[:, k, 0:512], start=(k == 0), stop=(k == 2))
                    for k in range(3):
                        K.mm(pq1[:, 0:256], ccT[:, k, cs], wuq[:, k, 512:768], start=(k == 0), stop=(k == 2))
                    K.actf(qf.v(qfl.ap[:, 0:512]), pq0[:, :], AF.Identity, scale=r2[:, 0:1])
                    K.actf(qf.v(qfl.ap[:, 512:768]), pq1[:, 0:256], AF.Identity, scale=r2[:, 0:1])
                    K.tt(K.dve, sq[:, :, :], qf[:, :, :], qf[:, :, :], ALU.mult)
                    K.red(ssh[:, 0:NH], sq[:, :, :])
                pk0, pk1 = pQ[0], pQ[1]
                for n_, pk in enumerate((pk0, pk1)):
                    for k in range(2):
                        K.mm(pk[:, :], ccT[:, 3 + k, cs], wukv[:, k, n_ * 512:(n_ + 1) * 512], start=(k == 0), stop=(k == 1))
                K.actf(kvf.v(kvfl.ap[:, 0:512]), pk0[:, :], AF.Identity, scale=r2[:, 1:2])
                K.actf(kvf.v(kvfl.ap[:, 512:1024]), pk1[:, :], AF.Identity, scale=r2[:, 1:2])
                K.tt(K.dve, sq[:, :, 0:64], kvf[:, :, 0:64], kvf[:, :, 0:64], ALU.mult)
                K.red(ssh[:, NH:2 * NH], sq[:, :, 0:64])
                K.ts(K.dve, ssh[:, NH:2 * NH], ssh[:, NH:2 * NH], st2t[:, 3:4], None, ALU.add)
                lo = NH if isctx else 0
                K.ts(K.dve, rh[:, lo:2 * NH], ssh[:, lo:2 * NH], 1.0 / QK, EPS, ALU.mult, ALU.add)
                K.actf(rh[:, lo:2 * NH], rh[:, lo:2 * NH], AF.Sqrt)
                K.recip(rh[:, lo:2 * NH], rh[:, lo:2 * NH])
                rp = rope[:, (b["t0"] // 128 + i), :] if not isctx else None

                def do_rope(Rv, outv):
                    C = rope.v(bc(rope.t[:, (b["t0"] // 128 + i), 0:32].unsqueeze(1), [128, NH, 32]))
                    K.tt(K.dve, T1[:, :, :], Rv, C, ALU.mult)
                    R5 = Rv.ap.rearrange("p h (a b c) -> p h a b c", a=2, b=2)
                    U5 = U.t[:, :, :].rearrange("p h (a b c) -> p h a b c", a=2, b=2)
                    S5 = rope.t[:, (b["t0"] // 128 + i), 32:64].rearrange("p (a b c) -> p a b c", a=2, b=2)
                    for hb_ in range(2):
                        K.tt(K.dve, U.v(U5[:, :, :, hb_, :]), View(Rv.buf, R5[:, :, :, 1 - hb_, :]),
                             rope.v(bc(S5[:, :, hb_, :].unsqueeze(1), [128, NH, 2, 8])), ALU.mult)
                    K.tt(K.dve, outv, T1[:, :, :], U[:, :, :], ALU.add)

                if not isctx:
                    q_ = qb[tcount % 2]
                    rq = rh.v(bc(rh.t[:, 0:NH].unsqueeze(2), [128, NH, QK]))
                    K.tt(K.dve, qf[:, :, :], qf[:, :, :], rq, ALU.mult)
                    K.tt(K.dve, q_[:, :, 0:64], qf[:, :, 0:64], QG.v(bc(QG.t[:, 0:64].unsqueeze(1), [128, NH, 64])), ALU.mult)
                    K.tt(K.dve, R[:, :, :], qf[:, :, 64:96], QG.v(bc(QG.t[:, 64:96].unsqueeze(1), [128, NH, 32])), ALU.mult)
                    do_rope(R[:, :, :], q_[:, :, 64:96])
                k_ = kb[tcount % 2]
                v_ = vb[tcount % 2]
                K.cp(K.pool, v_[:, :, 0:VD], kvf[:, :, 64:128])
                rk64 = rh.v(bc(rh.t[:, NH:2 * NH].unsqueeze(2), [128, NH, 64]))
                rk32 = rh.v(bc(rh.t[:, NH:2 * NH].unsqueeze(2), [128, NH, 32]))
                K.tt(K.dve, sq[:, :, 0:64], kvf[:, :, 0:64], rk64, ALU.mult)
                K.tt(K.dve, k_[:, :, 0:64], sq[:, :, 0:64], KG.v(bc(KG.t[:, 0:64].unsqueeze(1), [128, NH, 64])), ALU.mult)
                K.tt(K.dve, R[:, :, :], krs.v(bc(krs.t[:, :].unsqueeze(1), [128, NH, 32])), rk32, ALU.mult)
                if isctx:
                    K.tt(K.dve, k_[:, :, 64:96], R[:, :, :], KG.v(bc(KG.t[:, 64:96].unsqueeze(1), [128, NH, 32])), ALU.mult)
                else:
                    K.tt(K.dve, R[:, :, :], R[:, :, :], KG.v(bc(KG.t[:, 64:96].unsqueeze(1), [128, NH, 32])), ALU.mult)
                    do_rope(R[:, :, :], k_[:, :, 64:96])
                if not isctx:
                    for h in range(NH):
                        K.tr(pTq[0:QK, h, :], q_[:, h, :], ident[:, :], sig=(h == NH - 1))
                    K.cp(K.act, qs[0:QK, :, cs], pTq[0:QK, :, :])
                for h in range(NH):
                    K.tr(pTk[0:QK, h, :], k_[:, h, :], ident[:, :], sig=(h == NH - 1))
                K.cp(K.act, ks[0:QK, :, cs], pTk[0:QK, :, :])
                key = b["key0"] + 128 * i
                K.dma(K.sp, vE[:, key // 128, :, :], v_[:, :, :])
                tcount += 1
            if not isctx:
                K.dma(K.sp, qT[:, :, b["t0"]:b["t0"] + 512].rearrange("h d t -> d h t"), qs[0:QK, :, :])
            K.dma(K.sp, kT[:, :, b["key0"]:b["key0"] + Wd].rearrange("h d t -> d h t"), ks[0:QK, :, 0:Wd])
        K.barrier()


def phase_attn(K, I, S, qT, kT, vE, attT):
    nc = K.nc
    NK = S + CTX
    NT = NK // 128
    with contextlib.ExitStack() as st:
        kTh = [K.sb(st, [128, NK], BF16, "kTh") for _ in range(2)]
        vEh = [K.sb(st, [128, NT, VD + 1], BF16, "vEh") for _ in range(2)]
        qTh = [K.sb(st, [128, 512], BF16, "qTh") for _ in range(2)]
        pt = [K.sb(st, [128, 1024], BF16, "pt") for _ in range(3)]
        osb = [K.sb(st, [128, 512], F32, "osb") for _ in range(2)]
        rd = [K.sb(st, [128, 512], F32, "rd") for _ in range(2)]
        atts = [K.sb(st, [128, 512], BF16, "atts") for _ in range(2)]
        ones = K.sb(st, [128, 64], F32, "ones")
        K.memset(K.dve, ones[:, :], 0.0)
        K.memset(K.dve, ones[64:65, :], 1.0)
        for r__ in rd:
            K.memset(K.dve, r__[:, :], 0.0)
        ps = [K.ps(st, [128, 1024], F32, "ps") for _ in range(2)]
        po = [K.ps(st, [128, 512], F32, "po") for _ in range(2)]
        pb = K.ps(st, [128, 512], F32, "pb")
        npair = NT // 2
        u = 0
        ipt = 0
        for h in range(NH):
            kk, vv = kTh[h % 2], vEh[h % 2]
            K.dma(K.sp, kk[0:QK, :], kT[h, :, :])
            K.dma(K.sp, vv[:, :, :], vE[:, :, h, :])
            for qbk in range(S // 512):
                qq = qTh[u % 2]
                K.dma(K.sp, qq[0:QK, :], qT[h, :, qbk * 512:(qbk + 1) * 512])
                o_ = po[u % 2]
                for pr in range(npair):
                    p_ = ps[pr % 2]
                    for j in range(2):
                        kt = 2 * pr + j
                        K.mm(p_[:, j * 512:(j + 1) * 512], kk[0:QK, kt * 128:(kt + 1) * 128], qq[0:QK, :],
                             start=True, stop=True, sig=(j == 1))
                    e_ = pt[ipt % 3]
                    ipt += 1
                    K.actf(e_[:, :], p_[:, :], AF.Exp, scale=SM_SCALE)
                    for j in range(2):
                        kt = 2 * pr + j
                        K.mm(o_[0:VD + 1, :], vv[:, kt, :], e_[:, j * 512:(j + 1) * 512],
                             start=(kt == 0), stop=(kt == NT - 1), sig=(j == 1))
                ob = osb[u % 2]
                K.cp(K.act, ob[0:VD + 1, :], o_[0:VD + 1, :])
                r_ = rd[u % 2]
                K.recip(r_[64:65, :], ob[64:65, :])
                K.mm(pb[0:VD, :], ones[0:VD + 1, 0:VD], r_[0:VD + 1, :], start=True, stop=True)
                a_ = atts[u % 2]
                K.tt(K.dve, a_[0:VD, :], ob[0:VD, :], pb[0:VD, :], ALU.mult)
                K.dma(K.pool, attT[h * VD:(h + 1) * VD, qbk * 512:(qbk + 1) * 512], a_[0:VD, :])
                u += 1
        K.barrier()


def phase_l0c(K, I, S, modrow, uT, attT, x_in, x_out):
    nc = K.nc
    HALO = CONV_K // 2
    with contextlib.ExitStack() as st:
        wout = K.sb(st, [128, 8, D], BF16, "evwout")
        cw = K.sb(st, [128, 4, CONV_K], F32, "cw31")
        cb = K.sb(st, [128, 4], F32, "cb31")
        lg = K.sb(st, [128, 4], F32, "lng")
        lb = K.sb(st, [128, 4], F32, "lnb")
        onesF = K.sb(st, [128, 128], F32, "onesF")
        for kk in range(CONV_K):
            K.cload(cw.v(cw.t[:, :, kk:kk + 1]), colvec(I["ev_conv_w"][0, kk, :], 4), allow_slow_non_contiguous=True)
        K.cload(cb.v(cb.t[:, :].rearrange("p (k o) -> p k o", o=1)), colvec(I["ev_conv_b"][0, :], 4))
        K.cload(lg.v(lg.t[:, :].rearrange("p (k o) -> p k o", o=1)), colvec(I["ev_ln_g"][0, :], 4))
        K.cload(lb.v(lb.t[:, :].rearrange("p (k o) -> p k o", o=1)), colvec(I["ev_ln_b"][0, :], 4))
        with contextlib.ExitStack() as st2:
            G = K.sb(st2, [128, D], F32, "G")
            stg = [K.sb(st2, [128, 1024], F32, "wstg") for _ in range(3)]
            K.cload(G[:, :], modrow[2 * D:3 * D].partition_broadcast(128))
            K.setup_done()
            K.memset(K.dve, onesF[:, :], 1.0 / CONV_CH)
            K.ts(K.dve, cw[:, :, :], cw[:, :, :], 0.5, None, ALU.mult)
            load_cast(K, st2, wout, I["ev_w_out"][0], 8, D, colscale=G, stg=stg)
            K.barrier()
        uw = [K.sb(st, [128, 4, 512 + 2 * HALO], F32, "uw") for _ in range(2)]
        acc = K.sb(st, [128, 4, 512], F32, "acc31")
        sqb = K.sb(st, [128, 4, 512], F32, "sq31")
        mean = K.sb(st, [128, 512], F32, "mean")
        rs = K.sb(st, [128, 512], F32, "rs")
        aT = K.sb(st, [128, 4, 512], BF16, "aT")
        at = [K.sb(st, [128, 4, 512], BF16, "attblk") for _ in range(2)]
        xr = [K.sb(st, [128, D], F32, "xr") for _ in range(2)]
        pm = K.ps(st, [128, 512], F32, "pmean")
        pq = K.ps(st, [128, 512], F32, "pmsq")
        po = [K.ps(st, [128, 512], F32, "po") for _ in range(2)]
        cnt = 0
        for j in range(S // 512):
            t0 = 512 * j
            w_ = uw[j % 2]
            lo = max(t0 - HALO, 0)
            hi = min(t0 + 512 + HALO, S)
            if lo != t0 - HALO or hi != t0 + 512 + HALO:
                K.memset(K.pool, w_[:, :, :], 0.0)
            c0 = lo - (t0 - HALO)
            K.dma(K.sp, w_[:, :, c0:c0 + (hi - lo)], uT[:, lo:hi].rearrange("(m p) t -> p m t", p=128))
            a_ = at[j % 2]
            K.dma(K.sp, a_[:, :, :], attT[:, t0:t0 + 512].rearrange("(m p) t -> p m t", p=128))
            for m in range(4):
                K.ts(K.dve, acc[:, m, :], w_[:, m, 0:512], cw[:, m, 0:1], cb[:, m:m + 1], ALU.mult, ALU.add)
                for kk in range(1, CONV_K):
                    K.stt(acc[:, m, :], w_[:, m, kk:kk + 512], cw[:, m, kk:kk + 1], acc[:, m, :], ALU.mult, ALU.add)
                K.actf(sqb[:, m, :], acc[:, m, :], AF.Square)
            for m in range(4):
                K.mm(pm[:, :], onesF[:, :], acc[:, m, :], start=(m == 0), stop=(m == 3))
            for m in range(4):
                K.mm(pq[:, :], onesF[:, :], sqb[:, m, :], start=(m == 0), stop=(m == 3))
            K.cp(K.act, mean[:, :], pm[:, :])
            K.tt(K.dve, rs[:, :], mean[:, :], mean[:, :], ALU.mult)
            K.tt(K.dve, rs[:, :], pq[:, :], rs[:, :], ALU.subtract)
            K.ts(K.dve, rs[:, :], rs[:, :], EPS, None, ALU.add)
            K.actf(rs[:, :], rs[:, :], AF.Sqrt)
            K.recip(rs[:, :], rs[:, :])
            for m in range(4):
                K.tt(K.dve, acc[:, m, :], acc[:, m, :], mean[:, :], ALU.subtract)
                K.tt(K.dve, acc[:, m, :], acc[:, m, :], rs[:, :], ALU.mult)
                K.actf(aT[:, m, :], acc[:, m, :], AF.Silu, scale=lg[:, m:m + 1], bias=lb[:, m:m + 1])
            for i in range(4):
                x = xr[cnt % 2]
                cnt += 1
                K.dma(K.sp, x[:, :], x_in[t0 + 128 * i:t0 + 128 * (i + 1), :])
                cs = slice(128 * i, 128 * (i + 1))
                for n_ in range(2):
                    for k in range(8):
                        l_ = aT[:, k, cs] if k < 4 else a_[:, k - 4, cs]
                        K.mm(po[n_][:, :], l_, wout[:, k, n_ * 512:(n_ + 1) * 512], start=(k == 0), stop=(k == 7))
                for n_ in range(2):
                    K.tt(K.dve, x[:, n_ * 512:(n_ + 1) * 512], x[:, n_ * 512:(n_ + 1) * 512], po[n_][:, :], ALU.add)
                K.dma(K.pool, x_out[t0 + 128 * i:t0 + 128 * (i + 1), :], x[:, :])
        K.barrier()


WEIGHT_NAMES = ["ada_w", "ada_b", "norm_mix_g", "norm_ffn_g", "ffn_w_up", "ffn_conv_w", "ffn_conv_b",
                "ffn_w_down", "ev_w_in", "ev_conv_w", "ev_conv_b", "ev_ln_g", "ev_ln_b", "ev_qa_norm_g", "ev_w_uq",
                "ev_kva_norm_g", "ev_w_ukv", "ev_q_norm_g", "ev_k_norm_g", "ev_w_out", "od_w_in", "od_conv_w",
                "od_conv_b", "od_w_out"]


def build(S, shapes, debug=False, phases=("mod", "a", "b", "c", "f0", "m1", "f1")):
    nc = bass.Bass("TRN2", target_bir_lowering=False)
    I = {}
    I["x"] = nc.dram_tensor("x", [S, D], F32, kind="ExternalInput").ap()
    I["c"] = nc.dram_tensor("c", [D], F32, kind="ExternalInput").ap()
    I["ctx"] = nc.dram_tensor("ctx", [CTX, D], F32, kind="ExternalInput").ap()
    I["c_ctx"] = nc.dram_tensor("c_ctx", [D], F32, kind="ExternalInput").ap()
    for n in WEIGHT_NAMES:
        I[n] = nc.dram_tensor(n, list(shapes[n]), F32, kind="ExternalInput").ap()
    I["ident"] = nc.dram_tensor("ident", [128, 128], BF16, kind="ExternalInput").ap()
    I["rope"] = nc.dram_tensor("rope", [S, 64], F32, kind="ExternalInput").ap()
    y = nc.dram_tensor("y", [S, D], F32, kind="ExternalOutput").ap()
    sk = "ExternalOutput" if debug else "Internal"
    NK = S + CTX
    modv = nc.dram_tensor("modv", [2, 6 * D], F32, kind=sk).ap()
    modc = nc.dram_tensor("modc", [1, 2 * D], F32, kind=sk).ap()
    uT = nc.dram_tensor("uT", [CONV_CH, S], F32, kind=sk).ap()
    qT = nc.dram_tensor("qT", [NH, QK, S], BF16, kind=sk).ap()
    kT = nc.dram_tensor("kT", [NH, QK, NK], BF16, kind=sk).ap()
    vE = nc.dram_tensor("vE", [128, NK // 128, NH, VD + 1], BF16, kind=sk).ap()
    attT = nc.dram_tensor("attT", [NH * VD, S], BF16, kind=sk).ap()
    xa = nc.dram_tensor("xa", [S, D], F32, kind=sk).ap()
    xb = nc.dram_tensor("xb", [S, D], F32, kind=sk).ap()
    K = Kern(nc)
    with K.es:
        if "mod" in phases:
            phase_mod(K, I, modv, modc)
        if "a" in phases:
            phase_l0a(K, I, S, modv[0, :], modc[0, :], uT, qT, kT, vE)
        if "b" in phases:
            phase_attn(K, I, S, qT, kT, vE, attT)
        if "c" in phases:
            phase_l0c(K, I, S, modv[0, :], uT, attT, I["x"], xa)
        if "f0" in phases:
            phase_ffn(K, I, 0, S, xa, xb, modv[0, :])
        if "m1" in phases:
            phase_sconv(K, I, S, xb, xa, modv[1, :])
        if "f1" in phases:
            phase_ffn(K, I, 1, S, xa, y, modv[1, :])
        K.barrier()
    return nc


def rope_table(S):
    t = np.arange(S)
    row = (t // 64).astype(np.float32)
    col = (t % 64).astype(np.float32)
    half = 16
    inv = (10000.0 ** (-np.arange(0, half, 2, dtype=np.float32) / half)).astype(np.float32)
    ar = row[:, None] * inv[None, :]
    ac = col[:, None] * inv[None, :]
    cr, sr, cc, sc = np.cos(ar), np.sin(ar), np.cos(ac), np.sin(ac)
    C = np.concatenate([cr, cr, cc, cc], axis=1)
    Sg = np.concatenate([-sr, sr, -sc, sc], axis=1)
    return np.ascontiguousarray(np.concatenate([C, Sg], axis=1).astype(np.float32))


def kernel(debug=False, phases=("mod", "a", "b", "c", "f0", "m1", "f1"), **inputs):
    x = np.asarray(inputs["x"], dtype=np.float32)
    B, S, _ = x.shape
    assert B == 8
    shapes = {n: np.asarray(inputs[n]).shape for n in WEIGHT_NAMES}
    nc = build(S, shapes, debug=debug, phases=phases)
    ident = np.eye(128, dtype=np.float32).astype(ml_dtypes.bfloat16)
    rope = rope_table(S)
    shared = {n: np.ascontiguousarray(np.asarray(inputs[n], dtype=np.float32)) for n in WEIGHT_NAMES}
    shared["c_ctx"] = np.ascontiguousarray(np.asarray(inputs["c_ctx"], dtype=np.float32))
    shared["ident"] = ident
    shared["rope"] = rope
    in_maps = []
    for b in range(B):
        m = dict(shared)
        m["x"] = np.ascontiguousarray(x[b])
        m["c"] = np.ascontiguousarray(np.asarray(inputs["c"], dtype=np.float32)[b])
        m["ctx"] = np.ascontiguousarray(np.asarray(inputs["ctx"], dtype=np.float32)[b])
        in_maps.append(m)
    res = run_bass_kernel_spmd(nc, in_maps, core_ids=list(range(B)))
    if debug:
        return res
    return np.stack([np.asarray(r["y"], dtype=np.float32) for r in res.results], axis=0)
```

```python
import contextlib
import math
import numpy as np
import ml_dtypes
import concourse.bass as bass
import concourse.mybir as mybir
from concourse.bass_utils import run_bass_kernel_spmd

F32 = mybir.dt.float32
BF16 = mybir.dt.bfloat16
AF = mybir.ActivationFunctionType
ALU = mybir.AluOpType
AX = mybir.AxisListType

D = 1024
CTX = 256
CONV_CH = 512
CONV_K = 31
NH = 8
QK = 96
VD = 64
QL = 384
KVL = 256
EVEN_IN = 1696
FFN = 2816
EPS = 1e-6
SM_SCALE = QK ** -0.5
WSTR = 510


class Eng:
    def __init__(self, name, h, sem):
        self.name = name
        self.h = h
        self.sem = sem
        self.n = 0
        self.seen = {}


class SemC:
    def __init__(self, sem):
        self.sem = sem
        self.n = 0


class View:
    __slots__ = ("buf", "ap")

    def __init__(self, buf, ap):
        self.buf = buf
        self.ap = ap


class Buf:
    def __init__(self, K, t):
        self.K = K
        self.t = t
        self.cw = {}
        self.cr = {}
        self.pw = {}
        self.pr = {}
        self.has_read = False
        self.ld = None
        self.st = None

    def __getitem__(self, key):
        return View(self, self.t[key])

    def v(self, ap):
        return View(self, ap)


class Kern:
    def __init__(self, nc):
        self.nc = nc
        self.es = contextlib.ExitStack()
        mk = lambda n, h: Eng(n, h, self.es.enter_context(nc.semaphore("sem_" + n)))
        self.pe = mk("pe", nc.tensor)
        self.act = mk("act", nc.scalar)
        self.dve = mk("dve", nc.vector)
        self.pool = mk("pool", nc.gpsimd)
        self.sp = mk("sp", nc.sync)
        self.engs = [self.pe, self.act, self.dve, self.pool, self.sp]
        self.free_sems = [SemC(self.es.enter_context(nc.semaphore("dsem%d" % i))) for i in range(88)]
        self.live = []
        self.setup_sem = self.free_sems.pop()
        self.uid = 0

    def sb(self, st, shape, dt, name=None):
        self.uid += 1
        t = st.enter_context(self.nc.sbuf_tensor("%s_%d" % (name or "sb", self.uid), list(shape), dt))
        return Buf(self, t)

    def ps(self, st, shape, dt, name=None):
        self.uid += 1
        t = st.enter_context(self.nc.psum_tensor("%s_%d" % (name or "ps", self.uid), list(shape), dt))
        return Buf(self, t)

    def _need(self, e, sem, val):
        key = id(sem)
        if e.seen.get(key, 0) >= val:
            return
        e.h.wait_ge(sem, val)
        e.seen[key] = val

    def _dep(self, e, b, tag, idx, raw):
        if tag == "LD":
            self._need(e, b.ld.sem, idx)
        elif tag == "ST":
            self._need(e, b.st.sem, idx)
        else:
            if tag is e and not raw:
                return
            self._need(e, tag.sem, idx + 1)

    def _rdeps(self, e, b):
        for tag, idx in b.cw.items():
            self._dep(e, b, tag, idx, True)

    def _wdeps(self, e, b, also_cur=False):
        if b.has_read:
            b.pr, b.pw, b.cr, b.cw, b.has_read = b.cr, b.cw, {}, {}, False
        for tag, idx in b.pr.items():
            self._dep(e, b, tag, idx, False)
        for tag, idx in b.pw.items():
            self._dep(e, b, tag, idx, False)
        if also_cur:
            for tag, idx in b.cw.items():
                self._dep(e, b, tag, idx, False)

    def issue(self, e, reads, writes, fn, sig=True, wait_cur=False):
        rb = []
        for v in reads:
            if isinstance(v, View) and v.buf not in rb:
                rb.append(v.buf)
        wb = []
        for v in writes:
            if v.buf not in wb:
                wb.append(v.buf)
        for b in rb:
            self._rdeps(e, b)
        for b in wb:
            self._wdeps(e, b, also_cur=wait_cur)
        ins = fn()
        idx = e.n
        if sig:
            ins.then_inc(e.sem, 1)
            e.n += 1
        for b in rb:
            if b not in wb:
                b.cr[e] = idx
                b.has_read = True
        for b in wb:
            b.cw[e] = idx
        return ins

    def _getsem(self, b, which):
        s = getattr(b, which)
        if s is None:
            s = self.free_sems.pop()
            setattr(b, which, s)
            if b not in self.live:
                self.live.append(b)
        return s

    def dma(self, q, out, in_, **kw):
        nc = self.nc
        if isinstance(out, View):
            b = out.buf
            self._wdeps(q, b, also_cur=True)
            s = self._getsem(b, "ld")
            q.h.dma_start(out=out.ap, in_=in_, **kw).then_inc(s.sem, 16)
            s.n += 16
            b.cw["LD"] = s.n
        else:
            b = in_.buf
            self._rdeps(q, b)
            s = self._getsem(b, "st")
            q.h.dma_start(out=out, in_=in_.ap, **kw).then_inc(s.sem, 16)
            s.n += 16
            b.cr["ST"] = s.n
            b.has_read = True

    def cload(self, out, in_, q=None, **kw):
        q = q or self.sp
        kw.setdefault("allow_slow_non_contiguous", True)
        s = self.setup_sem
        q.h.dma_start(out=out.ap, in_=in_, **kw).then_inc(s.sem, 16)
        s.n += 16

    def setup_done(self):
        for e in self.engs:
            self._need(e, self.setup_sem.sem, self.setup_sem.n)

    def barrier(self):
        for e in self.engs:
            for f in self.engs:
                if f is not e and f.n > 0:
                    self._need(e, f.sem, f.n)
            for b in self.live:
                for s in (b.ld, b.st):
                    if s is not None and s.n > 0:
                        self._need(e, s.sem, s.n)
            self._need(e, self.setup_sem.sem, self.setup_sem.n)
        for b in self.live:
            for w in ("ld", "st"):
                s = getattr(b, w)
                if s is not None:
                    self.free_sems.append(s)
                    setattr(b, w, None)
            b.cw, b.cr, b.pw, b.pr, b.has_read = {}, {}, {}, {}, False
        self.live = []

    @staticmethod
    def _a(v):
        return v.ap if isinstance(v, View) else v

    def mm(self, out, lhsT, rhs, start, stop, sig=None):
        if sig is None:
            sig = stop
        return self.issue(self.pe, [lhsT, rhs], [out],
                          lambda: self.nc.tensor.matmul(out.ap, lhsT.ap, rhs.ap, start=start, stop=stop), sig)

    def tr(self, out, in_, ident, sig=True):
        return self.issue(self.pe, [in_, ident], [out],
                          lambda: self.nc.tensor.transpose(out.ap, in_.ap, ident.ap), sig)

    def actf(self, out, in_, func, scale=None, bias=None, accum=None):
        kw = {}
        rd = [in_]
        wr = [out]
        if scale is not None:
            kw["scale"] = self._a(scale)
            rd.append(scale)
        if bias is not None:
            kw["bias"] = self._a(bias)
            rd.append(bias)
        if accum is not None:
            kw["accum_out"] = accum.ap
            wr.append(accum)
        return self.issue(self.act, rd, wr, lambda: self.nc.scalar.activation(out.ap, in_.ap, func, **kw))

    def _ve(self, e):
        return self.nc.vector if e is self.dve else self.nc.gpsimd

    def tt(self, e, out, in0, in1, op):
        return self.issue(e, [in0, in1], [out], lambda: self._ve(e).tensor_tensor(out.ap, in0.ap, in1.ap, op))

    def ts(self, e, out, in0, s1, s2, op0, op1=None):
        rd = [in0, s1, s2]
        if op1 is None:
            return self.issue(e, rd, [out], lambda: self._ve(e).tensor_scalar(out.ap, in0.ap, self._a(s1), None, op0))
        return self.issue(e, rd, [out],
                          lambda: self._ve(e).tensor_scalar(out.ap, in0.ap, self._a(s1), self._a(s2), op0, op1))

    def stt(self, out, in0, s, in1, op0, op1):
        return self.issue(self.dve, [in0, s, in1], [out],
                          lambda: self.nc.vector.scalar_tensor_tensor(out.ap, in0.ap, self._a(s), in1.ap, op0, op1))

    def cp(self, e, out, in_):
        if e is self.act:
            return self.issue(e, [in_], [out], lambda: self.nc.scalar.copy(out.ap, in_.ap))
        return self.issue(e, [in_], [out], lambda: self._ve(e).tensor_copy(out.ap, in_.ap))

    def recip(self, out, in_):
        return self.issue(self.dve, [in_], [out], lambda: self.nc.vector.reciprocal(out.ap, in_.ap))

    def red(self, out, in_, op=None):
        return self.issue(self.dve, [in_], [out],
                          lambda: self.nc.vector.tensor_reduce(out.ap, in_.ap, AX.X, op or ALU.add))

    def memset(self, e, out, val, wait_cur=False):
        return self.issue(e, [], [out], lambda: self._ve(e).memset(out.ap, val), wait_cur=wait_cur)

    def rsqrt(self, out, in_, mul, eps):
        self.ts(self.dve, out, in_, mul, eps, ALU.mult, ALU.add)
        self.actf(out, out, AF.Sqrt)
        self.recip(out, out)


def bc(ap, shape):
    return ap.broadcast_to(list(shape))


def load_cast(K, st, dst, src, nk, F, colscale=None, rowscale=None, stg=None):
    CH = 1024
    i = 0
    for k in range(nk):
        for c0 in range(0, F, CH):
            w = min(CH, F - c0)
            s = stg[i % len(stg)]
            K.dma(K.sp, s[:, 0:w], src[k * 128:(k + 1) * 128, c0:c0 + w])
            e = K.dve if i % 2 == 0 else K.pool
            d = dst[:, k, c0:c0 + w]
            if colscale is not None:
                K.tt(e, d, s[:, 0:w], colscale[:, c0:c0 + w], ALU.mult)
            elif rowscale is not None:
                K.ts(e, d, s[:, 0:w], rowscale[:, k:k + 1], None, ALU.mult)
            else:
                K.cp(e, d, s[:, 0:w])
            i += 1


def colvec(ap1d, nk):
    return ap1d.rearrange("(k p o) -> p k o", p=128, o=1)


def norm_block(K, A, x_rows, tiles, gs, sh, ident, hT, hb, xt, ss, rstd, pT, do_norm_now=True):
    nt = len(tiles)
    for (i, p_lo, p_hi, tok_lo, w) in tiles:
        x = xt[i % len(xt)]
        if p_lo > 0 or p_hi < 128:
            K.memset(K.dve, x[:, :], 0.0)
        K.dma(K.sp, x[p_lo:p_hi, :], x_rows[tok_lo:tok_lo + (p_hi - p_lo), :])
        K.actf(A["junk"][:, :], x[:, :], AF.Square, accum=ss[:, i:i + 1])
    K.rsqrt(rstd[:, 0:nt], ss[:, 0:nt], 1.0 / D, EPS)
    for (i, p_lo, p_hi, tok_lo, w) in tiles:
        x = xt[i % len(xt)]
        K.ts(K.dve, hb[i][:, :], x[:, :], rstd[:, i:i + 1], None, ALU.mult)


def transposes_block(K, tiles, gs, sh, ident, hT, hb, pT, zero_cols):
    for (i, p_lo, p_hi, tok_lo, w) in tiles:
        p = pT[i % len(pT)]
        for k in range(8):
            K.tr(p[:, k, :], hb[i][:, k * 128:(k + 1) * 128], ident[:, :], sig=(k == 7))
        for k in range(8):
            K.actf(hT[:, k, i * 128:i * 128 + w], p[:, k, 0:w], AF.Identity,
                   scale=gs[:, k:k + 1], bias=sh[:, k:k + 1])
    for c in zero_cols:
        K.memset(K.dve, hT[:, :, c:c + 1], 0.0, wait_cur=True)


def win_blocks(S):
    blocks = []
    j = 0
    while WSTR * j < S:
        nvalid = min(WSTR, S - WSTR * j)
        t0 = WSTR * j - 1
        Wd = nvalid + 2
        tiles = []
        for i in range((Wd + 127) // 128):
            a = t0 + 128 * i
            tok_lo = max(a, 0)
            tok_hi = min(a + 128, S, t0 + Wd)
            w = min(128, Wd - 128 * i)
            tiles.append((i, tok_lo - a, tok_hi - a, tok_lo, w))
        zero_cols = []
        if t0 < 0:
            zero_cols.append(0)
        if t0 + Wd - 1 >= S:
            zero_cols.append(Wd - 1)
        blocks.append(dict(j=j, t0=t0, Wd=Wd, nvalid=nvalid, tiles=tiles, zero_cols=zero_cols))
        j += 1
    return blocks


def load_mod_cols(K, st, modrow, sec_shift, sec_scale, normg, name):
    gs = K.sb(st, [128, 8], F32, name + "gs")
    sh = K.sb(st, [128, 8], F32, name + "sh")
    ng = K.sb(st, [128, 8], F32, name + "ng")
    K.cload(gs.v(gs.t[:, :].rearrange("p (k o) -> p k o", o=1)), colvec(modrow[sec_scale * D:(sec_scale + 1) * D], 8))
    K.cload(sh.v(sh.t[:, :].rearrange("p (k o) -> p k o", o=1)), colvec(modrow[sec_shift * D:(sec_shift + 1) * D], 8))
    K.cload(ng.v(ng.t[:, :].rearrange("p (k o) -> p k o", o=1)), colvec(normg, 8))
    K.setup_done()
    K.stt(gs[:, :], gs[:, :], 1.0, ng[:, :], ALU.add, ALU.mult)
    return gs, sh


def phase_mod(K, I, modv, modc):
    nc = K.nc
    with contextlib.ExitStack() as st:
        cT = K.sb(st, [128, 8], F32, "cT")
        ccT = K.sb(st, [128, 8], F32, "ccT")
        ab = K.sb(st, [1, 2 * 6 * D], F32, "ab")
        stg = [K.sb(st, [128, 8, 512], F32, "adastg") for _ in range(2)]
        row = [K.sb(st, [1, 512], F32, "modrow") for _ in range(2)]
        pm = [K.ps(st, [128, 512], F32, "pm") for _ in range(2)]
        K.cload(cT.v(cT.t[:, :].rearrange("p (k o) -> p k o", o=1)), colvec(I["c"], 8))
        K.cload(ccT.v(ccT.t[:, :].rearrange("p (k o) -> p k o", o=1)), colvec(I["c_ctx"], 8))
        K.cload(ab[:, :], I["ada_b"].rearrange("(o l) f -> o (l f)", o=1))
        K.setup_done()
        K.actf(cT[:, :], cT[:, :], AF.Silu)
        K.actf(ccT[:, :], ccT[:, :], AF.Silu)
        it = 0
        for layer in range(2):
            for g in range(12):
                s = stg[it % 2]
                K.dma(K.sp, s[:, :, :], I["ada_w"][layer, :, g * 512:(g + 1) * 512].rearrange("(k p) f -> p k f", p=128))
                srcs = [(cT, modv[layer:layer + 1, g * 512:(g + 1) * 512])]
                if layer == 0 and g < 4:
                    srcs.append((ccT, modc[0:1, g * 512:(g + 1) * 512]))
                for (vec, dst) in srcs:
                    p = pm[it % 2]
                    r = row[it % 2]
                    for k in range(8):
                        K.mm(p[0:1, :], vec[:, k:k + 1], s[:, k, :], start=(k == 0), stop=(k == 7))
                    K.tt(K.dve, r[:, :], p[0:1, :], ab[:, layer * 6 * D + g * 512: layer * 6 * D + (g + 1) * 512], ALU.add)
                    K.dma(K.pool, dst, r[:, :])
                    it += 1
        K.barrier()


def phase_ffn(K, I, layer, S, x_in, x_out, modrow):
    nc = K.nc
    NM = FFN // 128
    with contextlib.ExitStack() as st:
        wup = K.sb(st, [128, 8, 2 * FFN], BF16, "wup")
        wdn = K.sb(st, [128, NM, D], BF16, "wdn")
        ident = K.sb(st, [128, 128], BF16, "ident")
        cw = K.sb(st, [128, NM, 3], F32, "cw")
        cb = K.sb(st, [128, NM], F32, "cb")
        K.cload(ident[:, :], I["ident"])
        for kk in range(3):
            K.cload(cw.v(cw.t[:, :, kk:kk + 1]), colvec(I["ffn_conv_w"][layer, kk, :], NM), allow_slow_non_contiguous=True)
        K.cload(cb.v(cb.t[:, :].rearrange("p (k o) -> p k o", o=1)), colvec(I["ffn_conv_b"][layer, :], NM))
        gs, sh = load_mod_cols(K, st, modrow, 3, 4, I["norm_ffn_g"][layer, :], "f")
        with contextlib.ExitStack() as st2:
            G = K.sb(st2, [128, D], F32, "G")
            stg = [K.sb(st2, [128, 1024], F32, "wstg") for _ in range(3)]
            K.cload(G[:, :], modrow[5 * D:6 * D].partition_broadcast(128))
            K.setup_done()
            load_cast(K, st2, wup, I["ffn_w_up"][layer], 8, 2 * FFN, stg=stg)
            load_cast(K, st2, wdn, I["ffn_w_down"][layer], NM, D, colscale=G, stg=stg)
            K.barrier()
        hid = K.sb(st, [128, NM, 512], BF16, "hid")
        hT = K.sb(st, [128, 8, 512], BF16, "hT")
        hb = [K.sb(st, [128, D], BF16, "hb") for _ in range(4)]
        xt = [K.sb(st, [128, D], F32, "xt") for _ in range(4)]
        xr = [K.sb(st, [128, D], F32, "xr") for _ in range(2)]
        acc = [K.sb(st, [128, 512], F32, "acc") for _ in range(2)]
        sl = [K.sb(st, [128, 512], BF16, "sl") for _ in range(2)]
        A = {"junk": K.sb(st, [128, D], BF16, "junk")}
        ss = K.sb(st, [128, 4], F32, "ss")
        rstd = K.sb(st, [128, 4], F32, "rstd")
        pT = [K.ps(st, [128, 8, 128], BF16, "pT") for _ in range(2)]
        pg = [K.ps(st, [128, 512], F32, "pg") for _ in range(2)]
        pv = [K.ps(st, [128, 512], F32, "pv") for _ in range(2)]
        po = [K.ps(st, [128, 512], F32, "po") for _ in range(2)]
        blocks = win_blocks(S)

        def pre_norm(b):
            norm_block(K, A, x_in, b["tiles"], gs, sh, ident, hT, hb, xt, ss, rstd, pT)

        def pre_tr(b):
            transposes_block(K, b["tiles"], gs, sh, ident, hT, hb, pT, b["zero_cols"])

        def up(b):
            Wd = b["Wd"]
            n = Wd - 2
            for m in range(NM):
                g_, v_ = pg[m % 2], pv[m % 2]
                for k in range(8):
                    K.mm(g_[:, 0:Wd], wup[:, k, m * 128:(m + 1) * 128], hT[:, k, 0:Wd], start=(k == 0), stop=(k == 7))
                for k in range(8):
                    K.mm(v_[:, 0:Wd], wup[:, k, FFN + m * 128:FFN + (m + 1) * 128], hT[:, k, 0:Wd], start=(k == 0), stop=(k == 7))
                a = acc[m % 2]
                K.ts(K.dve, a[:, 0:n], g_[:, 0:n], cw[:, m, 0:1], None, ALU.mult)
                K.stt(a[:, 0:n], g_[:, 1:n + 1], cw[:, m, 1:2], a[:, 0:n], ALU.mult, ALU.add)
                K.stt(a[:, 0:n], g_[:, 2:n + 2], cw[:, m, 2:3], a[:, 0:n], ALU.mult, ALU.add)
                s_ = sl[m % 2]
                K.actf(s_[:, 0:n], a[:, 0:n], AF.Silu, bias=cb[:, m:m + 1])
                K.tt(K.dve, hid[:, m, 0:n], s_[:, 0:n], v_[:, 1:n + 1], ALU.mult)

        def down(b, cnt):
            nv = b["nvalid"]
            tok0 = WSTR * b["j"]
            for i2 in range((nv + 127) // 128):
                r = min(128, nv - 128 * i2)
                x = xr[cnt[0] % 2]
                cnt[0] += 1
                K.dma(K.sp, x[0:r, :], x_in[tok0 + 128 * i2: tok0 + 128 * i2 + r, :])
                for n_ in range(2):
                    for k in range(NM):
                        K.mm(po[n_][0:r, :], hid[:, k, 128 * i2:128 * i2 + r], wdn[:, k, n_ * 512:(n_ + 1) * 512],
                             start=(k == 0), stop=(k == NM - 1))
                for n_ in range(2):
                    K.tt(K.dve, x[0:r, n_ * 512:(n_ + 1) * 512], x[0:r, n_ * 512:(n_ + 1) * 512], po[n_][0:r, :], ALU.add)
                K.dma(K.pool, x_out[tok0 + 128 * i2: tok0 + 128 * i2 + r, :], x[0:r, :])

        cnt = [0]
        pre_norm(blocks[0])
        pre_tr(blocks[0])
        for bi, b in enumerate(blocks):
            nxt = blocks[bi + 1] if bi + 1 < len(blocks) else None
            if nxt:
                pre_norm(nxt)
            up(b)
            if nxt:
                pre_tr(nxt)
            down(b, cnt)
        K.barrier()


def phase_sconv(K, I, S, x_in, x_out, modrow):
    nc = K.nc
    with contextlib.ExitStack() as st:
        win = K.sb(st, [128, 8, 3 * D], BF16, "odwin")
        wout = K.sb(st, [128, 8, D], BF16, "odwout")
        ident = K.sb(st, [128, 128], BF16, "ident")
        cw = K.sb(st, [128, 8, 3], F32, "cw")
        cb = K.sb(st, [128, 8], F32, "cb")
        K.cload(ident[:, :], I["ident"])
        for kk in range(3):
            K.cload(cw.v(cw.t[:, :, kk:kk + 1]), colvec(I["od_conv_w"][0, kk, :], 8), allow_slow_non_contiguous=True)
        K.cload(cb.v(cb.t[:, :].rearrange("p (k o) -> p k o", o=1)), colvec(I["od_conv_b"][0, :], 8))
        gs, sh = load_mod_cols(K, st, modrow, 0, 1, I["norm_mix_g"][1, :], "m1")
        with contextlib.ExitStack() as st2:
            G = K.sb(st2, [128, D], F32, "G")
            stg = [K.sb(st2, [128, 1024], F32, "wstg") for _ in range(3)]
            K.cload(G[:, :], modrow[2 * D:3 * D].partition_broadcast(128))
            K.setup_done()
            load_cast(K, st2, win, I["od_w_in"][0], 8, 3 * D, stg=stg)
            load_cast(K, st2, wout, I["od_w_out"][0], 8, D, colscale=G, stg=stg)
            K.barrier()
        yT = K.sb(st, [128, 8, 512], BF16, "yT")
        hT = K.sb(st, [128, 8, 512], BF16, "hT")
        hb = [K.sb(st, [128, D], BF16, "hb") for _ in range(4)]
        xt = [K.sb(st, [128, D], F32, "xt") for _ in range(4)]
        xr = [K.sb(st, [128, D], F32, "xr") for _ in range(2)]
        acc = [K.sb(st, [128, 512], F32, "acc") for _ in range(2)]
        us = [K.sb(st, [128, 512], F32, "us") for _ in range(2)]
        A = {"junk": K.sb(st, [128, D], BF16, "junk")}
        ss = K.sb(st, [128, 4], F32, "ss")
        rstd = K.sb(st, [128, 4], F32, "rstd")
        pT = [K.ps(st, [128, 8, 128], BF16, "pT") for _ in range(1)]
        pb = [K.ps(st, [128, 512], F32, "pb") for _ in range(2)]
        pc = [K.ps(st, [128, 512], F32, "pc") for _ in range(2)]
        pu = [K.ps(st, [128, 512], F32, "pu") for _ in range(1)]
        po = [K.ps(st, [128, 512], F32, "po") for _ in range(2)]
        blocks = win_blocks(S)

        def mix(b):
            Wd = b["Wd"]
            n = Wd - 2
            for m in range(8):
                b_, c_, u_ = pb[m % 2], pc[m % 2], pu[0]
                for (dst, off) in ((u_, 2 * D), (c_, D), (b_, 0)):
                    for k in range(8):
                        K.mm(dst[:, 0:Wd], win[:, k, off + m * 128: off + (m + 1) * 128], hT[:, k, 0:Wd],
                             start=(k == 0), stop=(k == 7))
                u = us[m % 2]
                K.cp(K.act, u[:, 0:Wd], u_[:, 0:Wd])
                K.tt(K.dve, u[:, 0:Wd], u[:, 0:Wd], c_[:, 0:Wd], ALU.mult)
                a = acc[m % 2]
                K.ts(K.dve, a[:, 0:n], u[:, 0:n], cw[:, m, 0:1], cb[:, m:m + 1], ALU.mult, ALU.add)
                K.stt(a[:, 0:n], u[:, 1:n + 1], cw[:, m, 1:2], a[:, 0:n], ALU.mult, ALU.add)
                K.stt(a[:, 0:n], u[:, 2:n + 2], cw[:, m, 2:3], a[:, 0:n], ALU.mult, ALU.add)
                K.tt(K.dve, yT[:, m, 0:n], a[:, 0:n], b_[:, 1:n + 1], ALU.mult)

        def outp(b, cnt):
            nv = b["nvalid"]
            tok0 = WSTR * b["j"]
            for i2 in range((nv + 127) // 128):
                r = min(128, nv - 128 * i2)
                x = xr[cnt[0] % 2]
                cnt[0] += 1
                K.dma(K.sp, x[0:r, :], x_in[tok0 + 128 * i2: tok0 + 128 * i2 + r, :])
                for n_ in range(2):
                    for k in range(8):
                        K.mm(po[n_][0:r, :], yT[:, k, 128 * i2:128 * i2 + r], wout[:, k, n_ * 512:(n_ + 1) * 512],
                             start=(k == 0), stop=(k == 7))
                for n_ in range(2):
                    K.tt(K.dve, x[0:r, n_ * 512:(n_ + 1) * 512], x[0:r, n_ * 512:(n_ + 1) * 512], po[n_][0:r, :], ALU.add)
                K.dma(K.pool, x_out[tok0 + 128 * i2: tok0 + 128 * i2 + r, :], x[0:r, :])

        cnt = [0]
        norm_block(K, A, x_in, blocks[0]["tiles"], gs, sh, ident, hT, hb, xt, ss, rstd, pT)
        transposes_block(K, blocks[0]["tiles"], gs, sh, ident, hT, hb, pT, blocks[0]["zero_cols"])
        for bi, b in enumerate(blocks):
            nxt = blocks[bi + 1] if bi + 1 < len(blocks) else None
            if nxt:
                norm_block(K, A, x_in, nxt["tiles"], gs, sh, ident, hT, hb, xt, ss, rstd, pT)
            mix(b)
            if nxt:
                transposes_block(K, nxt["tiles"], gs, sh, ident, hT, hb, pT, nxt["zero_cols"])
            outp(b, cnt)
        K.barrier()


def phase_l0a(K, I, S, modrow, modc, uT, qT, kT, vE):
    nc = K.nc
    NK = S + CTX
    with contextlib.ExitStack() as st:
        win = K.sb(st, [128, 8, EVEN_IN], BF16, "evwin")
        wuq = K.sb(st, [128, 3, NH * QK], BF16, "wuq")
        wukv = K.sb(st, [128, 2, NH * 128], BF16, "wukv")
        ident = K.sb(st, [128, 128], BF16, "ident")
        QG = K.sb(st, [128, QK], F32, "QG")
        KG = K.sb(st, [128, QK], F32, "KG")
        rope = K.sb(st, [128, S // 128, 64], F32, "rope")
        qag = K.sb(st, [128, 3], F32, "qag")
        kvag = K.sb(st, [128, 2], F32, "kvag")
        K.cload(ident[:, :], I["ident"])
        K.cload(QG[:, :], I["ev_q_norm_g"][0, :].partition_broadcast(128))
        K.cload(KG[:, :], I["ev_k_norm_g"][0, :].partition_broadcast(128))
        K.cload(rope[:, :, :], I["rope"].rearrange("(i p) c -> p i c", p=128))
        K.cload(qag.v(qag.t[:, :].rearrange("p (k o) -> p k o", o=1)), colvec(I["ev_qa_norm_g"][0, :], 3))
        K.cload(kvag.v(kvag.t[:, :].rearrange("p (k o) -> p k o", o=1)), colvec(I["ev_kva_norm_g"][0, :], 2))
        gs, sh = load_mod_cols(K, st, modrow, 0, 1, I["norm_mix_g"][0, :], "m0")
        gsc, shc = load_mod_cols(K, st, modc, 0, 1, I["norm_mix_g"][0, :], "m0c")
        with contextlib.ExitStack() as st2:
            stg = [K.sb(st2, [128, 1024], F32, "wstg") for _ in range(3)]
            load_cast(K, st2, win, I["ev_w_in"][0], 8, EVEN_IN, stg=stg)
            load_cast(K, st2, wuq, I["ev_w_uq"][0], 3, NH * QK, rowscale=qag, stg=stg)
            load_cast(K, st2, wukv, I["ev_w_ukv"][0], 2, NH * 128, rowscale=kvag, stg=stg)
            K.barrier()
        hT = K.sb(st, [128, 8, 512], BF16, "hT")
        hb = [K.sb(st, [128, D], BF16, "hb") for _ in range(4)]
        xt = [K.sb(st, [128, D], F32, "xt") for _ in range(4)]
        A = {"junk": K.sb(st, [128, D], BF16, "junk")}
        ss = K.sb(st, [128, 4], F32, "ss")
        rstd = K.sb(st, [128, 4], F32, "rstd")
        th = [K.sb(st, [128, 512], F32, "th") for _ in range(2)]
        ust = [K.sb(st, [128, 512], F32, "ust") for _ in range(2)]
        ccT = K.sb(st, [128, 5, 512], BF16, "ccT")
        krs = K.sb(st, [128, 32], F32, "krs")
        st2t = K.sb(st, [128, 4], F32, "st2")
        r2 = K.sb(st, [128, 2], F32, "r2")
        qf = K.sb(st, [128, NH, QK], F32, "qf")
        kvf = K.sb(st, [128, NH, 128], F32, "kvf")
        sq = K.sb(st, [128, NH, QK], F32, "sq")
        ssh = K.sb(st, [128, 2 * NH], F32, "ssh")
        rh = K.sb(st, [128, 2 * NH], F32, "rh")
        R = K.sb(st, [128, NH, 32], F32, "R")
        T1 = K.sb(st, [128, NH, 32], F32, "T1")
        U = K.sb(st, [128, NH, 32], F32, "U")
        qb = [K.sb(st, [128, NH, QK], BF16, "qb") for _ in range(2)]
        kb = [K.sb(st, [128, NH, QK], BF16, "kb") for _ in range(2)]
        vb = [K.sb(st, [128, NH, VD + 1], BF16, "vb") for _ in range(2)]
        qTs = [K.sb(st, [128, NH, 512], BF16, "qTs") for _ in range(2)]
        kTs = [K.sb(st, [128, NH, 512], BF16, "kTs") for _ in range(2)]
        for v_ in vb:
            K.memset(K.dve, v_[:, :, VD:VD + 1], 1.0)
        pT = [K.ps(st, [128, 8, 128], BF16, "pT")]
        pA = [K.ps(st, [128, 512], F32, "pA") for _ in range(2)]
        pQ = [K.ps(st, [128, 512], F32, "pQ") for _ in range(2)]
        pTq = K.ps(st, [128, 8, 128], BF16, "pTq")
        pTk = K.ps(st, [128, 8, 128], BF16, "pTk")
        blks = [dict(ctx=True, x=I["ctx"], t0=0, nt=CTX // 128, key0=0)]
        for j in range(S // 512):
            blks.append(dict(ctx=False, x=I["x"], t0=512 * j, nt=4, key0=CTX + 512 * j))
        tcount = 0
        for bi, b in enumerate(blks):
            nt = b["nt"]
            Wd = nt * 128
            isctx = b["ctx"]
            g_, s_ = (gsc, shc) if isctx else (gs, sh)
            tiles = [(i, 0, 128, b["t0"] + 128 * i, 128) for i in range(nt)]
            norm_block(K, A, b["x"], tiles, g_, s_, ident, hT, hb, xt, ss, rstd, pT)
            transposes_block(K, tiles, g_, s_, ident, hT, hb, pT, [])
            pi = 0
            if not isctx:
                for m in range(4):
                    pv_, pg_ = pA[0], pA[1]
                    for k in range(8):
                        K.mm(pv_[:, 0:Wd], win[:, k, m * 128:(m + 1) * 128], hT[:, k, 0:Wd], start=(k == 0), stop=(k == 7))
                    for k in range(8):
                        K.mm(pg_[:, 0:Wd], win[:, k, 512 + m * 128:512 + (m + 1) * 128], hT[:, k, 0:Wd], start=(k == 0), stop=(k == 7))
                    t_ = th[m % 2]
                    K.actf(t_[:, :], pg_[:, :], AF.Tanh, scale=0.5)
                    u_ = ust[m % 2]
                    K.stt(u_[:, :], t_[:, :], 1.0, pv_[:, :], ALU.add, ALU.mult)
                    K.dma(K.pool, uT[m * 128:(m + 1) * 128, b["t0"]:b["t0"] + 512], u_[:, :])
            for m in range(5):
                if isctx and m < 3:
                    continue
                p_ = pA[m % 2]
                for k in range(8):
                    K.mm(p_[:, 0:Wd], win[:, k, 1024 + m * 128:1024 + (m + 1) * 128], hT[:, k, 0:Wd], start=(k == 0), stop=(k == 7))
                K.cp(K.dve, ccT[:, m, 0:Wd], p_[:, 0:Wd])
            qs, ks = qTs[bi % 2], kTs[bi % 2]
            for i in range(nt):
                cs = slice(i * 128, (i + 1) * 128)
                p1, p2 = pA[0], pA[1]
                for k in range(8):
                    K.mm(p1[:, :], hT[:, k, cs], win[:, k, 1024:1536], start=(k == 0), stop=(k == 7))
                for k in range(8):
                    K.mm(p2[:, 0:160], hT[:, k, cs], win[:, k, 1536:1696], start=(k == 0), stop=(k == 7))
                J = A["junk"]
                K.actf(J[:, 0:384], p1[:, 0:384], AF.Square, accum=st2t[:, 0:1])
                K.actf(J[:, 0:128], p1[:, 384:512], AF.Square, accum=st2t[:, 1:2])
                K.actf(J[:, 0:128], p2[:, 0:128], AF.Square, accum=st2t[:, 2:3])
                K.actf(krs[:, :], p2[:, 128:160], AF.Identity)
                K.actf(J[:, 0:32], p2[:, 128:160], AF.Square, accum=st2t[:, 3:4])
                K.tt(K.dve, st2t[:, 1:2], st2t[:, 1:2], st2t[:, 2:3], ALU.add)
                K.ts(K.dve, r2[:, 0:1], st2t[:, 0:1], 1.0 / QL, EPS, ALU.mult, ALU.add)
                K.ts(K.dve, r2[:, 1:2], st2t[:, 1:2], 1.0 / KVL, EPS, ALU.mult, ALU.add)
                K.actf(r2[:, :], r2[:, :], AF.Sqrt)
                K.recip(r2[:, :], r2[:, :])
                qfl = qf.v(qf.t[:, :, :].rearrange("p h d -> p (h d)"))
                kvfl = kvf.v(kvf.t[:, :, :].rearrange("p h d -> p (h d)"))
                if not isctx:
                    pq0, pq1 = pQ[0], pQ[1]
                    for k in range(3):
                        K.mm(pq0[:, :], ccT[:, k, cs], wuq[:, k, 0:512], start=(k == 0), stop=(k == 2))
                    for k in range(3):
                        K.mm(pq1[:, 0:256], ccT[:, k, cs], wuq[:, k, 512:768], start=(k == 0), stop=(k == 2))
                    K.actf(qf.v(qfl.ap[:, 0:512]), pq0[:, :], AF.Identity, scale=r2[:, 0:1])
                    K.actf(qf.v(qfl.ap[:, 512:768]), pq1[:, 0:256], AF.Identity, scale=r2[:, 0:1])
                    K.tt(K.dve, sq[:, :, :], qf[:, :, :], qf[:, :, :], ALU.mult)
                    K.red(ssh[:, 0:NH], sq[:, :, :])
                pk0, pk1 = pQ[0], pQ[1]
                for n_, pk in enumerate((pk0, pk1)):
                    for k in range(2):
                        K.mm(pk[:, :], ccT[:, 3 + k, cs], wukv[:, k, n_ * 512:(n_ + 1) * 512], start=(k == 0), stop=(k == 1))
                K.actf(kvf.v(kvfl.ap[:, 0:512]), pk0[:, :], AF.Identity, scale=r2[:, 1:2])
                K.actf(kvf.v(kvfl.ap[:, 512:1024]), pk1[:, :], AF.Identity, scale=r2[:, 1:2])
                K.tt(K.dve, sq[:, :, 0:64], kvf[:, :, 0:64], kvf[:, :, 0:64], ALU.mult)
                K.red(ssh[:, NH:2 * NH], sq[:, :, 0:64])
                K.ts(K.dve, ssh[:, NH:2 * NH], ssh[:, NH:2 * NH], st2t[:, 3:4], None, ALU.add)
                lo = NH if isctx else 0
                K.ts(K.dve, rh[:, lo:2 * NH], ssh[:, lo:2 * NH], 1.0 / QK, EPS, ALU.mult, ALU.add)
                K.actf(rh[:, lo:2 * NH], rh[:, lo:2 * NH], AF.Sqrt)
                K.recip(rh[:, lo:2 * NH], rh[:, lo:2 * NH])
                rp = rope[:, (b["t0"] // 128 + i), :] if not isctx else None

                def do_rope(Rv, outv):
                    C = rope.v(bc(rope.t[:, (b["t0"] // 128 + i), 0:32].unsqueeze(1), [128, NH, 32]))
                    K.tt(K.dve, T1[:, :, :], Rv, C, ALU.mult)
                    R5 = Rv.ap.rearrange("p h (a b c) -> p h a b c", a=2, b=2)
                    U5 = U.t[:, :, :].rearrange("p h (a b c) -> p h a b c", a=2, b=2)
                    S5 = rope.t[:, (b["t0"] // 128 + i), 32:64].rearrange("p (a b c) -> p a b c", a=2, b=2)
                    for hb_ in range(2):
                        K.tt(K.dve, U.v(U5[:, :, :, hb_, :]), View(Rv.buf, R5[:, :, :, 1 - hb_, :]),
                             rope.v(bc(S5[:, :, hb_, :].unsqueeze(1), [128, NH, 2, 8])), ALU.mult)
                    K.tt(K.dve, outv, T1[:, :, :], U[:, :, :], ALU.add)

                if not isctx:
                    q_ = qb[tcount % 2]
                    rq = rh.v(bc(rh.t[:, 0:NH].unsqueeze(2), [128, NH, QK]))
                    K.tt(K.dve, qf[:, :, :], qf[:, :, :], rq, ALU.mult)
                    K.tt(K.dve, q_[:, :, 0:64], qf[:, :, 0:64], QG.v(bc(QG.t[:, 0:64].unsqueeze(1), [128, NH, 64])), ALU.mult)
                    K.tt(K.dve, R[:, :, :], qf[:, :, 64:96], QG.v(bc(QG.t[:, 64:96].unsqueeze(1), [128, NH, 32])), ALU.mult)
                    do_rope(R[:, :, :], q_[:, :, 64:96])
                k_ = kb[tcount % 2]
                v_ = vb[tcount % 2]
                K.cp(K.pool, v_[:, :, 0:VD], kvf[:, :, 64:128])
                rk64 = rh.v(bc(rh.t[:, NH:2 * NH].unsqueeze(2), [128, NH, 64]))
                rk32 = rh.v(bc(rh.t[:, NH:2 * NH].unsqueeze(2), [128, NH, 32]))
                K.tt(K.dve, sq[:, :, 0:64], kvf[:, :, 0:64], rk64, ALU.mult)
                K.tt(K.dve, k_[:, :, 0:64], sq[:, :, 0:64], KG.v(bc(KG.t[:, 0:64].unsqueeze(1), [128, NH, 64])), ALU.mult)
                K.tt(K.dve, R[:, :, :], krs.v(bc(krs.t[:, :].unsqueeze(1), [128, NH, 32])), rk32, ALU.mult)
                if isctx:
                    K.tt(K.dve, k_[:, :, 64:96], R[:, :, :], KG.v(bc(KG.t[:, 64:96].unsqueeze(1), [128, NH, 32])), ALU.mult)
                else:
                    K.tt(K.dve, R[:, :, :], R[:, :, :], KG.v(bc(KG.t[:, 64:96].unsqueeze(1), [128, NH, 32])), ALU.mult)
                    do_rope(R[:, :, :], k_[:, :, 64:96])
                if not isctx:
                    for h in range(NH):
                        K.tr(pTq[0:QK, h, :], q_[:, h, :], ident[:, :], sig=(h == NH - 1))
                    K.cp(K.act, qs[0:QK, :, cs], pTq[0:QK, :, :])
                for h in range(NH):
                    K.tr(pTk[0:QK, h, :], k_[:, h, :], ident[:, :], sig=(h == NH - 1))
                K.cp(K.act, ks[0:QK, :, cs], pTk[0:QK, :, :])
                key = b["key0"] + 128 * i
                K.dma(K.sp, vE[:, key // 128, :, :], v_[:, :, :])
                tcount += 1
            if not isctx:
                K.dma(K.sp, qT[:, :, b["t0"]:b["t0"] + 512].rearrange("h d t -> d h t"), qs[0:QK, :, :])
            K.dma(K.sp, kT[:, :, b["key0"]:b["key0"] + Wd].rearrange("h d t -> d h t"), ks[0:QK, :, 0:Wd])
        K.barrier()


def phase_attn(K, I, S, qT, kT, vE, attT):
    nc = K.nc
    NK = S + CTX
    NT = NK // 128
    with contextlib.ExitStack() as st:
        kTh = [K.sb(st, [128, NK], BF16, "kTh") for _ in range(2)]
        vEh = [K.sb(st, [128, NT, VD + 1], BF16, "vEh") for _ in range(2)]
        qTh = [K.sb(st, [128, 512], BF16, "qTh") for _ in range(2)]
        pt = [K.sb(st, [128, 1024], BF16, "pt") for _ in range(3)]
        osb = [K.sb(st, [128, 512], F32, "osb") for _ in range(2)]
        rd = [K.sb(st, [128, 512], F32, "rd") for _ in range(2)]
        atts = [K.sb(st, [128, 512], BF16, "atts") for _ in range(2)]
        ones = K.sb(st, [128, 64], F32, "ones")
        K.memset(K.dve, ones[:, :], 0.0)
        K.memset(K.dve, ones[64:65, :], 1.0)
        for r__ in rd:
            K.memset(K.dve, r__[:, :], 0.0)
        ps = [K.ps(st, [128, 1024], F32, "ps") for _ in range(2)]
        po = [K.ps(st, [128, 512], F32, "po") for _ in range(2)]
        pb = K.ps(st, [128, 512], F32, "pb")
        npair = NT // 2
        units = [(h, qbk) for h in range(NH) for qbk in range(S // 512)]
        jobs = [(u, pr) for u in range(len(units)) for pr in range(npair)]

        def load_head(h):
            K.dma(K.sp, kTh[h % 2][0:QK, :], kT[h, :, :])
            K.dma(K.sp, vEh[h % 2][:, :, :], vE[:, :, h, :])

        def load_q(u):
            h, qbk = units[u]
            K.dma(K.sp, qTh[u % 2][0:QK, :], qT[h, :, qbk * 512:(qbk + 1) * 512])

        def S_(g):
            u, pr = jobs[g]
            h, qbk = units[u]
            if pr == 0 and u + 1 < len(units):
                if units[u + 1][0] != h:
                    load_head(h + 1)
                load_q(u + 1)
            kk, qq, p_ = kTh[h % 2], qTh[u % 2], ps[g % 2]
            for j in range(2):
                kt = 2 * pr + j
                K.mm(p_[:, j * 512:(j + 1) * 512], kk[0:QK, kt * 128:(kt + 1) * 128], qq[0:QK, :],
                     start=True, stop=True, sig=(j == 1))

        def E_(g):
            K.actf(pt[g % 3][:, :], ps[g % 2][:, :], AF.Exp, scale=SM_SCALE)

        def PV_(g):
            u, pr = jobs[g]
            h, qbk = units[u]
            vv, e_, o_ = vEh[h % 2], pt[g % 3], po[u % 2]
            for j in range(2):
                kt = 2 * pr + j
                K.mm(o_[0:VD + 1, :], vv[:, kt, :], e_[:, j * 512:(j + 1) * 512],
                     start=(kt == 0), stop=(kt == NT - 1), sig=(j == 1))

        def norm1(u):
            ob, r_, o_ = osb[u % 2], rd[u % 2], po[u % 2]
            K.cp(K.dve, ob[0:VD + 1, :], o_[0:VD + 1, :])
            K.recip(r_[64:65, :], ob[64:65, :])

        def norm2(u):
            h, qbk = units[u]
            ob, r_, a_ = osb[u % 2], rd[u % 2], atts[u % 2]
            K.mm(pb[0:VD, :], ones[0:VD + 1, 0:VD], r_[0:VD + 1, :], start=True, stop=True)
            K.tt(K.dve, a_[0:VD, :], ob[0:VD, :], pb[0:VD, :], ALU.mult)
            K.dma(K.pool, attT[h * VD:(h + 1) * VD, qbk * 512:(qbk + 1) * 512], a_[0:VD, :])

        load_head(0)
        load_q(0)
        S_(0)
        pending = None
        for g in range(len(jobs)):
            if g + 1 < len(jobs):
                S_(g + 1)
            E_(g)
            PV_(g)
            if pending is not None:
                norm2(pending)
                pending = None
            if jobs[g][1] == npair - 1:
                norm1(jobs[g][0])
                pending = jobs[g][0]
        if pending is not None:
            norm2(pending)
        K.barrier()


def phase_l0c(K, I, S, modrow, uT, attT, x_in, x_out):
    nc = K.nc
    HALO = CONV_K // 2
    with contextlib.ExitStack() as st:
        wout = K.sb(st, [128, 8, D], BF16, "evwout")
        cw = K.sb(st, [128, 4, CONV_K], F32, "cw31")
        cb = K.sb(st, [128, 4], F32, "cb31")
        lg = K.sb(st, [128, 4], F32, "lng")
        lb = K.sb(st, [128, 4], F32, "lnb")
        onesF = K.sb(st, [128, 128], F32, "onesF")
        for kk in range(CONV_K):
            K.cload(cw.v(cw.t[:, :, kk:kk + 1]), colvec(I["ev_conv_w"][0, kk, :], 4), allow_slow_non_contiguous=True)
        K.cload(cb.v(cb.t[:, :].rearrange("p (k o) -> p k o", o=1)), colvec(I["ev_conv_b"][0, :], 4))
        K.cload(lg.v(lg.t[:, :].rearrange("p (k o) -> p k o", o=1)), colvec(I["ev_ln_g"][0, :], 4))
        K.cload(lb.v(lb.t[:, :].rearrange("p (k o) -> p k o", o=1)), colvec(I["ev_ln_b"][0, :], 4))
        with contextlib.ExitStack() as st2:
            G = K.sb(st2, [128, D], F32, "G")
            stg = [K.sb(st2, [128, 1024], F32, "wstg") for _ in range(3)]
            K.cload(G[:, :], modrow[2 * D:3 * D].partition_broadcast(128))
            K.setup_done()
            K.memset(K.dve, onesF[:, :], 1.0 / CONV_CH)
            K.ts(K.dve, cw[:, :, :], cw[:, :, :], 0.5, None, ALU.mult)
            load_cast(K, st2, wout, I["ev_w_out"][0], 8, D, colscale=G, stg=stg)
            K.barrier()
        uw = [K.sb(st, [128, 4, 512 + 2 * HALO], F32, "uw") for _ in range(2)]
        acc = K.sb(st, [128, 4, 512], F32, "acc31")
        sqb = K.sb(st, [128, 4, 512], F32, "sq31")
        mean = K.sb(st, [128, 512], F32, "mean")
        rs = K.sb(st, [128, 512], F32, "rs")
        aT = K.sb(st, [128, 4, 512], BF16, "aT")
        at = [K.sb(st, [128, 4, 512], BF16, "attblk") for _ in range(2)]
        xr = [K.sb(st, [128, D], F32, "xr") for _ in range(2)]
        pm = K.ps(st, [128, 512], F32, "pmean")
        pq = K.ps(st, [128, 512], F32, "pmsq")
        po = [K.ps(st, [128, 512], F32, "po") for _ in range(2)]
        cnt = 0
        for j in range(S // 512):
            t0 = 512 * j
            w_ = uw[j % 2]
            lo = max(t0 - HALO, 0)
            hi = min(t0 + 512 + HALO, S)
            if lo != t0 - HALO or hi != t0 + 512 + HALO:
                K.memset(K.pool, w_[:, :, :], 0.0)
            c0 = lo - (t0 - HALO)
            K.dma(K.sp, w_[:, :, c0:c0 + (hi - lo)], uT[:, lo:hi].rearrange("(m p) t -> p m t", p=128))
            a_ = at[j % 2]
            K.dma(K.sp, a_[:, :, :], attT[:, t0:t0 + 512].rearrange("(m p) t -> p m t", p=128))
            for m in range(4):
                K.ts(K.dve, acc[:, m, :], w_[:, m, 0:512], cw[:, m, 0:1], cb[:, m:m + 1], ALU.mult, ALU.add)
                for kk in range(1, CONV_K):
                    K.stt(acc[:, m, :], w_[:, m, kk:kk + 512], cw[:, m, kk:kk + 1], acc[:, m, :], ALU.mult, ALU.add)
                K.actf(sqb[:, m, :], acc[:, m, :], AF.Square)
            for m in range(4):
                K.mm(pm[:, :], onesF[:, :], acc[:, m, :], start=(m == 0), stop=(m == 3))
            for m in range(4):
                K.mm(pq[:, :], onesF[:, :], sqb[:, m, :], start=(m == 0), stop=(m == 3))
            K.cp(K.act, mean[:, :], pm[:, :])
            K.tt(K.dve, rs[:, :], mean[:, :], mean[:, :], ALU.mult)
            K.tt(K.dve, rs[:, :], pq[:, :], rs[:, :], ALU.subtract)
            K.ts(K.dve, rs[:, :], rs[:, :], EPS, None, ALU.add)
            K.actf(rs[:, :], rs[:, :], AF.Sqrt)
            K.recip(rs[:, :], rs[:, :])
            for m in range(4):
                K.tt(K.dve, acc[:, m, :], acc[:, m, :], mean[:, :], ALU.subtract)
                K.tt(K.dve, acc[:, m, :], acc[:, m, :], rs[:, :], ALU.mult)
                K.actf(aT[:, m, :], acc[:, m, :], AF.Silu, scale=lg[:, m:m + 1], bias=lb[:, m:m + 1])
            for i in range(4):
                x = xr[cnt % 2]
                cnt += 1
                K.dma(K.sp, x[:, :], x_in[t0 + 128 * i:t0 + 128 * (i + 1), :])
                cs = slice(128 * i, 128 * (i + 1))
                for n_ in range(2):
                    for k in range(8):
                        l_ = aT[:, k, cs] if k < 4 else a_[:, k - 4, cs]
                        K.mm(po[n_][:, :], l_, wout[:, k, n_ * 512:(n_ + 1) * 512], start=(k == 0), stop=(k == 7))
                for n_ in range(2):
                    K.tt(K.dve, x[:, n_ * 512:(n_ + 1) * 512], x[:, n_ * 512:(n_ + 1) * 512], po[n_][:, :], ALU.add)
                K.dma(K.pool, x_out[t0 + 128 * i:t0 + 128 * (i + 1), :], x[:, :])
        K.barrier()


WEIGHT_NAMES = ["ada_w", "ada_b", "norm_mix_g", "norm_ffn_g", "ffn_w_up", "ffn_conv_w", "ffn_conv_b",
                "ffn_w_down", "ev_w_in", "ev_conv_w", "ev_conv_b", "ev_ln_g", "ev_ln_b", "ev_qa_norm_g", "ev_w_uq",
                "ev_kva_norm_g", "ev_w_ukv", "ev_q_norm_g", "ev_k_norm_g", "ev_w_out", "od_w_in", "od_conv_w",
                "od_conv_b", "od_w_out"]


def build(S, shapes, debug=False, phases=("mod", "a", "b", "c", "f0", "m1", "f1")):
    nc = bass.Bass("TRN2", target_bir_lowering=False)
    I = {}
    I["x"] = nc.dram_tensor("x", [S, D], F32, kind="ExternalInput").ap()
    I["c"] = nc.dram_tensor("c", [D], F32, kind="ExternalInput").ap()
    I["ctx"] = nc.dram_tensor("ctx", [CTX, D], F32, kind="ExternalInput").ap()
    I["c_ctx"] = nc.dram_tensor("c_ctx", [D], F32, kind="ExternalInput").ap()
    for n in WEIGHT_NAMES:
        I[n] = nc.dram_tensor(n, list(shapes[n]), F32, kind="ExternalInput").ap()
    I["ident"] = nc.dram_tensor("ident", [128, 128], BF16, kind="ExternalInput").ap()
    I["rope"] = nc.dram_tensor("rope", [S, 64], F32, kind="ExternalInput").ap()
    y = nc.dram_tensor("y", [S, D], F32, kind="ExternalOutput").ap()
    sk = "ExternalOutput" if debug else "Internal"
    NK = S + CTX
    modv = nc.dram_tensor("modv", [2, 6 * D], F32, kind=sk).ap()
    modc = nc.dram_tensor("modc", [1, 2 * D], F32, kind=sk).ap()
    uT = nc.dram_tensor("uT", [CONV_CH, S], F32, kind=sk).ap()
    qT = nc.dram_tensor("qT", [NH, QK, S], BF16, kind=sk).ap()
    kT = nc.dram_tensor("kT", [NH, QK, NK], BF16, kind=sk).ap()
    vE = nc.dram_tensor("vE", [128, NK // 128, NH, VD + 1], BF16, kind=sk).ap()
    attT = nc.dram_tensor("attT", [NH * VD, S], BF16, kind=sk).ap()
    xa = nc.dram_tensor("xa", [S, D], F32, kind=sk).ap()
    xb = nc.dram_tensor("xb", [S, D], F32, kind=sk).ap()
    K = Kern(nc)
    with K.es:
        if "mod" in phases:
            phase_mod(K, I, modv, modc)
        if "a" in phases:
            phase_l0a(K, I, S, modv[0, :], modc[0, :], uT, qT, kT, vE)
        if "b" in phases:
            phase_attn(K, I, S, qT, kT, vE, attT)
        if "c" in phases:
            phase_l0c(K, I, S, modv[0, :], uT, attT, I["x"], xa)
        if "f0" in phases:
            phase_ffn(K, I, 0, S, xa, xb, modv[0, :])
        if "m1" in phases:
            phase_sconv(K, I, S, xb, xa, modv[1, :])
        if "f1" in phases:
            phase_ffn(K, I, 1, S, xa, y, modv[1, :])
        K.barrier()
    return nc


def rope_table(S):
    t = np.arange(S)
    row = (t // 64).astype(np.float32)
    col = (t % 64).astype(np.float32)
    half = 16
    inv = (10000.0 ** (-np.arange(0, half, 2, dtype=np.float32) / half)).astype(np.float32)
    ar = row[:, None] * inv[None, :]
    ac = col[:, None] * inv[None, :]
    cr, sr, cc, sc = np.cos(ar), np.sin(ar), np.cos(ac), np.sin(ac)
    C = np.concatenate([cr, cr, cc, cc], axis=1)
    Sg = np.concatenate([-sr, sr, -sc, sc], axis=1)
    return np.ascontiguousarray(np.concatenate([C, Sg], axis=1).astype(np.float32))


def kernel(debug=False, phases=("mod", "a", "b", "c", "f0", "m1", "f1"), **inputs):
    x = np.asarray(inputs["x"], dtype=np.float32)
    B, S, _ = x.shape
    assert B == 8
    shapes = {n: np.asarray(inputs[n]).shape for n in WEIGHT_NAMES}
    nc = build(S, shapes, debug=debug, phases=phases)
    ident = np.eye(128, dtype=np.float32).astype(ml_dtypes.bfloat16)
    rope = rope_table(S)
    shared = {n: np.ascontiguousarray(np.asarray(inputs[n], dtype=np.float32)) for n in WEIGHT_NAMES}
    shared["c_ctx"] = np.ascontiguousarray(np.asarray(inputs["c_ctx"], dtype=np.float32))
    shared["ident"] = ident
    shared["rope"] = rope
    in_maps = []
    for b in range(B):
        m = dict(shared)
        m["x"] = np.ascontiguousarray(x[b])
        m["c"] = np.ascontiguousarray(np.asarray(inputs["c"], dtype=np.float32)[b])
        m["ctx"] = np.ascontiguousarray(np.asarray(inputs["ctx"], dtype=np.float32)[b])
        in_maps.append(m)
    res = run_bass_kernel_spmd(nc, in_maps, core_ids=list(range(B)))
    if debug:
        return res
    return np.stack([np.asarray(r["y"], dtype=np.float32) for r in res.results], axis=0)
```

```python
import contextlib
import math
import numpy as np
import ml_dtypes
import concourse.bass as bass
import concourse.mybir as mybir
from concourse.bass_utils import run_bass_kernel_spmd

F32 = mybir.dt.float32
BF16 = mybir.dt.bfloat16
AF = mybir.ActivationFunctionType
ALU = mybir.AluOpType
AX = mybir.AxisListType

D = 1024
CTX = 256
CONV_CH = 512
CONV_K = 31
NH = 8
QK = 96
VD = 64
QL = 384
KVL = 256
EVEN_IN = 1696
FFN = 2816
EPS = 1e-6
SM_SCALE = QK ** -0.5
WSTR = 510


class Eng:
    def __init__(self, name, h, sem):
        self.name = name
        self.h = h
        self.sem = sem
        self.n = 0
        self.seen = {}


class SemC:
    def __init__(self, sem):
        self.sem = sem
        self.n = 0


class View:
    __slots__ = ("buf", "ap")

    def __init__(self, buf, ap):
        self.buf = buf
        self.ap = ap


class Buf:
    def __init__(self, K, t):
        self.K = K
        self.t = t
        self.cw = {}
        self.cr = {}
        self.pw = {}
        self.pr = {}
        self.has_read = False
        self.ld = None
        self.st = None

    def __getitem__(self, key):
        return View(self, self.t[key])

    def v(self, ap):
        return View(self, ap)


class Kern:
    def __init__(self, nc):
        self.nc = nc
        self.es = contextlib.ExitStack()
        mk = lambda n, h: Eng(n, h, self.es.enter_context(nc.semaphore("sem_" + n)))
        self.pe = mk("pe", nc.tensor)
        self.act = mk("act", nc.scalar)
        self.dve = mk("dve", nc.vector)
        self.pool = mk("pool", nc.gpsimd)
        self.sp = mk("sp", nc.sync)
        self.engs = [self.pe, self.act, self.dve, self.pool, self.sp]
        self.free_sems = [SemC(self.es.enter_context(nc.semaphore("dsem%d" % i))) for i in range(88)]
        self.live = []
        self.setup_sem = self.free_sems.pop()
        self.uid = 0

    def sb(self, st, shape, dt, name=None):
        self.uid += 1
        t = st.enter_context(self.nc.sbuf_tensor("%s_%d" % (name or "sb", self.uid), list(shape), dt))
        return Buf(self, t)

    def ps(self, st, shape, dt, name=None):
        self.uid += 1
        t = st.enter_context(self.nc.psum_tensor("%s_%d" % (name or "ps", self.uid), list(shape), dt))
        return Buf(self, t)

    def _need(self, e, sem, val):
        key = id(sem)
        if e.seen.get(key, 0) >= val:
            return
        e.h.wait_ge(sem, val)
        e.seen[key] = val

    def _dep(self, e, b, tag, idx, raw):
        if tag == "LD":
            self._need(e, b.ld.sem, idx)
        elif tag == "ST":
            self._need(e, b.st.sem, idx)
        else:
            if tag is e and not raw:
                return
            self._need(e, tag.sem, idx + 1)

    def _rdeps(self, e, b):
        for tag, idx in b.cw.items():
            self._dep(e, b, tag, idx, True)

    def _wdeps(self, e, b, also_cur=False):
        if b.has_read:
            b.pr, b.pw, b.cr, b.cw, b.has_read = b.cr, b.cw, {}, {}, False
        for tag, idx in b.pr.items():
            self._dep(e, b, tag, idx, False)
        for tag, idx in b.pw.items():
            self._dep(e, b, tag, idx, False)
        if also_cur:
            for tag, idx in b.cw.items():
                self._dep(e, b, tag, idx, False)

    def issue(self, e, reads, writes, fn, sig=True, wait_cur=False):
        rb = []
        for v in reads:
            if isinstance(v, View) and v.buf not in rb:
                rb.append(v.buf)
        wb = []
        for v in writes:
            if v.buf not in wb:
                wb.append(v.buf)
        for b in rb:
            self._rdeps(e, b)
        for b in wb:
            self._wdeps(e, b, also_cur=wait_cur)
        ins = fn()
        idx = e.n
        if sig:
            ins.then_inc(e.sem, 1)
            e.n += 1
        for b in rb:
            if b not in wb:
                b.cr[e] = idx
                b.has_read = True
        for b in wb:
            b.cw[e] = idx
        return ins

    def _getsem(self, b, which):
        s = getattr(b, which)
        if s is None:
            s = self.free_sems.pop()
            setattr(b, which, s)
            if b not in self.live:
                self.live.append(b)
        return s

    def dma(self, q, out, in_, **kw):
        nc = self.nc
        if isinstance(out, View):
            b = out.buf
            self._wdeps(q, b, also_cur=True)
            s = self._getsem(b, "ld")
            q.h.dma_start(out=out.ap, in_=in_, **kw).then_inc(s.sem, 16)
            s.n += 16
            b.cw["LD"] = s.n
        else:
            b = in_.buf
            self._rdeps(q, b)
            s = self._getsem(b, "st")
            q.h.dma_start(out=out, in_=in_.ap, **kw).then_inc(s.sem, 16)
            s.n += 16
            b.cr["ST"] = s.n
            b.has_read = True

    def cload(self, out, in_, q=None, **kw):
        q = q or self.sp
        kw.setdefault("allow_slow_non_contiguous", True)
        s = self.setup_sem
        q.h.dma_start(out=out.ap, in_=in_, **kw).then_inc(s.sem, 16)
        s.n += 16

    def setup_done(self):
        for e in self.engs:
            self._need(e, self.setup_sem.sem, self.setup_sem.n)

    def barrier(self):
        for e in self.engs:
            for f in self.engs:
                if f is not e and f.n > 0:
                    self._need(e, f.sem, f.n)
            for b in self.live:
                for s in (b.ld, b.st):
                    if s is not None and s.n > 0:
                        self._need(e, s.sem, s.n)
            self._need(e, self.setup_sem.sem, self.setup_sem.n)
        for b in self.live:
            for w in ("ld", "st"):
                s = getattr(b, w)
                if s is not None:
                    self.free_sems.append(s)
                    setattr(b, w, None)
            b.cw, b.cr, b.pw, b.pr, b.has_read = {}, {}, {}, {}, False
        self.live = []

    @staticmethod
    def _a(v):
        return v.ap if isinstance(v, View) else v

    def mm(self, out, lhsT, rhs, start, stop, sig=None):
        if sig is None:
            sig = stop
        return self.issue(self.pe, [lhsT, rhs], [out],
                          lambda: self.nc.tensor.matmul(out.ap, lhsT.ap, rhs.ap, start=start, stop=stop), sig)

    def tr(self, out, in_, ident, sig=True):
        return self.issue(self.pe, [in_, ident], [out],
                          lambda: self.nc.tensor.transpose(out.ap, in_.ap, ident.ap), sig)

    def actf(self, out, in_, func, scale=None, bias=None, accum=None):
        kw = {}
        rd = [in_]
        wr = [out]
        if scale is not None:
            kw["scale"] = self._a(scale)
            rd.append(scale)
        if bias is not None:
            kw["bias"] = self._a(bias)
            rd.append(bias)
        if accum is not None:
            kw["accum_out"] = accum.ap
            wr.append(accum)
        return self.issue(self.act, rd, wr, lambda: self.nc.scalar.activation(out.ap, in_.ap, func, **kw))

    def _ve(self, e):
        return self.nc.vector if e is self.dve else self.nc.gpsimd

    def tt(self, e, out, in0, in1, op):
        return self.issue(e, [in0, in1], [out], lambda: self._ve(e).tensor_tensor(out.ap, in0.ap, in1.ap, op))

    def ts(self, e, out, in0, s1, s2, op0, op1=None):
        rd = [in0, s1, s2]
        if op1 is None:
            return self.issue(e, rd, [out], lambda: self._ve(e).tensor_scalar(out.ap, in0.ap, self._a(s1), None, op0))
        return self.issue(e, rd, [out],
                          lambda: self._ve(e).tensor_scalar(out.ap, in0.ap, self._a(s1), self._a(s2), op0, op1))

    def stt(self, out, in0, s, in1, op0, op1):
        return self.issue(self.dve, [in0, s, in1], [out],
                          lambda: self.nc.vector.scalar_tensor_tensor(out.ap, in0.ap, self._a(s), in1.ap, op0, op1))

    def cp(self, e, out, in_):
        if e is self.act:
            return self.issue(e, [in_], [out], lambda: self.nc.scalar.copy(out.ap, in_.ap))
        return self.issue(e, [in_], [out], lambda: self._ve(e).tensor_copy(out.ap, in_.ap))

    def recip(self, out, in_):
        return self.issue(self.dve, [in_], [out], lambda: self.nc.vector.reciprocal(out.ap, in_.ap))

    def red(self, out, in_, op=None):
        return self.issue(self.dve, [in_], [out],
                          lambda: self.nc.vector.tensor_reduce(out.ap, in_.ap, AX.X, op or ALU.add))

    def memset(self, e, out, val, wait_cur=False):
        return self.issue(e, [], [out], lambda: self._ve(e).memset(out.ap, val), wait_cur=wait_cur)

    def rsqrt(self, out, in_, mul, eps):
        self.ts(self.dve, out, in_, mul, eps, ALU.mult, ALU.add)
        self.actf(out, out, AF.Sqrt)
        self.recip(out, out)


def bc(ap, shape):
    return ap.broadcast_to(list(shape))


def load_cast(K, st, dst, src, nk, F, colscale=None, rowscale=None, stg=None):
    CH = 1024
    i = 0
    for k in range(nk):
        for c0 in range(0, F, CH):
            w = min(CH, F - c0)
            s = stg[i % len(stg)]
            K.dma(K.sp, s[:, 0:w], src[k * 128:(k + 1) * 128, c0:c0 + w])
            e = K.dve if i % 2 == 0 else K.pool
            d = dst[:, k, c0:c0 + w]
            if colscale is not None:
                K.tt(e, d, s[:, 0:w], colscale[:, c0:c0 + w], ALU.mult)
            elif rowscale is not None:
                K.ts(e, d, s[:, 0:w], rowscale[:, k:k + 1], None, ALU.mult)
            else:
                K.cp(e, d, s[:, 0:w])
            i += 1


def colvec(ap1d, nk):
    return ap1d.rearrange("(k p o) -> p k o", p=128, o=1)


def norm_block(K, A, x_rows, tiles, gs, sh, ident, hT, hb, xt, ss, rstd, pT, do_norm_now=True):
    nt = len(tiles)
    for (i, p_lo, p_hi, tok_lo, w) in tiles:
        x = xt[i % len(xt)]
        if p_lo > 0 or p_hi < 128:
            K.memset(K.dve, x[:, :], 0.0)
        K.dma(K.sp, x[p_lo:p_hi, :], x_rows[tok_lo:tok_lo + (p_hi - p_lo), :])
        K.actf(A["junk"][:, :], x[:, :], AF.Square, accum=ss[:, i:i + 1])
    K.rsqrt(rstd[:, 0:nt], ss[:, 0:nt], 1.0 / D, EPS)
    for (i, p_lo, p_hi, tok_lo, w) in tiles:
        x = xt[i % len(xt)]
        K.ts(K.dve, hb[i][:, :], x[:, :], rstd[:, i:i + 1], None, ALU.mult)


def transposes_block(K, tiles, gs, sh, ident, hT, hb, pT, zero_cols):
    for (i, p_lo, p_hi, tok_lo, w) in tiles:
        p = pT[i % len(pT)]
        for k in range(8):
            K.tr(p[:, k, :], hb[i][:, k * 128:(k + 1) * 128], ident[:, :], sig=(k == 7))
        for k in range(8):
            K.actf(hT[:, k, i * 128:i * 128 + w], p[:, k, 0:w], AF.Identity,
                   scale=gs[:, k:k + 1], bias=sh[:, k:k + 1])
    for c in zero_cols:
        K.memset(K.dve, hT[:, :, c:c + 1], 0.0, wait_cur=True)


def win_blocks(S):
    blocks = []
    j = 0
    while WSTR * j < S:
        nvalid = min(WSTR, S - WSTR * j)
        t0 = WSTR * j - 1
        Wd = nvalid + 2
        tiles = []
        for i in range((Wd + 127) // 128):
            a = t0 + 128 * i
            tok_lo = max(a, 0)
            tok_hi = min(a + 128, S, t0 + Wd)
            w = min(128, Wd - 128 * i)
            tiles.append((i, tok_lo - a, tok_hi - a, tok_lo, w))
        zero_cols = []
        if t0 < 0:
            zero_cols.append(0)
        if t0 + Wd - 1 >= S:
            zero_cols.append(Wd - 1)
        blocks.append(dict(j=j, t0=t0, Wd=Wd, nvalid=nvalid, tiles=tiles, zero_cols=zero_cols))
        j += 1
    return blocks


def load_mod_cols(K, st, modrow, sec_shift, sec_scale, normg, name):
    gs = K.sb(st, [128, 8], F32, name + "gs")
    sh = K.sb(st, [128, 8], F32, name + "sh")
    ng = K.sb(st, [128, 8], F32, name + "ng")
    K.cload(gs.v(gs.t[:, :].rearrange("p (k o) -> p k o", o=1)), colvec(modrow[sec_scale * D:(sec_scale + 1) * D], 8))
    K.cload(sh.v(sh.t[:, :].rearrange("p (k o) -> p k o", o=1)), colvec(modrow[sec_shift * D:(sec_shift + 1) * D], 8))
    K.cload(ng.v(ng.t[:, :].rearrange("p (k o) -> p k o", o=1)), colvec(normg, 8))
    K.setup_done()
    K.stt(gs[:, :], gs[:, :], 1.0, ng[:, :], ALU.add, ALU.mult)
    return gs, sh


def phase_mod(K, I, modv, modc):
    nc = K.nc
    with contextlib.ExitStack() as st:
        cT = K.sb(st, [128, 8], F32, "cT")
        ccT = K.sb(st, [128, 8], F32, "ccT")
        ab = K.sb(st, [1, 2 * 6 * D], F32, "ab")
        stg = [K.sb(st, [128, 8, 512], F32, "adastg") for _ in range(2)]
        row = [K.sb(st, [1, 512], F32, "modrow") for _ in range(2)]
        pm = [K.ps(st, [128, 512], F32, "pm") for _ in range(2)]
        K.cload(cT.v(cT.t[:, :].rearrange("p (k o) -> p k o", o=1)), colvec(I["c"], 8))
        K.cload(ccT.v(ccT.t[:, :].rearrange("p (k o) -> p k o", o=1)), colvec(I["c_ctx"], 8))
        K.cload(ab[:, :], I["ada_b"].rearrange("(o l) f -> o (l f)", o=1))
        K.setup_done()
        K.actf(cT[:, :], cT[:, :], AF.Silu)
        K.actf(ccT[:, :], ccT[:, :], AF.Silu)
        it = 0
        for layer in range(2):
            for g in range(12):
                s = stg[it % 2]
                K.dma(K.sp, s[:, :, :], I["ada_w"][layer, :, g * 512:(g + 1) * 512].rearrange("(k p) f -> p k f", p=128))
                srcs = [(cT, modv[layer:layer + 1, g * 512:(g + 1) * 512])]
                if layer == 0 and g < 4:
                    srcs.append((ccT, modc[0:1, g * 512:(g + 1) * 512]))
                for (vec, dst) in srcs:
                    p = pm[it % 2]
                    r = row[it % 2]
                    for k in range(8):
                        K.mm(p[0:1, :], vec[:, k:k + 1], s[:, k, :], start=(k == 0), stop=(k == 7))
                    K.tt(K.dve, r[:, :], p[0:1, :], ab[:, layer * 6 * D + g * 512: layer * 6 * D + (g + 1) * 512], ALU.add)
                    K.dma(K.pool, dst, r[:, :])
                    it += 1
        K.barrier()


def phase_ffn(K, I, layer, S, x_in, x_out, modrow):
    nc = K.nc
    NM = FFN // 128
    with contextlib.ExitStack() as st:
        wup = K.sb(st, [128, 8, 2 * FFN], BF16, "wup")
        wdn = K.sb(st, [128, NM, D], BF16, "wdn")
        ident = K.sb(st, [128, 128], BF16, "ident")
        cw = K.sb(st, [128, NM, 3], F32, "cw")
        cb = K.sb(st, [128, NM], F32, "cb")
        K.cload(ident[:, :], I["ident"])
        for kk in range(3):
            K.cload(cw.v(cw.t[:, :, kk:kk + 1]), colvec(I["ffn_conv_w"][layer, kk, :], NM), allow_slow_non_contiguous=True)
        K.cload(cb.v(cb.t[:, :].rearrange("p (k o) -> p k o", o=1)), colvec(I["ffn_conv_b"][layer, :], NM))
        gs, sh = load_mod_cols(K, st, modrow, 3, 4, I["norm_ffn_g"][layer, :], "f")
        with contextlib.ExitStack() as st2:
            G = K.sb(st2, [128, D], F32, "G")
            stg = [K.sb(st2, [128, 1024], F32, "wstg") for _ in range(3)]
            K.cload(G[:, :], modrow[5 * D:6 * D].partition_broadcast(128))
            K.setup_done()
            load_cast(K, st2, wup, I["ffn_w_up"][layer], 8, 2 * FFN, stg=stg)
            load_cast(K, st2, wdn, I["ffn_w_down"][layer], NM, D, colscale=G, stg=stg)
            K.barrier()
        hid = K.sb(st, [128, NM, 512], BF16, "hid")
        hT = K.sb(st, [128, 8, 512], BF16, "hT")
        hb = [K.sb(st, [128, D], BF16, "hb") for _ in range(4)]
        xt = [K.sb(st, [128, D], F32, "xt") for _ in range(4)]
        xr = [K.sb(st, [128, D], F32, "xr") for _ in range(2)]
        acc = [K.sb(st, [128, 512], F32, "acc") for _ in range(2)]
        sl = [K.sb(st, [128, 512], BF16, "sl") for _ in range(2)]
        A = {"junk": K.sb(st, [128, D], BF16, "junk")}
        ss = K.sb(st, [128, 4], F32, "ss")
        rstd = K.sb(st, [128, 4], F32, "rstd")
        pT = [K.ps(st, [128, 8, 128], BF16, "pT") for _ in range(2)]
        pg = [K.ps(st, [128, 512], F32, "pg") for _ in range(2)]
        pv = [K.ps(st, [128, 512], F32, "pv") for _ in range(2)]
        po = [K.ps(st, [128, 512], F32, "po") for _ in range(2)]
        blocks = win_blocks(S)

        def pre_norm(b):
            norm_block(K, A, x_in, b["tiles"], gs, sh, ident, hT, hb, xt, ss, rstd, pT)

        def pre_tr(b):
            transposes_block(K, b["tiles"], gs, sh, ident, hT, hb, pT, b["zero_cols"])

        def up(b):
            Wd = b["Wd"]
            n = Wd - 2
            for m in range(NM):
                g_, v_ = pg[m % 2], pv[m % 2]
                for k in range(8):
                    K.mm(g_[:, 0:Wd], wup[:, k, m * 128:(m + 1) * 128], hT[:, k, 0:Wd], start=(k == 0), stop=(k == 7))
                for k in range(8):
                    K.mm(v_[:, 0:Wd], wup[:, k, FFN + m * 128:FFN + (m + 1) * 128], hT[:, k, 0:Wd], start=(k == 0), stop=(k == 7))
                a = acc[m % 2]
                K.ts(K.dve, a[:, 0:n], g_[:, 0:n], cw[:, m, 0:1], None, ALU.mult)
                K.stt(a[:, 0:n], g_[:, 1:n + 1], cw[:, m, 1:2], a[:, 0:n], ALU.mult, ALU.add)
                K.stt(a[:, 0:n], g_[:, 2:n + 2], cw[:, m, 2:3], a[:, 0:n], ALU.mult, ALU.add)
                s_ = sl[m % 2]
                K.actf(s_[:, 0:n], a[:, 0:n], AF.Silu, bias=cb[:, m:m + 1])
                K.tt(K.dve, hid[:, m, 0:n], s_[:, 0:n], v_[:, 1:n + 1], ALU.mult)

        def down(b, cnt):
            nv = b["nvalid"]
            tok0 = WSTR * b["j"]
            for i2 in range((nv + 127) // 128):
                r = min(128, nv - 128 * i2)
                x = xr[cnt[0] % 2]
                cnt[0] += 1
                K.dma(K.sp, x[0:r, :], x_in[tok0 + 128 * i2: tok0 + 128 * i2 + r, :])
                for n_ in range(2):
                    for k in range(NM):
                        K.mm(po[n_][0:r, :], hid[:, k, 128 * i2:128 * i2 + r], wdn[:, k, n_ * 512:(n_ + 1) * 512],
                             start=(k == 0), stop=(k == NM - 1))
                for n_ in range(2):
                    K.tt(K.dve, x[0:r, n_ * 512:(n_ + 1) * 512], x[0:r, n_ * 512:(n_ + 1) * 512], po[n_][0:r, :], ALU.add)
                K.dma(K.pool, x_out[tok0 + 128 * i2: tok0 + 128 * i2 + r, :], x[0:r, :])

        cnt = [0]
        pre_norm(blocks[0])
        pre_tr(blocks[0])
        for bi, b in enumerate(blocks):
            nxt = blocks[bi + 1] if bi + 1 < len(blocks) else None
            if nxt:
                pre_norm(nxt)
            up(b)
            if nxt:
                pre_tr(nxt)
            down(b, cnt)
        K.barrier()


def phase_sconv(K, I, S, x_in, x_out, modrow):
    nc = K.nc
    with contextlib.ExitStack() as st:
        win = K.sb(st, [128, 8, 3 * D], BF16, "odwin")
        wout = K.sb(st, [128, 8, D], BF16, "odwout")
        ident = K.sb(st, [128, 128], BF16, "ident")
        cw = K.sb(st, [128, 8, 3], F32, "cw")
        cb = K.sb(st, [128, 8], F32, "cb")
        K.cload(ident[:, :], I["ident"])
        for kk in range(3):
            K.cload(cw.v(cw.t[:, :, kk:kk + 1]), colvec(I["od_conv_w"][0, kk, :], 8), allow_slow_non_contiguous=True)
        K.cload(cb.v(cb.t[:, :].rearrange("p (k o) -> p k o", o=1)), colvec(I["od_conv_b"][0, :], 8))
        gs, sh = load_mod_cols(K, st, modrow, 0, 1, I["norm_mix_g"][1, :], "m1")
        with contextlib.ExitStack() as st2:
            G = K.sb(st2, [128, D], F32, "G")
            stg = [K.sb(st2, [128, 1024], F32, "wstg") for _ in range(3)]
            K.cload(G[:, :], modrow[2 * D:3 * D].partition_broadcast(128))
            K.setup_done()
            load_cast(K, st2, win, I["od_w_in"][0], 8, 3 * D, stg=stg)
            load_cast(K, st2, wout, I["od_w_out"][0], 8, D, colscale=G, stg=stg)
            K.barrier()
        yT = K.sb(st, [128, 8, 512], BF16, "yT")
        hT = K.sb(st, [128, 8, 512], BF16, "hT")
        hb = [K.sb(st, [128, D], BF16, "hb") for _ in range(4)]
        xt = [K.sb(st, [128, D], F32, "xt") for _ in range(4)]
        xr = [K.sb(st, [128, D], F32, "xr") for _ in range(2)]
        acc = [K.sb(st, [128, 512], F32, "acc") for _ in range(2)]
        us = [K.sb(st, [128, 512], F32, "us") for _ in range(2)]
        A = {"junk": K.sb(st, [128, D], BF16, "junk")}
        ss = K.sb(st, [128, 4], F32, "ss")
        rstd = K.sb(st, [128, 4], F32, "rstd")
        pT = [K.ps(st, [128, 8, 128], BF16, "pT") for _ in range(1)]
        pb = [K.ps(st, [128, 512], F32, "pb") for _ in range(2)]
        pc = [K.ps(st, [128, 512], F32, "pc") for _ in range(2)]
        pu = [K.ps(st, [128, 512], F32, "pu") for _ in range(1)]
        po = [K.ps(st, [128, 512], F32, "po") for _ in range(2)]
        blocks = win_blocks(S)

        def mix(b):
            Wd = b["Wd"]
            n = Wd - 2
            for m in range(8):
                b_, c_, u_ = pb[m % 2], pc[m % 2], pu[0]
                for (dst, off) in ((u_, 2 * D), (c_, D), (b_, 0)):
                    for k in range(8):
                        K.mm(dst[:, 0:Wd], win[:, k, off + m * 128: off + (m + 1) * 128], hT[:, k, 0:Wd],
                             start=(k == 0), stop=(k == 7))
                u = us[m % 2]
                K.cp(K.act, u[:, 0:Wd], u_[:, 0:Wd])
                K.tt(K.dve, u[:, 0:Wd], u[:, 0:Wd], c_[:, 0:Wd], ALU.mult)
                a = acc[m % 2]
                K.ts(K.dve, a[:, 0:n], u[:, 0:n], cw[:, m, 0:1], cb[:, m:m + 1], ALU.mult, ALU.add)
                K.stt(a[:, 0:n], u[:, 1:n + 1], cw[:, m, 1:2], a[:, 0:n], ALU.mult, ALU.add)
                K.stt(a[:, 0:n], u[:, 2:n + 2], cw[:, m, 2:3], a[:, 0:n], ALU.mult, ALU.add)
                K.tt(K.dve, yT[:, m, 0:n], a[:, 0:n], b_[:, 1:n + 1], ALU.mult)

        def outp(b, cnt):
            nv = b["nvalid"]
            tok0 = WSTR * b["j"]
            for i2 in range((nv + 127) // 128):
                r = min(128, nv - 128 * i2)
                x = xr[cnt[0] % 2]
                cnt[0] += 1
                K.dma(K.sp, x[0:r, :], x_in[tok0 + 128 * i2: tok0 + 128 * i2 + r, :])
                for n_ in range(2):
                    for k in range(8):
                        K.mm(po[n_][0:r, :], yT[:, k, 128 * i2:128 * i2 + r], wout[:, k, n_ * 512:(n_ + 1) * 512],
                             start=(k == 0), stop=(k == 7))
                for n_ in range(2):
                    K.tt(K.dve, x[0:r, n_ * 512:(n_ + 1) * 512], x[0:r, n_ * 512:(n_ + 1) * 512], po[n_][0:r, :], ALU.add)
                K.dma(K.pool, x_out[tok0 + 128 * i2: tok0 + 128 * i2 + r, :], x[0:r, :])

        cnt = [0]
        norm_block(K, A, x_in, blocks[0]["tiles"], gs, sh, ident, hT, hb, xt, ss, rstd, pT)
        transposes_block(K, blocks[0]["tiles"], gs, sh, ident, hT, hb, pT, blocks[0]["zero_cols"])
        for bi, b in enumerate(blocks):
            nxt = blocks[bi + 1] if bi + 1 < len(blocks) else None
            if nxt:
                norm_block(K, A, x_in, nxt["tiles"], gs, sh, ident, hT, hb, xt, ss, rstd, pT)
            mix(b)
            if nxt:
                transposes_block(K, nxt["tiles"], gs, sh, ident, hT, hb, pT, nxt["zero_cols"])
            outp(b, cnt)
        K.barrier()


def phase_l0a(K, I, S, modrow, modc, uT, qT, kT, vE):
    nc = K.nc
    NK = S + CTX
    with contextlib.ExitStack() as st:
        win = K.sb(st, [128, 8, EVEN_IN], BF16, "evwin")
        wuq = K.sb(st, [128, 3, NH * QK], BF16, "wuq")
        wukv = K.sb(st, [128, 2, NH * 128], BF16, "wukv")
        ident = K.sb(st, [128, 128], BF16, "ident")
        QG = K.sb(st, [128, QK], F32, "QG")
        KG = K.sb(st, [128, QK], F32, "KG")
        rope = K.sb(st, [128, S // 128, 64], F32, "rope")
        qag = K.sb(st, [128, 3], F32, "qag")
        kvag = K.sb(st, [128, 2], F32, "kvag")
        K.cload(ident[:, :], I["ident"])
        K.cload(QG[:, :], I["ev_q_norm_g"][0, :].partition_broadcast(128))
        K.cload(KG[:, :], I["ev_k_norm_g"][0, :].partition_broadcast(128))
        K.cload(rope[:, :, :], I["rope"].rearrange("(i p) c -> p i c", p=128))
        K.cload(qag.v(qag.t[:, :].rearrange("p (k o) -> p k o", o=1)), colvec(I["ev_qa_norm_g"][0, :], 3))
        K.cload(kvag.v(kvag.t[:, :].rearrange("p (k o) -> p k o", o=1)), colvec(I["ev_kva_norm_g"][0, :], 2))
        gs, sh = load_mod_cols(K, st, modrow, 0, 1, I["norm_mix_g"][0, :], "m0")
        gsc, shc = load_mod_cols(K, st, modc, 0, 1, I["norm_mix_g"][0, :], "m0c")
        with contextlib.ExitStack() as st2:
            stg = [K.sb(st2, [128, 1024], F32, "wstg") for _ in range(3)]
            load_cast(K, st2, win, I["ev_w_in"][0], 8, EVEN_IN, stg=stg)
            load_cast(K, st2, wuq, I["ev_w_uq"][0], 3, NH * QK, rowscale=qag, stg=stg)
            load_cast(K, st2, wukv, I["ev_w_ukv"][0], 2, NH * 128, rowscale=kvag, stg=stg)
            K.barrier()
        hT = K.sb(st, [128, 8, 512], BF16, "hT")
        hb = [K.sb(st, [128, D], BF16, "hb") for _ in range(4)]
        xt = [K.sb(st, [128, D], F32, "xt") for _ in range(4)]
        A = {"junk": K.sb(st, [128, D], BF16, "junk")}
        ss = K.sb(st, [128, 4], F32, "ss")
        rstd = K.sb(st, [128, 4], F32, "rstd")
        th = [K.sb(st, [128, 512], F32, "th") for _ in range(2)]
        ust = [K.sb(st, [128, 512], F32, "ust") for _ in range(2)]
        ccT = K.sb(st, [128, 5, 512], BF16, "ccT")
        krs = K.sb(st, [128, 32], F32, "krs")
        st2t = K.sb(st, [128, 4], F32, "st2")
        r2 = K.sb(st, [128, 2], F32, "r2")
        qf = K.sb(st, [128, NH, QK], F32, "qf")
        kvf = K.sb(st, [128, NH, 128], F32, "kvf")
        sq = K.sb(st, [128, NH, QK], F32, "sq")
        ssh = K.sb(st, [128, 2 * NH], F32, "ssh")
        rh = K.sb(st, [128, 2 * NH], F32, "rh")
        R = K.sb(st, [128, NH, 32], F32, "R")
        T1 = K.sb(st, [128, NH, 32], F32, "T1")
        U = K.sb(st, [128, NH, 32], F32, "U")
        qb = [K.sb(st, [128, NH, QK], BF16, "qb") for _ in range(2)]
        kb = [K.sb(st, [128, NH, QK], BF16, "kb") for _ in range(2)]
        vb = [K.sb(st, [128, NH, VD + 1], BF16, "vb") for _ in range(2)]
        qTs = [K.sb(st, [128, NH, 512], BF16, "qTs") for _ in range(2)]
        kTs = [K.sb(st, [128, NH, 512], BF16, "kTs") for _ in range(2)]
        for v_ in vb:
            K.memset(K.dve, v_[:, :, VD:VD + 1], 1.0)
        pT = [K.ps(st, [128, 8, 128], BF16, "pT")]
        pA = [K.ps(st, [128, 512], F32, "pA") for _ in range(2)]
        pQ = [K.ps(st, [128, 512], F32, "pQ") for _ in range(2)]
        pTq = K.ps(st, [128, 8, 128], BF16, "pTq")
        pTk = K.ps(st, [128, 8, 128], BF16, "pTk")
        blks = [dict(ctx=True, x=I["ctx"], t0=0, nt=CTX // 128, key0=0)]
        for j in range(S // 512):
            blks.append(dict(ctx=False, x=I["x"], t0=512 * j, nt=4, key0=CTX + 512 * j))
        tcount = 0
        for bi, b in enumerate(blks):
            nt = b["nt"]
            Wd = nt * 128
            isctx = b["ctx"]
            g_, s_ = (gsc, shc) if isctx else (gs, sh)
            tiles = [(i, 0, 128, b["t0"] + 128 * i, 128) for i in range(nt)]
            norm_block(K, A, b["x"], tiles, g_, s_, ident, hT, hb, xt, ss, rstd, pT)
            transposes_block(K, tiles, g_, s_, ident, hT, hb, pT, [])
            pi = 0
            if not isctx:
                for m in range(4):
                    pv_, pg_ = pA[0], pA[1]
                    for k in range(8):
                        K.mm(pv_[:, 0:Wd], win[:, k, m * 128:(m + 1) * 128], hT[:, k, 0:Wd], start=(k == 0), stop=(k == 7))
                    for k in range(8):
                        K.mm(pg_[:, 0:Wd], win[:, k, 512 + m * 128:512 + (m + 1) * 128], hT[:, k, 0:Wd], start=(k == 0), stop=(k == 7))
                    t_ = th[m % 2]
                    K.actf(t_[:, :], pg_[:, :], AF.Tanh, scale=0.5)
                    u_ = ust[m % 2]
                    K.stt(u_[:, :], t_[:, :], 1.0, pv_[:, :], ALU.add, ALU.mult)
                    K.dma(K.pool, uT[m * 128:(m + 1) * 128, b["t0"]:b["t0"] + 512], u_[:, :])
            for m in range(5):
                if isctx and m < 3:
                    continue
                p_ = pA[m % 2]
                for k in range(8):
                    K.mm(p_[:, 0:Wd], win[:, k, 1024 + m * 128:1024 + (m + 1) * 128], hT[:, k, 0:Wd], start=(k == 0), stop=(k == 7))
                K.cp(K.dve, ccT[:, m, 0:Wd], p_[:, 0:Wd])
            qs, ks = qTs[bi % 2], kTs[bi % 2]
            for i in range(nt):
                cs = slice(i * 128, (i + 1) * 128)
                p1, p2 = pA[0], pA[1]
                for k in range(8):
                    K.mm(p1[:, :], hT[:, k, cs], win[:, k, 1024:1536], start=(k == 0), stop=(k == 7))
                for k in range(8):
                    K.mm(p2[:, 0:160], hT[:, k, cs], win[:, k, 1536:1696], start=(k == 0), stop=(k == 7))
                J = A["junk"]
                K.actf(J[:, 0:384], p1[:, 0:384], AF.Square, accum=st2t[:, 0:1])
                K.actf(J[:, 0:128], p1[:, 384:512], AF.Square, accum=st2t[:, 1:2])
                K.actf(J[:, 0:128], p2[:, 0:128], AF.Square, accum=st2t[:, 2:3])
                K.actf(krs[:, :], p2[:, 128:160], AF.Identity)
                K.actf(J[:, 0:32], p2[:, 128:160], AF.Square, accum=st2t[:, 3:4])
                K.tt(K.dve, st2t[:, 1:2], st2t[:, 1:2], st2t[:, 2:3], ALU.add)
                K.ts(K.dve, r2[:, 0:1], st2t[:, 0:1], 1.0 / QL, EPS, ALU.mult, ALU.add)
                K.ts(K.dve, r2[:, 1:2], st2t[:, 1:2], 1.0 / KVL, EPS, ALU.mult, ALU.add)
                K.actf(r2[:, :], r2[:, :], AF.Sqrt)
                K.recip(r2[:, :], r2[:, :])
                qfl = qf.v(qf.t[:, :, :].rearrange("p h d -> p (h d)"))
                kvfl = kvf.v(kvf.t[:, :, :].rearrange("p h d -> p (h d)"))
                if not isctx:
                    pq0, pq1 = pQ[0], pQ[1]
                    for k in range(3):
                        K.mm(pq0[:, :], ccT[:, k, cs], wuq[:, k, 0:512], start=(k == 0), stop=(k == 2))
                    for k in range(3):
                        K.mm(pq1[:, 0:256], ccT[:, k, cs], wuq[:, k, 512:768], start=(k == 0), stop=(k == 2))
                    K.actf(qf.v(qfl.ap[:, 0:512]), pq0[:, :], AF.Identity, scale=r2[:, 0:1])
                    K.actf(qf.v(qfl.ap[:, 512:768]), pq1[:, 0:256], AF.Identity, scale=r2[:, 0:1])
                    K.tt(K.dve, sq[:, :, :], qf[:, :, :], qf[:, :, :], ALU.mult)
                    K.red(ssh[:, 0:NH], sq[:, :, :])
                pk0, pk1 = pQ[0], pQ[1]
                for n_, pk in enumerate((pk0, pk1)):
                    for k in range(2):
                        K.mm(pk[:, :], ccT[:, 3 + k, cs], wukv[:, k, n_ * 512:(n_ + 1) * 512], start=(k == 0), stop=(k == 1))
                K.actf(kvf.v(kvfl.ap[:, 0:512]), pk0[:, :], AF.Identity, scale=r2[:, 1:2])
                K.actf(kvf.v(kvfl.ap[:, 512:1024]), pk1[:, :], AF.Identity, scale=r2[:, 1:2])
                K.tt(K.dve, sq[:, :, 0:64], kvf[:, :, 0:64], kvf[:, :, 0:64], ALU.mult)
                K.red(ssh[:, NH:2 * NH], sq[:, :, 0:64])
                K.ts(K.dve, ssh[:, NH:2 * NH], ssh[:, NH:2 * NH], st2t[:, 3:4], None, ALU.add)
                lo = NH if isctx else 0
                K.ts(K.dve, rh[:, lo:2 * NH], ssh[:, lo:2 * NH], 1.0 / QK, EPS, ALU.mult, ALU.add)
                K.actf(rh[:, lo:2 * NH], rh[:, lo:2 * NH], AF.Sqrt)
                K.recip(rh[:, lo:2 * NH], rh[:, lo:2 * NH])
                rp = rope[:, (b["t0"] // 128 + i), :] if not isctx else None

                def do_rope(Rv, outv):
                    C = rope.v(bc(rope.t[:, (b["t0"] // 128 + i), 0:32].unsqueeze(1), [128, NH, 32]))
                    K.tt(K.dve, T1[:, :, :], Rv, C, ALU.mult)
                    R5 = Rv.ap.rearrange("p h (a b c) -> p h a b c", a=2, b=2)
                    U5 = U.t[:, :, :].rearrange("p h (a b c) -> p h a b c", a=2, b=2)
                    S5 = rope.t[:, (b["t0"] // 128 + i), 32:64].rearrange("p (a b c) -> p a b c", a=2, b=2)
                    for hb_ in range(2):
                        K.tt(K.dve, U.v(U5[:, :, :, hb_, :]), View(Rv.buf, R5[:, :, :, 1 - hb_, :]),
                             rope.v(bc(S5[:, :, hb_, :].unsqueeze(1), [128, NH, 2, 8])), ALU.mult)
                    K.tt(K.dve, outv, T1[:, :, :], U[:, :, :], ALU.add)

                if not isctx:
                    q_ = qb[tcount % 2]
                    rq = rh.v(bc(rh.t[:, 0:NH].unsqueeze(2), [128, NH, QK]))
                    K.tt(K.dve, qf[:, :, :], qf[:, :, :], rq, ALU.mult)
                    K.tt(K.dve, q_[:, :, 0:64], qf[:, :, 0:64], QG.v(bc(QG.t[:, 0:64].unsqueeze(1), [128, NH, 64])), ALU.mult)
                    K.tt(K.dve, R[:, :, :], qf[:, :, 64:96], QG.v(bc(QG.t[:, 64:96].unsqueeze(1), [128, NH, 32])), ALU.mult)
                    do_rope(R[:, :, :], q_[:, :, 64:96])
                k_ = kb[tcount % 2]
                v_ = vb[tcount % 2]
                K.cp(K.pool, v_[:, :, 0:VD], kvf[:, :, 64:128])
                rk64 = rh.v(bc(rh.t[:, NH:2 * NH].unsqueeze(2), [128, NH, 64]))
                rk32 = rh.v(bc(rh.t[:, NH:2 * NH].unsqueeze(2), [128, NH, 32]))
                K.tt(K.dve, sq[:, :, 0:64], kvf[:, :, 0:64], rk64, ALU.mult)
                K.tt(K.dve, k_[:, :, 0:64], sq[:, :, 0:64], KG.v(bc(KG.t[:, 0:64].unsqueeze(1), [128, NH, 64])), ALU.mult)
                K.tt(K.dve, R[:, :, :], krs.v(bc(krs.t[:, :].unsqueeze(1), [128, NH, 32])), rk32, ALU.mult)
                if isctx:
                    K.tt(K.dve, k_[:, :, 64:96], R[:, :, :], KG.v(bc(KG.t[:, 64:96].unsqueeze(1), [128, NH, 32])), ALU.mult)
                else:
                    K.tt(K.dve, R[:, :, :], R[:, :, :], KG.v(bc(KG.t[:, 64:96].unsqueeze(1), [128, NH, 32])), ALU.mult)
                    do_rope(R[:, :, :], k_[:, :, 64:96])
                if not isctx:
                    for h in range(NH):
                        K.tr(pTq[0:QK, h, :], q_[:, h, :], ident[:, :], sig=(h == NH - 1))
                    K.cp(K.act, qs[0:QK, :, cs], pTq[0:QK, :, :])
                for h in range(NH):
                    K.tr(pTk[0:QK, h, :], k_[:, h, :], ident[:, :], sig=(h == NH - 1))
                K.cp(K.act, ks[0:QK, :, cs], pTk[0:QK, :, :])
                key = b["key0"] + 128 * i
                K.dma(K.sp, vE[:, key // 128, :, :], v_[:, :, :])
                tcount += 1
            if not isctx:
                K.dma(K.sp, qT[:, :, b["t0"]:b["t0"] + 512].rearrange("h d t -> d h t"), qs[0:QK, :, :])
            K.dma(K.sp, kT[:, :, b["key0"]:b["key0"] + Wd].rearrange("h d t -> d h t"), ks[0:QK, :, 0:Wd])
        K.barrier()


def phase_attn(K, I, S, qT, kT, vE, attT):
    nc = K.nc
    NK = S + CTX
    NT = NK // 128
    with contextlib.ExitStack() as st:
        kTh = [K.sb(st, [128, NK], BF16, "kTh") for _ in range(2)]
        vEh = [K.sb(st, [128, NT, VD + 1], BF16, "vEh") for _ in range(2)]
        qTh = [K.sb(st, [128, 512], BF16, "qTh") for _ in range(2)]
        pt = [K.sb(st, [128, 1024], BF16, "pt") for _ in range(3)]
        osb = [K.sb(st, [128, 512], F32, "osb") for _ in range(2)]
        rd = [K.sb(st, [128, 512], F32, "rd") for _ in range(2)]
        atts = [K.sb(st, [128, 512], BF16, "atts") for _ in range(2)]
        ones = K.sb(st, [128, 64], F32, "ones")
        K.memset(K.dve, ones[:, :], 0.0)
        K.memset(K.dve, ones[64:65, :], 1.0)
        for r__ in rd:
            K.memset(K.dve, r__[:, :], 0.0)
        ps = [K.ps(st, [128, 1024], F32, "ps") for _ in range(2)]
        po = [K.ps(st, [128, 512], F32, "po") for _ in range(2)]
        pb = K.ps(st, [128, 512], F32, "pb")
        npair = NT // 2
        units = [(h, qbk) for h in range(NH) for qbk in range(S // 512)]
        jobs = [(u, pr) for u in range(len(units)) for pr in range(npair)]

        def load_head(h):
            K.dma(K.sp, kTh[h % 2][0:QK, :], kT[h, :, :])
            K.dma(K.sp, vEh[h % 2][:, :, :], vE[:, :, h, :])

        def load_q(u):
            h, qbk = units[u]
            K.dma(K.sp, qTh[u % 2][0:QK, :], qT[h, :, qbk * 512:(qbk + 1) * 512])

        def S_(g):
            u, pr = jobs[g]
            h, qbk = units[u]
            if pr == 0 and u + 1 < len(units):
                if units[u + 1][0] != h:
                    load_head(h + 1)
                load_q(u + 1)
            kk, qq, p_ = kTh[h % 2], qTh[u % 2], ps[g % 2]
            for j in range(2):
                kt = 2 * pr + j
                K.mm(p_[:, j * 512:(j + 1) * 512], kk[0:QK, kt * 128:(kt + 1) * 128], qq[0:QK, :],
                     start=True, stop=True, sig=(j == 1))

        def E_(g):
            K.actf(pt[g % 3][:, :], ps[g % 2][:, :], AF.Exp, scale=SM_SCALE)

        def PV_(g):
            u, pr = jobs[g]
            h, qbk = units[u]
            vv, e_, o_ = vEh[h % 2], pt[g % 3], po[u % 2]
            for j in range(2):
                kt = 2 * pr + j
                K.mm(o_[0:VD + 1, :], vv[:, kt, :], e_[:, j * 512:(j + 1) * 512],
                     start=(kt == 0), stop=(kt == NT - 1), sig=(j == 1))

        def norm1(u):
            ob, r_, o_ = osb[u % 2], rd[u % 2], po[u % 2]
            K.cp(K.dve, ob[0:VD + 1, :], o_[0:VD + 1, :])
            K.recip(r_[64:65, :], ob[64:65, :])

        def norm2(u):
            h, qbk = units[u]
            ob, r_, a_ = osb[u % 2], rd[u % 2], atts[u % 2]
            K.mm(pb[0:VD, :], ones[0:VD + 1, 0:VD], r_[0:VD + 1, :], start=True, stop=True)
            K.tt(K.dve, a_[0:VD, :], ob[0:VD, :], pb[0:VD, :], ALU.mult)
            K.dma(K.pool, attT[h * VD:(h + 1) * VD, qbk * 512:(qbk + 1) * 512], a_[0:VD, :])

        load_head(0)
        load_q(0)
        S_(0)
        pending = None
        for g in range(len(jobs)):
            if g + 1 < len(jobs):
                S_(g + 1)
            E_(g)
            PV_(g)
            if pending is not None:
                norm2(pending)
                pending = None
            if jobs[g][1] == npair - 1:
                norm1(jobs[g][0])
                pending = jobs[g][0]
        if pending is not None:
            norm2(pending)
        K.barrier()


def phase_l0c(K, I, S, modrow, uT, attT, x_in, x_out):
    nc = K.nc
    HALO = CONV_K // 2
    with contextlib.ExitStack() as st:
        wout = K.sb(st, [128, 8, D], BF16, "evwout")
        cw = K.sb(st, [128, 4, CONV_K], F32, "cw31")
        cb = K.sb(st, [128, 4], F32, "cb31")
        lg = K.sb(st, [128, 4], F32, "lng")
        lb = K.sb(st, [128, 4], F32, "lnb")
        onesF = K.sb(st, [128, 128], F32, "onesF")
        identb = K.sb(st, [128, 128], BF16, "identb")
        identF = K.sb(st, [128, 128], F32, "identF")
        diag = K.sb(st, [128, 4, CONV_K, 128], BF16, "diag")
        K.cload(identb[:, :], I["ident"])
        for kk in range(CONV_K):
            K.cload(cw.v(cw.t[:, :, kk:kk + 1]), colvec(I["ev_conv_w"][0, kk, :], 4), allow_slow_non_contiguous=True)
        K.cload(cb.v(cb.t[:, :].rearrange("p (k o) -> p k o", o=1)), colvec(I["ev_conv_b"][0, :], 4))
        K.cload(lg.v(lg.t[:, :].rearrange("p (k o) -> p k o", o=1)), colvec(I["ev_ln_g"][0, :], 4))
        K.cload(lb.v(lb.t[:, :].rearrange("p (k o) -> p k o", o=1)), colvec(I["ev_ln_b"][0, :], 4))
        with contextlib.ExitStack() as st2:
            G = K.sb(st2, [128, D], F32, "G")
            stg = [K.sb(st2, [128, 1024], F32, "wstg") for _ in range(3)]
            K.cload(G[:, :], modrow[2 * D:3 * D].partition_broadcast(128))
            K.setup_done()
            K.memset(K.dve, onesF[:, :], 1.0 / CONV_CH)
            K.ts(K.dve, cw[:, :, :], cw[:, :, :], 0.5, None, ALU.mult)
            K.cp(K.dve, identF[:, :], identb[:, :])
            for m in range(4):
                for kk in range(CONV_K):
                    K.ts(K.dve, diag[:, m, kk, :], identF[:, :], cw[:, m, kk:kk + 1], None, ALU.mult)
            load_cast(K, st2, wout, I["ev_w_out"][0], 8, D, colscale=G, stg=stg)
            K.barrier()
        uw = [K.sb(st, [128, 4, 512 + 2 * HALO], F32, "uw") for _ in range(2)]
        uwa = [K.sb(st, [128, 4, 544], BF16, "uwa") for _ in range(2)]
        uwb = [K.sb(st, [128, 4, 544], BF16, "uwb") for _ in range(2)]
        pcv = [K.ps(st, [128, 512], F32, "pcv") for _ in range(2)]
        acc = K.sb(st, [128, 4, 512], F32, "acc31")
        sqb = K.sb(st, [128, 4, 512], F32, "sq31")
        mean = K.sb(st, [128, 512], F32, "mean")
        rs = K.sb(st, [128, 512], F32, "rs")
        aT = K.sb(st, [128, 4, 512], BF16, "aT")
        at = [K.sb(st, [128, 4, 512], BF16, "attblk") for _ in range(2)]
        xr = [K.sb(st, [128, D], F32, "xr") for _ in range(2)]
        pm = K.ps(st, [128, 512], F32, "pmean")
        pq = K.ps(st, [128, 512], F32, "pmsq")
        po = [K.ps(st, [128, 512], F32, "po") for _ in range(2)]
        cnt = 0
        for j in range(S // 512):
            t0 = 512 * j
            w_ = uw[j % 2]
            lo = max(t0 - HALO, 0)
            hi = min(t0 + 512 + HALO, S)
            if lo != t0 - HALO or hi != t0 + 512 + HALO:
                K.memset(K.pool, w_[:, :, :], 0.0)
            c0 = lo - (t0 - HALO)
            K.dma(K.sp, w_[:, :, c0:c0 + (hi - lo)], uT[:, lo:hi].rearrange("(m p) t -> p m t", p=128))
            a_ = at[j % 2]
            K.dma(K.sp, a_[:, :, :], attT[:, t0:t0 + 512].rearrange("(m p) t -> p m t", p=128))
            wa, wb = uwa[j % 2], uwb[j % 2]
            K.cp(K.act, wa[:, :, 0:542], w_[:, :, :])
            K.cp(K.dve, wb[:, :, 1:543], w_[:, :, :])
            for m in range(4):
                pc_ = pcv[m % 2]
                for kk in range(CONV_K):
                    mv = wa[:, m, kk:kk + 512] if kk % 2 == 0 else wb[:, m, kk + 1:kk + 513]
                    K.mm(pc_[:, :], diag[:, m, kk, :], mv, start=(kk == 0), stop=(kk == CONV_K - 1))
                K.ts(K.dve, acc[:, m, :], pc_[:, :], cb[:, m:m + 1], None, ALU.add)
                K.actf(sqb[:, m, :], acc[:, m, :], AF.Square)
            for m in range(4):
                K.mm(pm[:, :], onesF[:, :], acc[:, m, :], start=(m == 0), stop=(m == 3))
            for m in range(4):
                K.mm(pq[:, :], onesF[:, :], sqb[:, m, :], start=(m == 0), stop=(m == 3))
            K.cp(K.act, mean[:, :], pm[:, :])
            K.tt(K.dve, rs[:, :], mean[:, :], mean[:, :], ALU.mult)
            K.tt(K.dve, rs[:, :], pq[:, :], rs[:, :], ALU.subtract)
            K.ts(K.dve, rs[:, :], rs[:, :], EPS, None, ALU.add)
            K.actf(rs[:, :], rs[:, :], AF.Sqrt)
            K.recip(rs[:, :], rs[:, :])
            for m in range(4):
                K.tt(K.dve, acc[:, m, :], acc[:, m, :], mean[:, :], ALU.subtract)
                K.tt(K.dve, acc[:, m, :], acc[:, m, :], rs[:, :], ALU.mult)
                K.actf(aT[:, m, :], acc[:, m, :], AF.Silu, scale=lg[:, m:m + 1], bias=lb[:, m:m + 1])
            for i in range(4):
                x = xr[cnt % 2]
                cnt += 1
                K.dma(K.sp, x[:, :], x_in[t0 + 128 * i:t0 + 128 * (i + 1), :])
                cs = slice(128 * i, 128 * (i + 1))
                for n_ in range(2):
                    for k in range(8):
                        l_ = aT[:, k, cs] if k < 4 else a_[:, k - 4, cs]
                        K.mm(po[n_][:, :], l_, wout[:, k, n_ * 512:(n_ + 1) * 512], start=(k == 0), stop=(k == 7))
                for n_ in range(2):
                    K.tt(K.dve, x[:, n_ * 512:(n_ + 1) * 512], x[:, n_ * 512:(n_ + 1) * 512], po[n_][:, :], ALU.add)
                K.dma(K.pool, x_out[t0 + 128 * i:t0 + 128 * (i + 1), :], x[:, :])
        K.barrier()


WEIGHT_NAMES = ["ada_w", "ada_b", "norm_mix_g", "norm_ffn_g", "ffn_w_up", "ffn_conv_w", "ffn_conv_b",
                "ffn_w_down", "ev_w_in", "ev_conv_w", "ev_conv_b", "ev_ln_g", "ev_ln_b", "ev_qa_norm_g", "ev_w_uq",
                "ev_kva_norm_g", "ev_w_ukv", "ev_q_norm_g", "ev_k_norm_g", "ev_w_out", "od_w_in", "od_conv_w",
                "od_conv_b", "od_w_out"]


def build(S, shapes, debug=False, phases=("mod", "a", "b", "c", "f0", "m1", "f1")):
    nc = bass.Bass("TRN2", target_bir_lowering=False)
    I = {}
    I["x"] = nc.dram_tensor("x", [S, D], F32, kind="ExternalInput").ap()
    I["c"] = nc.dram_tensor("c", [D], F32, kind="ExternalInput").ap()
    I["ctx"] = nc.dram_tensor("ctx", [CTX, D], F32, kind="ExternalInput").ap()
    I["c_ctx"] = nc.dram_tensor("c_ctx", [D], F32, kind="ExternalInput").ap()
    for n in WEIGHT_NAMES:
        I[n] = nc.dram_tensor(n, list(shapes[n]), F32, kind="ExternalInput").ap()
    I["ident"] = nc.dram_tensor("ident", [128, 128], BF16, kind="ExternalInput").ap()
    I["rope"] = nc.dram_tensor("rope", [S, 64], F32, kind="ExternalInput").ap()
    y = nc.dram_tensor("y", [S, D], F32, kind="ExternalOutput").ap()
    sk = "ExternalOutput" if debug else "Internal"
    NK = S + CTX
    modv = nc.dram_tensor("modv", [2, 6 * D], F32, kind=sk).ap()
    modc = nc.dram_tensor("modc", [1, 2 * D], F32, kind=sk).ap()
    uT = nc.dram_tensor("uT", [CONV_CH, S], F32, kind=sk).ap()
    qT = nc.dram_tensor("qT", [NH, QK, S], BF16, kind=sk).ap()
    kT = nc.dram_tensor("kT", [NH, QK, NK], BF16, kind=sk).ap()
    vE = nc.dram_tensor("vE", [128, NK // 128, NH, VD + 1], BF16, kind=sk).ap()
    attT = nc.dram_tensor("attT", [NH * VD, S], BF16, kind=sk).ap()
    xa = nc.dram_tensor("xa", [S, D], F32, kind=sk).ap()
    xb = nc.dram_tensor("xb", [S, D], F32, kind=sk).ap()
    K = Kern(nc)
    with K.es:
        if "mod" in phases:
            phase_mod(K, I, modv, modc)
        if "a" in phases:
            phase_l0a(K, I, S, modv[0, :], modc[0, :], uT, qT, kT, vE)
        if "b" in phases:
            phase_attn(K, I, S, qT, kT, vE, attT)
        if "c" in phases:
            phase_l0c(K, I, S, modv[0, :], uT, attT, I["x"], xa)
        if "f0" in phases:
            phase_ffn(K, I, 0, S, xa, xb, modv[0, :])
        if "m1" in phases:
            phase_sconv(K, I, S, xb, xa, modv[1, :])
        if "f1" in phases:
            phase_ffn(K, I, 1, S, xa, y, modv[1, :])
        K.barrier()
    return nc


def rope_table(S):
    t = np.arange(S)
    row = (t // 64).astype(np.float32)
    col = (t % 64).astype(np.float32)
    half = 16
    inv = (10000.0 ** (-np.arange(0, half, 2, dtype=np.float32) / half)).astype(np.float32)
    ar = row[:, None] * inv[None, :]
    ac = col[:, None] * inv[None, :]
    cr, sr, cc, sc = np.cos(ar), np.sin(ar), np.cos(ac), np.sin(ac)
    C = np.concatenate([cr, cr, cc, cc], axis=1)
    Sg = np.concatenate([-sr, sr, -sc, sc], axis=1)
    return np.ascontiguousarray(np.concatenate([C, Sg], axis=1).astype(np.float32))


def kernel(debug=False, phases=("mod", "a", "b", "c", "f0", "m1", "f1"), **inputs):
    x = np.asarray(inputs["x"], dtype=np.float32)
    B, S, _ = x.shape
    assert B == 8
    shapes = {n: np.asarray(inputs[n]).shape for n in WEIGHT_NAMES}
    nc = build(S, shapes, debug=debug, phases=phases)
    ident = np.eye(128, dtype=np.float32).astype(ml_dtypes.bfloat16)
    rope = rope_table(S)
    shared = {n: np.ascontiguousarray(np.asarray(inputs[n], dtype=np.float32)) for n in WEIGHT_NAMES}
    shared["c_ctx"] = np.ascontiguousarray(np.asarray(inputs["c_ctx"], dtype=np.float32))
    shared["ident"] = ident
    shared["rope"] = rope
    in_maps = []
    for b in range(B):
        m = dict(shared)
        m["x"] = np.ascontiguousarray(x[b])
        m["c"] = np.ascontiguousarray(np.asarray(inputs["c"], dtype=np.float32)[b])
        m["ctx"] = np.ascontiguousarray(np.asarray(inputs["ctx"], dtype=np.float32)[b])
        in_maps.append(m)
    res = run_bass_kernel_spmd(nc, in_maps, core_ids=list(range(B)))
    if debug:
        return res
    return np.stack([np.asarray(r["y"], dtype=np.float32) for r in res.results], axis=0)
```

```python
import contextlib
import math
import numpy as np
import ml_dtypes
import concourse.bass as bass
import concourse.mybir as mybir
from concourse.bass_utils import run_bass_kernel_spmd

F32 = mybir.dt.float32
BF16 = mybir.dt.bfloat16
AF = mybir.ActivationFunctionType
ALU = mybir.AluOpType
AX = mybir.AxisListType

D = 1024
CTX = 256
CONV_CH = 512
CONV_K = 31
NH = 8
QK = 96
VD = 64
QL = 384
KVL = 256
EVEN_IN = 1696
FFN = 2816
EPS = 1e-6
SM_SCALE = QK ** -0.5
WSTR = 510


class Eng:
    def __init__(self, name, h, sem):
        self.name = name
        self.h = h
        self.sem = sem
        self.n = 0
        self.seen = {}


class SemC:
    def __init__(self, sem):
        self.sem = sem
        self.n = 0


class View:
    __slots__ = ("buf", "ap")

    def __init__(self, buf, ap):
        self.buf = buf
        self.ap = ap


class Buf:
    def __init__(self, K, t):
        self.K = K
        self.t = t
        self.cw = {}
        self.cr = {}
        self.pw = {}
        self.pr = {}
        self.has_read = False
        self.ld = None
        self.st = None

    def __getitem__(self, key):
        return View(self, self.t[key])

    def v(self, ap):
        return View(self, ap)


class Kern:
    def __init__(self, nc):
        self.nc = nc
        self.es = contextlib.ExitStack()
        mk = lambda n, h: Eng(n, h, self.es.enter_context(nc.semaphore("sem_" + n)))
        self.pe = mk("pe", nc.tensor)
        self.act = mk("act", nc.scalar)
        self.dve = mk("dve", nc.vector)
        self.pool = mk("pool", nc.gpsimd)
        self.sp = mk("sp", nc.sync)
        self.engs = [self.pe, self.act, self.dve, self.pool, self.sp]
        self.free_sems = [SemC(self.es.enter_context(nc.semaphore("dsem%d" % i))) for i in range(88)]
        self.live = []
        self.setup_sem = self.free_sems.pop()
        self.uid = 0

    def sb(self, st, shape, dt, name=None):
        self.uid += 1
        t = st.enter_context(self.nc.sbuf_tensor("%s_%d" % (name or "sb", self.uid), list(shape), dt))
        return Buf(self, t)

    def ps(self, st, shape, dt, name=None):
        self.uid += 1
        t = st.enter_context(self.nc.psum_tensor("%s_%d" % (name or "ps", self.uid), list(shape), dt))
        return Buf(self, t)

    def _need(self, e, sem, val):
        key = id(sem)
        if e.seen.get(key, 0) >= val:
            return
        e.h.wait_ge(sem, val)
        e.seen[key] = val

    def _dep(self, e, b, tag, idx, raw):
        if tag == "LD":
            self._need(e, b.ld.sem, idx)
        elif tag == "ST":
            self._need(e, b.st.sem, idx)
        else:
            if tag is e and not raw:
                return
            self._need(e, tag.sem, idx + 1)

    def _rdeps(self, e, b):
        for tag, idx in b.cw.items():
            self._dep(e, b, tag, idx, True)

    def _wdeps(self, e, b, also_cur=False):
        if b.has_read:
            b.pr, b.pw, b.cr, b.cw, b.has_read = b.cr, b.cw, {}, {}, False
        for tag, idx in b.pr.items():
            self._dep(e, b, tag, idx, False)
        for tag, idx in b.pw.items():
            self._dep(e, b, tag, idx, False)
        if also_cur:
            for tag, idx in b.cw.items():
                self._dep(e, b, tag, idx, False)

    def issue(self, e, reads, writes, fn, sig=True, wait_cur=False):
        rb = []
        for v in reads:
            if isinstance(v, View) and v.buf not in rb:
                rb.append(v.buf)
        wb = []
        for v in writes:
            if v.buf not in wb:
                wb.append(v.buf)
        for b in rb:
            self._rdeps(e, b)
        for b in wb:
            self._wdeps(e, b, also_cur=wait_cur)
        ins = fn()
        idx = e.n
        if sig:
            ins.then_inc(e.sem, 1)
            e.n += 1
        for b in rb:
            if b not in wb:
                b.cr[e] = idx
                b.has_read = True
        for b in wb:
            b.cw[e] = idx
        return ins

    def _getsem(self, b, which):
        s = getattr(b, which)
        if s is None:
            s = self.free_sems.pop()
            setattr(b, which, s)
            if b not in self.live:
                self.live.append(b)
        return s

    def dma(self, q, out, in_, **kw):
        nc = self.nc
        if isinstance(out, View):
            b = out.buf
            self._wdeps(q, b, also_cur=True)
            s = self._getsem(b, "ld")
            q.h.dma_start(out=out.ap, in_=in_, **kw).then_inc(s.sem, 16)
            s.n += 16
            b.cw["LD"] = s.n
        else:
            b = in_.buf
            self._rdeps(q, b)
            s = self._getsem(b, "st")
            q.h.dma_start(out=out, in_=in_.ap, **kw).then_inc(s.sem, 16)
            s.n += 16
            b.cr["ST"] = s.n
            b.has_read = True

    def cload(self, out, in_, q=None, **kw):
        q = q or self.sp
        kw.setdefault("allow_slow_non_contiguous", True)
        s = self.setup_sem
        q.h.dma_start(out=out.ap, in_=in_, **kw).then_inc(s.sem, 16)
        s.n += 16

    def setup_done(self):
        for e in self.engs:
            self._need(e, self.setup_sem.sem, self.setup_sem.n)

    def barrier(self):
        for e in self.engs:
            for f in self.engs:
                if f is not e and f.n > 0:
                    self._need(e, f.sem, f.n)
            for b in self.live:
                for s in (b.ld, b.st):
                    if s is not None and s.n > 0:
                        self._need(e, s.sem, s.n)
            self._need(e, self.setup_sem.sem, self.setup_sem.n)
        for b in self.live:
            for w in ("ld", "st"):
                s = getattr(b, w)
                if s is not None:
                    self.free_sems.append(s)
                    setattr(b, w, None)
            b.cw, b.cr, b.pw, b.pr, b.has_read = {}, {}, {}, {}, False
        self.live = []

    @staticmethod
    def _a(v):
        return v.ap if isinstance(v, View) else v

    def mm(self, out, lhsT, rhs, start, stop, sig=None):
        if sig is None:
            sig = stop
        return self.issue(self.pe, [lhsT, rhs], [out],
                          lambda: self.nc.tensor.matmul(out.ap, lhsT.ap, rhs.ap, start=start, stop=stop), sig)

    def tr(self, out, in_, ident, sig=True):
        return self.issue(self.pe, [in_, ident], [out],
                          lambda: self.nc.tensor.transpose(out.ap, in_.ap, ident.ap), sig)

    def actf(self, out, in_, func, scale=None, bias=None, accum=None):
        kw = {}
        rd = [in_]
        wr = [out]
        if scale is not None:
            kw["scale"] = self._a(scale)
            rd.append(scale)
        if bias is not None:
            kw["bias"] = self._a(bias)
            rd.append(bias)
        if accum is not None:
            kw["accum_out"] = accum.ap
            wr.append(accum)
        return self.issue(self.act, rd, wr, lambda: self.nc.scalar.activation(out.ap, in_.ap, func, **kw))

    def _ve(self, e):
        return self.nc.vector if e is self.dve else self.nc.gpsimd

    def tt(self, e, out, in0, in1, op):
        return self.issue(e, [in0, in1], [out], lambda: self._ve(e).tensor_tensor(out.ap, in0.ap, in1.ap, op))

    def ts(self, e, out, in0, s1, s2, op0, op1=None):
        rd = [in0, s1, s2]
        if op1 is None:
            return self.issue(e, rd, [out], lambda: self._ve(e).tensor_scalar(out.ap, in0.ap, self._a(s1), None, op0))
        return self.issue(e, rd, [out],
                          lambda: self._ve(e).tensor_scalar(out.ap, in0.ap, self._a(s1), self._a(s2), op0, op1))

    def stt(self, out, in0, s, in1, op0, op1):
        return self.issue(self.dve, [in0, s, in1], [out],
                          lambda: self.nc.vector.scalar_tensor_tensor(out.ap, in0.ap, self._a(s), in1.ap, op0, op1))

    def cp(self, e, out, in_):
        if e is self.act:
            return self.issue(e, [in_], [out], lambda: self.nc.scalar.copy(out.ap, in_.ap))
        return self.issue(e, [in_], [out], lambda: self._ve(e).tensor_copy(out.ap, in_.ap))

    def recip(self, out, in_):
        return self.issue(self.dve, [in_], [out], lambda: self.nc.vector.reciprocal(out.ap, in_.ap))

    def red(self, out, in_, op=None):
        return self.issue(self.dve, [in_], [out],
                          lambda: self.nc.vector.tensor_reduce(out.ap, in_.ap, AX.X, op or ALU.add))

    def memset(self, e, out, val, wait_cur=False):
        return self.issue(e, [], [out], lambda: self._ve(e).memset(out.ap, val), wait_cur=wait_cur)

    def rsqrt(self, out, in_, mul, eps):
        self.ts(self.dve, out, in_, mul, eps, ALU.mult, ALU.add)
        self.actf(out, out, AF.Sqrt)
        self.recip(out, out)


def bc(ap, shape):
    return ap.broadcast_to(list(shape))


def load_cast(K, st, dst, src, nk, F, colscale=None, rowscale=None, stg=None):
    CH = 1024
    i = 0
    for k in range(nk):
        for c0 in range(0, F, CH):
            w = min(CH, F - c0)
            s = stg[i % len(stg)]
            K.dma(K.sp, s[:, 0:w], src[k * 128:(k + 1) * 128, c0:c0 + w])
            e = K.dve if i % 2 == 0 else K.pool
            d = dst[:, k, c0:c0 + w]
            if colscale is not None:
                K.tt(e, d, s[:, 0:w], colscale[:, c0:c0 + w], ALU.mult)
            elif rowscale is not None:
                K.ts(e, d, s[:, 0:w], rowscale[:, k:k + 1], None, ALU.mult)
            else:
                K.cp(e, d, s[:, 0:w])
            i += 1


def colvec(ap1d, nk):
    return ap1d.rearrange("(k p o) -> p k o", p=128, o=1)


def norm_block(K, A, x_rows, tiles, gs, sh, ident, hT, hb, xt, ss, rstd, pT, do_norm_now=True):
    nt = len(tiles)
    for (i, p_lo, p_hi, tok_lo, w) in tiles:
        x = xt[i % len(xt)]
        if p_lo > 0 or p_hi < 128:
            K.memset(K.dve, x[:, :], 0.0)
        K.dma(K.sp, x[p_lo:p_hi, :], x_rows[tok_lo:tok_lo + (p_hi - p_lo), :])
        K.actf(A["junk"][:, :], x[:, :], AF.Square, accum=ss[:, i:i + 1])
    K.rsqrt(rstd[:, 0:nt], ss[:, 0:nt], 1.0 / D, EPS)
    for (i, p_lo, p_hi, tok_lo, w) in tiles:
        x = xt[i % len(xt)]
        K.ts(K.dve, hb[i][:, :], x[:, :], rstd[:, i:i + 1], None, ALU.mult)


def transposes_block(K, tiles, gs, sh, ident, hT, hb, pT, zero_cols):
    for (i, p_lo, p_hi, tok_lo, w) in tiles:
        p = pT[i % len(pT)]
        for k in range(8):
            K.tr(p[:, k, :], hb[i][:, k * 128:(k + 1) * 128], ident[:, :], sig=(k == 7))
        for k in range(8):
            K.actf(hT[:, k, i * 128:i * 128 + w], p[:, k, 0:w], AF.Identity,
                   scale=gs[:, k:k + 1], bias=sh[:, k:k + 1])
    for c in zero_cols:
        K.memset(K.dve, hT[:, :, c:c + 1], 0.0, wait_cur=True)


def win_blocks(S):
    blocks = []
    j = 0
    while WSTR * j < S:
        nvalid = min(WSTR, S - WSTR * j)
        t0 = WSTR * j - 1
        Wd = nvalid + 2
        tiles = []
        for i in range((Wd + 127) // 128):
            a = t0 + 128 * i
            tok_lo = max(a, 0)
            tok_hi = min(a + 128, S, t0 + Wd)
            w = min(128, Wd - 128 * i)
            tiles.append((i, tok_lo - a, tok_hi - a, tok_lo, w))
        zero_cols = []
        if t0 < 0:
            zero_cols.append(0)
        if t0 + Wd - 1 >= S:
            zero_cols.append(Wd - 1)
        blocks.append(dict(j=j, t0=t0, Wd=Wd, nvalid=nvalid, tiles=tiles, zero_cols=zero_cols))
        j += 1
    return blocks


def load_mod_cols(K, st, modrow, sec_shift, sec_scale, normg, name):
    gs = K.sb(st, [128, 8], F32, name + "gs")
    sh = K.sb(st, [128, 8], F32, name + "sh")
    ng = K.sb(st, [128, 8], F32, name + "ng")
    K.cload(gs.v(gs.t[:, :].rearrange("p (k o) -> p k o", o=1)), colvec(modrow[sec_scale * D:(sec_scale + 1) * D], 8))
    K.cload(sh.v(sh.t[:, :].rearrange("p (k o) -> p k o", o=1)), colvec(modrow[sec_shift * D:(sec_shift + 1) * D], 8))
    K.cload(ng.v(ng.t[:, :].rearrange("p (k o) -> p k o", o=1)), colvec(normg, 8))
    K.setup_done()
    K.stt(gs[:, :], gs[:, :], 1.0, ng[:, :], ALU.add, ALU.mult)
    return gs, sh


def phase_mod(K, I, modv, modc):
    nc = K.nc
    with contextlib.ExitStack() as st:
        cT = K.sb(st, [128, 8], F32, "cT")
        ccT = K.sb(st, [128, 8], F32, "ccT")
        ab = K.sb(st, [1, 2 * 6 * D], F32, "ab")
        stg = [K.sb(st, [128, 8, 512], F32, "adastg") for _ in range(2)]
        row = [K.sb(st, [1, 512], F32, "modrow") for _ in range(2)]
        pm = [K.ps(st, [128, 512], F32, "pm") for _ in range(2)]
        K.cload(cT.v(cT.t[:, :].rearrange("p (k o) -> p k o", o=1)), colvec(I["c"], 8))
        K.cload(ccT.v(ccT.t[:, :].rearrange("p (k o) -> p k o", o=1)), colvec(I["c_ctx"], 8))
        K.cload(ab[:, :], I["ada_b"].rearrange("(o l) f -> o (l f)", o=1))
        K.setup_done()
        K.actf(cT[:, :], cT[:, :], AF.Silu)
        K.actf(ccT[:, :], ccT[:, :], AF.Silu)
        it = 0
        for layer in range(2):
            for g in range(12):
                s = stg[it % 2]
                K.dma(K.sp, s[:, :, :], I["ada_w"][layer, :, g * 512:(g + 1) * 512].rearrange("(k p) f -> p k f", p=128))
                srcs = [(cT, modv[layer:layer + 1, g * 512:(g + 1) * 512])]
                if layer == 0 and g < 4:
                    srcs.append((ccT, modc[0:1, g * 512:(g + 1) * 512]))
                for (vec, dst) in srcs:
                    p = pm[it % 2]
                    r = row[it % 2]
                    for k in range(8):
                        K.mm(p[0:1, :], vec[:, k:k + 1], s[:, k, :], start=(k == 0), stop=(k == 7))
                    K.tt(K.dve, r[:, :], p[0:1, :], ab[:, layer * 6 * D + g * 512: layer * 6 * D + (g + 1) * 512], ALU.add)
                    K.dma(K.pool, dst, r[:, :])
                    it += 1
        K.barrier()


def phase_ffn(K, I, layer, S, x_in, x_out, modrow):
    nc = K.nc
    NM = FFN // 128
    with contextlib.ExitStack() as st:
        wup = K.sb(st, [128, 8, 2 * FFN], BF16, "wup")
        wdn = K.sb(st, [128, NM, D], BF16, "wdn")
        ident = K.sb(st, [128, 128], BF16, "ident")
        cw = K.sb(st, [128, NM, 3], F32, "cw")
        cb = K.sb(st, [128, NM], F32, "cb")
        K.cload(ident[:, :], I["ident"])
        for kk in range(3):
            K.cload(cw.v(cw.t[:, :, kk:kk + 1]), colvec(I["ffn_conv_w"][layer, kk, :], NM), allow_slow_non_contiguous=True)
        K.cload(cb.v(cb.t[:, :].rearrange("p (k o) -> p k o", o=1)), colvec(I["ffn_conv_b"][layer, :], NM))
        gs, sh = load_mod_cols(K, st, modrow, 3, 4, I["norm_ffn_g"][layer, :], "f")
        with contextlib.ExitStack() as st2:
            G = K.sb(st2, [128, D], F32, "G")
            stg = [K.sb(st2, [128, 1024], F32, "wstg") for _ in range(3)]
            K.cload(G[:, :], modrow[5 * D:6 * D].partition_broadcast(128))
            K.setup_done()
            load_cast(K, st2, wup, I["ffn_w_up"][layer], 8, 2 * FFN, stg=stg)
            load_cast(K, st2, wdn, I["ffn_w_down"][layer], NM, D, colscale=G, stg=stg)
            K.barrier()
        hid = K.sb(st, [128, NM, 512], BF16, "hid")
        hT = K.sb(st, [128, 8, 512], BF16, "hT")
        hb = [K.sb(st, [128, D], BF16, "hb") for _ in range(4)]
        xt = [K.sb(st, [128, D], F32, "xt") for _ in range(4)]
        xr = [K.sb(st, [128, D], F32, "xr") for _ in range(2)]
        acc = [K.sb(st, [128, 512], F32, "acc") for _ in range(2)]
        sl = [K.sb(st, [128, 512], BF16, "sl") for _ in range(2)]
        A = {"junk": K.sb(st, [128, D], BF16, "junk")}
        ss = K.sb(st, [128, 4], F32, "ss")
        rstd = K.sb(st, [128, 4], F32, "rstd")
        pT = [K.ps(st, [128, 8, 128], BF16, "pT") for _ in range(2)]
        pg = [K.ps(st, [128, 512], F32, "pg") for _ in range(2)]
        pv = [K.ps(st, [128, 512], F32, "pv") for _ in range(2)]
        po = [K.ps(st, [128, 512], F32, "po") for _ in range(2)]
        blocks = win_blocks(S)

        def pre_norm(b):
            norm_block(K, A, x_in, b["tiles"], gs, sh, ident, hT, hb, xt, ss, rstd, pT)

        def pre_tr(b):
            transposes_block(K, b["tiles"], gs, sh, ident, hT, hb, pT, b["zero_cols"])

        def up(b):
            Wd = b["Wd"]
            n = Wd - 2
            for m in range(NM):
                g_, v_ = pg[m % 2], pv[m % 2]
                for k in range(8):
                    K.mm(g_[:, 0:Wd], wup[:, k, m * 128:(m + 1) * 128], hT[:, k, 0:Wd], start=(k == 0), stop=(k == 7))
                for k in range(8):
                    K.mm(v_[:, 0:Wd], wup[:, k, FFN + m * 128:FFN + (m + 1) * 128], hT[:, k, 0:Wd], start=(k == 0), stop=(k == 7))
                a = acc[m % 2]
                K.ts(K.dve, a[:, 0:n], g_[:, 0:n], cw[:, m, 0:1], None, ALU.mult)
                K.stt(a[:, 0:n], g_[:, 1:n + 1], cw[:, m, 1:2], a[:, 0:n], ALU.mult, ALU.add)
                K.stt(a[:, 0:n], g_[:, 2:n + 2], cw[:, m, 2:3], a[:, 0:n], ALU.mult, ALU.add)
                s_ = sl[m % 2]
                K.actf(s_[:, 0:n], a[:, 0:n], AF.Silu, bias=cb[:, m:m + 1])
                K.tt(K.dve, hid[:, m, 0:n], s_[:, 0:n], v_[:, 1:n + 1], ALU.mult)

        def down(b, cnt):
            nv = b["nvalid"]
            tok0 = WSTR * b["j"]
            for i2 in range((nv + 127) // 128):
                r = min(128, nv - 128 * i2)
                x = xr[cnt[0] % 2]
                cnt[0] += 1
                K.dma(K.sp, x[0:r, :], x_in[tok0 + 128 * i2: tok0 + 128 * i2 + r, :])
                for n_ in range(2):
                    for k in range(NM):
                        K.mm(po[n_][0:r, :], hid[:, k, 128 * i2:128 * i2 + r], wdn[:, k, n_ * 512:(n_ + 1) * 512],
                             start=(k == 0), stop=(k == NM - 1))
                for n_ in range(2):
                    K.tt(K.dve, x[0:r, n_ * 512:(n_ + 1) * 512], x[0:r, n_ * 512:(n_ + 1) * 512], po[n_][0:r, :], ALU.add)
                K.dma(K.pool, x_out[tok0 + 128 * i2: tok0 + 128 * i2 + r, :], x[0:r, :])

        cnt = [0]
        pre_norm(blocks[0])
        pre_tr(blocks[0])
        for bi, b in enumerate(blocks):
            nxt = blocks[bi + 1] if bi + 1 < len(blocks) else None
            if nxt:
                pre_norm(nxt)
            up(b)
            if nxt:
                pre_tr(nxt)
            down(b, cnt)
        K.barrier()


def phase_sconv(K, I, S, x_in, x_out, modrow):
    nc = K.nc
    with contextlib.ExitStack() as st:
        win = K.sb(st, [128, 8, 3 * D], BF16, "odwin")
        wout = K.sb(st, [128, 8, D], BF16, "odwout")
        ident = K.sb(st, [128, 128], BF16, "ident")
        cw = K.sb(st, [128, 8, 3], F32, "cw")
        cb = K.sb(st, [128, 8], F32, "cb")
        K.cload(ident[:, :], I["ident"])
        for kk in range(3):
            K.cload(cw.v(cw.t[:, :, kk:kk + 1]), colvec(I["od_conv_w"][0, kk, :], 8), allow_slow_non_contiguous=True)
        K.cload(cb.v(cb.t[:, :].rearrange("p (k o) -> p k o", o=1)), colvec(I["od_conv_b"][0, :], 8))
        gs, sh = load_mod_cols(K, st, modrow, 0, 1, I["norm_mix_g"][1, :], "m1")
        with contextlib.ExitStack() as st2:
            G = K.sb(st2, [128, D], F32, "G")
            stg = [K.sb(st2, [128, 1024], F32, "wstg") for _ in range(3)]
            K.cload(G[:, :], modrow[2 * D:3 * D].partition_broadcast(128))
            K.setup_done()
            load_cast(K, st2, win, I["od_w_in"][0], 8, 3 * D, stg=stg)
            load_cast(K, st2, wout, I["od_w_out"][0], 8, D, colscale=G, stg=stg)
            K.barrier()
        yT = K.sb(st, [128, 8, 512], BF16, "yT")
        hT = K.sb(st, [128, 8, 512], BF16, "hT")
        hb = [K.sb(st, [128, D], BF16, "hb") for _ in range(4)]
        xt = [K.sb(st, [128, D], F32, "xt") for _ in range(4)]
        xr = [K.sb(st, [128, D], F32, "xr") for _ in range(2)]
        acc = [K.sb(st, [128, 512], F32, "acc") for _ in range(2)]
        us = [K.sb(st, [128, 512], F32, "us") for _ in range(2)]
        A = {"junk": K.sb(st, [128, D], BF16, "junk")}
        ss = K.sb(st, [128, 4], F32, "ss")
        rstd = K.sb(st, [128, 4], F32, "rstd")
        pT = [K.ps(st, [128, 8, 128], BF16, "pT") for _ in range(1)]
        pb = [K.ps(st, [128, 512], F32, "pb") for _ in range(2)]
        pc = [K.ps(st, [128, 512], F32, "pc") for _ in range(2)]
        pu = [K.ps(st, [128, 512], F32, "pu") for _ in range(1)]
        po = [K.ps(st, [128, 512], F32, "po") for _ in range(2)]
        blocks = win_blocks(S)

        def mix(b):
            Wd = b["Wd"]
            n = Wd - 2
            for m in range(8):
                b_, c_, u_ = pb[m % 2], pc[m % 2], pu[0]
                for (dst, off) in ((u_, 2 * D), (c_, D), (b_, 0)):
                    for k in range(8):
                        K.mm(dst[:, 0:Wd], win[:, k, off + m * 128: off + (m + 1) * 128], hT[:, k, 0:Wd],
                             start=(k == 0), stop=(k == 7))
                u = us[m % 2]
                K.cp(K.act, u[:, 0:Wd], u_[:, 0:Wd])
                K.tt(K.dve, u[:, 0:Wd], u[:, 0:Wd], c_[:, 0:Wd], ALU.mult)
                a = acc[m % 2]
                K.ts(K.dve, a[:, 0:n], u[:, 0:n], cw[:, m, 0:1], cb[:, m:m + 1], ALU.mult, ALU.add)
                K.stt(a[:, 0:n], u[:, 1:n + 1], cw[:, m, 1:2], a[:, 0:n], ALU.mult, ALU.add)
                K.stt(a[:, 0:n], u[:, 2:n + 2], cw[:, m, 2:3], a[:, 0:n], ALU.mult, ALU.add)
                K.tt(K.dve, yT[:, m, 0:n], a[:, 0:n], b_[:, 1:n + 1], ALU.mult)

        def outp(b, cnt):
            nv = b["nvalid"]
            tok0 = WSTR * b["j"]
            for i2 in range((nv + 127) // 128):
                r = min(128, nv - 128 * i2)
                x = xr[cnt[0] % 2]
                cnt[0] += 1
                K.dma(K.sp, x[0:r, :], x_in[tok0 + 128 * i2: tok0 + 128 * i2 + r, :])
                for n_ in range(2):
                    for k in range(8):
                        K.mm(po[n_][0:r, :], yT[:, k, 128 * i2:128 * i2 + r], wout[:, k, n_ * 512:(n_ + 1) * 512],
                             start=(k == 0), stop=(k == 7))
                for n_ in range(2):
                    K.tt(K.dve, x[0:r, n_ * 512:(n_ + 1) * 512], x[0:r, n_ * 512:(n_ + 1) * 512], po[n_][0:r, :], ALU.add)
                K.dma(K.pool, x_out[tok0 + 128 * i2: tok0 + 128 * i2 + r, :], x[0:r, :])

        cnt = [0]
        norm_block(K, A, x_in, blocks[0]["tiles"], gs, sh, ident, hT, hb, xt, ss, rstd, pT)
        transposes_block(K, blocks[0]["tiles"], gs, sh, ident, hT, hb, pT, blocks[0]["zero_cols"])
        for bi, b in enumerate(blocks):
            nxt = blocks[bi + 1] if bi + 1 < len(blocks) else None
            if nxt:
                norm_block(K, A, x_in, nxt["tiles"], gs, sh, ident, hT, hb, xt, ss, rstd, pT)
            mix(b)
            if nxt:
                transposes_block(K, nxt["tiles"], gs, sh, ident, hT, hb, pT, nxt["zero_cols"])
            outp(b, cnt)
        K.barrier()


def phase_l0a(K, I, S, modrow, modc, uT, qT, kT, vE):
    nc = K.nc
    NK = S + CTX
    with contextlib.ExitStack() as st:
        win = K.sb(st, [128, 8, EVEN_IN], BF16, "evwin")
        wuq = K.sb(st, [128, 3, NH * QK], BF16, "wuq")
        wukv = K.sb(st, [128, 2, NH * 128], BF16, "wukv")
        ident = K.sb(st, [128, 128], BF16, "ident")
        QG = K.sb(st, [128, QK], F32, "QG")
        KG = K.sb(st, [128, QK], F32, "KG")
        rope = K.sb(st, [128, S // 128, 64], F32, "rope")
        qag = K.sb(st, [128, 3], F32, "qag")
        kvag = K.sb(st, [128, 2], F32, "kvag")
        K.cload(ident[:, :], I["ident"])
        K.cload(QG[:, :], I["ev_q_norm_g"][0, :].partition_broadcast(128))
        K.cload(KG[:, :], I["ev_k_norm_g"][0, :].partition_broadcast(128))
        K.cload(rope[:, :, :], I["rope"].rearrange("(i p) c -> p i c", p=128))
        K.cload(qag.v(qag.t[:, :].rearrange("p (k o) -> p k o", o=1)), colvec(I["ev_qa_norm_g"][0, :], 3))
        K.cload(kvag.v(kvag.t[:, :].rearrange("p (k o) -> p k o", o=1)), colvec(I["ev_kva_norm_g"][0, :], 2))
        gs, sh = load_mod_cols(K, st, modrow, 0, 1, I["norm_mix_g"][0, :], "m0")
        gsc, shc = load_mod_cols(K, st, modc, 0, 1, I["norm_mix_g"][0, :], "m0c")
        with contextlib.ExitStack() as st2:
            stg = [K.sb(st2, [128, 1024], F32, "wstg") for _ in range(3)]
            load_cast(K, st2, win, I["ev_w_in"][0], 8, EVEN_IN, stg=stg)
            load_cast(K, st2, wuq, I["ev_w_uq"][0], 3, NH * QK, rowscale=qag, stg=stg)
            load_cast(K, st2, wukv, I["ev_w_ukv"][0], 2, NH * 128, rowscale=kvag, stg=stg)
            K.barrier()
        hT = K.sb(st, [128, 8, 512], BF16, "hT")
        hb = [K.sb(st, [128, D], BF16, "hb") for _ in range(4)]
        xt = [K.sb(st, [128, D], F32, "xt") for _ in range(4)]
        A = {"junk": K.sb(st, [128, D], BF16, "junk")}
        ss = K.sb(st, [128, 4], F32, "ss")
        rstd = K.sb(st, [128, 4], F32, "rstd")
        th = [K.sb(st, [128, 512], F32, "th") for _ in range(2)]
        ust = [K.sb(st, [128, 512], F32, "ust") for _ in range(2)]
        ccT = K.sb(st, [128, 5, 512], BF16, "ccT")
        krs = K.sb(st, [128, 32], F32, "krs")
        st2t = K.sb(st, [128, 4], F32, "st2")
        r2 = K.sb(st, [128, 2], F32, "r2")
        qf = K.sb(st, [128, NH, QK], F32, "qf")
        kvf = K.sb(st, [128, NH, 128], F32, "kvf")
        sq = K.sb(st, [128, NH, QK], F32, "sq")
        ssh = K.sb(st, [128, 2 * NH], F32, "ssh")
        rh = K.sb(st, [128, 2 * NH], F32, "rh")
        R = K.sb(st, [128, NH, 32], F32, "R")
        T1 = K.sb(st, [128, NH, 32], F32, "T1")
        U = K.sb(st, [128, NH, 32], F32, "U")
        qb = [K.sb(st, [128, NH, QK], BF16, "qb") for _ in range(2)]
        kb = [K.sb(st, [128, NH, QK], BF16, "kb") for _ in range(2)]
        vb = [K.sb(st, [128, NH, VD + 1], BF16, "vb") for _ in range(2)]
        qTs = [K.sb(st, [128, NH, 512], BF16, "qTs") for _ in range(2)]
        kTs = [K.sb(st, [128, NH, 512], BF16, "kTs") for _ in range(2)]
        for v_ in vb:
            K.memset(K.dve, v_[:, :, VD:VD + 1], 1.0)
        pT = [K.ps(st, [128, 8, 128], BF16, "pT")]
        pA = [K.ps(st, [128, 512], F32, "pA") for _ in range(2)]
        pQ = [K.ps(st, [128, 512], F32, "pQ") for _ in range(2)]
        pTq = K.ps(st, [128, 8, 128], BF16, "pTq")
        pTk = K.ps(st, [128, 8, 128], BF16, "pTk")
        blks = [dict(ctx=True, x=I["ctx"], t0=0, nt=CTX // 128, key0=0)]
        for j in range(S // 512):
            blks.append(dict(ctx=False, x=I["x"], t0=512 * j, nt=4, key0=CTX + 512 * j))
        tcount = 0
        for bi, b in enumerate(blks):
            nt = b["nt"]
            Wd = nt * 128
            isctx = b["ctx"]
            g_, s_ = (gsc, shc) if isctx else (gs, sh)
            tiles = [(i, 0, 128, b["t0"] + 128 * i, 128) for i in range(nt)]
            norm_block(K, A, b["x"], tiles, g_, s_, ident, hT, hb, xt, ss, rstd, pT)
            transposes_block(K, tiles, g_, s_, ident, hT, hb, pT, [])
            pi = 0
            if not isctx:
                for m in range(4):
                    pv_, pg_ = pA[0], pA[1]
                    for k in range(8):
                        K.mm(pv_[:, 0:Wd], win[:, k, m * 128:(m + 1) * 128], hT[:, k, 0:Wd], start=(k == 0), stop=(k == 7))
                    for k in range(8):
                        K.mm(pg_[:, 0:Wd], win[:, k, 512 + m * 128:512 + (m + 1) * 128], hT[:, k, 0:Wd], start=(k == 0), stop=(k == 7))
                    t_ = th[m % 2]
                    K.actf(t_[:, :], pg_[:, :], AF.Tanh, scale=0.5)
                    u_ = ust[m % 2]
                    K.stt(u_[:, :], t_[:, :], 1.0, pv_[:, :], ALU.add, ALU.mult)
                    K.dma(K.pool, uT[m * 128:(m + 1) * 128, b["t0"]:b["t0"] + 512], u_[:, :])
            for m in range(5):
                if isctx and m < 3:
                    continue
                p_ = pA[m % 2]
                for k in range(8):
                    K.mm(p_[:, 0:Wd], win[:, k, 1024 + m * 128:1024 + (m + 1) * 128], hT[:, k, 0:Wd], start=(k == 0), stop=(k == 7))
                K.cp(K.dve, ccT[:, m, 0:Wd], p_[:, 0:Wd])
            qs, ks = qTs[bi % 2], kTs[bi % 2]
            for i in range(nt):
                cs = slice(i * 128, (i + 1) * 128)
                p1, p2 = pA[0], pA[1]
                for k in range(8):
                    K.mm(p1[:, :], hT[:, k, cs], win[:, k, 1024:1536], start=(k == 0), stop=(k == 7))
                for k in range(8):
                    K.mm(p2[:, 0:160], hT[:, k, cs], win[:, k, 1536:1696], start=(k == 0), stop=(k == 7))
                J = A["junk"]
                K.actf(J[:, 0:384], p1[:, 0:384], AF.Square, accum=st2t[:, 0:1])
                K.actf(J[:, 0:128], p1[:, 384:512], AF.Square, accum=st2t[:, 1:2])
                K.actf(J[:, 0:128], p2[:, 0:128], AF.Square, accum=st2t[:, 2:3])
                K.actf(krs[:, :], p2[:, 128:160], AF.Identity)
                K.actf(J[:, 0:32], p2[:, 128:160], AF.Square, accum=st2t[:, 3:4])
                K.tt(K.dve, st2t[:, 1:2], st2t[:, 1:2], st2t[:, 2:3], ALU.add)
                K.ts(K.dve, r2[:, 0:1], st2t[:, 0:1], 1.0 / QL, EPS, ALU.mult, ALU.add)
                K.ts(K.dve, r2[:, 1:2], st2t[:, 1:2], 1.0 / KVL, EPS, ALU.mult, ALU.add)
                K.actf(r2[:, :], r2[:, :], AF.Sqrt)
                K.recip(r2[:, :], r2[:, :])
                qfl = qf.v(qf.t[:, :, :].rearrange("p h d -> p (h d)"))
                kvfl = kvf.v(kvf.t[:, :, :].rearrange("p h d -> p (h d)"))
                if not isctx:
                    pq0, pq1 = pQ[0], pQ[1]
                    for k in range(3):
                        K.mm(pq0[:, :], ccT[:, k, cs], wuq[:, k, 0:512], start=(k == 0), stop=(k == 2))
                    for k in range(3):
                        K.mm(pq1[:, 0:256], ccT[:, k, cs], wuq[:, k, 512:768], start=(k == 0), stop=(k == 2))
                    K.actf(qf.v(qfl.ap[:, 0:512]), pq0[:, :], AF.Identity, scale=r2[:, 0:1])
                    K.actf(qf.v(qfl.ap[:, 512:768]), pq1[:, 0:256], AF.Identity, scale=r2[:, 0:1])
                    K.tt(K.dve, sq[:, :, :], qf[:, :, :], qf[:, :, :], ALU.mult)
                    K.red(ssh[:, 0:NH], sq[:, :, :])
                pk0, pk1 = pQ[0], pQ[1]
                for n_, pk in enumerate((pk0, pk1)):
                    for k in range(2):
                        K.mm(pk[:, :], ccT[:, 3 + k, cs], wukv[:, k, n_ * 512:(n_ + 1) * 512], start=(k == 0), stop=(k == 1))
                K.actf(kvf.v(kvfl.ap[:, 0:512]), pk0[:, :], AF.Identity, scale=r2[:, 1:2])
                K.actf(kvf.v(kvfl.ap[:, 512:1024]), pk1[:, :], AF.Identity, scale=r2[:, 1:2])
                K.tt(K.dve, sq[:, :, 0:64], kvf[:, :, 0:64], kvf[:, :, 0:64], ALU.mult)
                K.red(ssh[:, NH:2 * NH], sq[:, :, 0:64])
                K.ts(K.dve, ssh[:, NH:2 * NH], ssh[:, NH:2 * NH], st2t[:, 3:4], None, ALU.add)
                lo = NH if isctx else 0
                K.ts(K.dve, rh[:, lo:2 * NH], ssh[:, lo:2 * NH], 1.0 / QK, EPS, ALU.mult, ALU.add)
                K.actf(rh[:, lo:2 * NH], rh[:, lo:2 * NH], AF.Sqrt)
                K.recip(rh[:, lo:2 * NH], rh[:, lo:2 * NH])
                rp = rope[:, (b["t0"] // 128 + i), :] if not isctx else None

                def do_rope(Rv, outv):
                    C = rope.v(bc(rope.t[:, (b["t0"] // 128 + i), 0:32].unsqueeze(1), [128, NH, 32]))
                    K.tt(K.dve, T1[:, :, :], Rv, C, ALU.mult)
                    R5 = Rv.ap.rearrange("p h (a b c) -> p h a b c", a=2, b=2)
                    U5 = U.t[:, :, :].rearrange("p h (a b c) -> p h a b c", a=2, b=2)
                    S5 = rope.t[:, (b["t0"] // 128 + i), 32:64].rearrange("p (a b c) -> p a b c", a=2, b=2)
                    for hb_ in range(2):
                        K.tt(K.dve, U.v(U5[:, :, :, hb_, :]), View(Rv.buf, R5[:, :, :, 1 - hb_, :]),
                             rope.v(bc(S5[:, :, hb_, :].unsqueeze(1), [128, NH, 2, 8])), ALU.mult)
                    K.tt(K.dve, outv, T1[:, :, :], U[:, :, :], ALU.add)

                if not isctx:
                    q_ = qb[tcount % 2]
                    rq = rh.v(bc(rh.t[:, 0:NH].unsqueeze(2), [128, NH, QK]))
                    K.tt(K.dve, qf[:, :, :], qf[:, :, :], rq, ALU.mult)
                    K.tt(K.dve, q_[:, :, 0:64], qf[:, :, 0:64], QG.v(bc(QG.t[:, 0:64].unsqueeze(1), [128, NH, 64])), ALU.mult)
                    K.tt(K.dve, R[:, :, :], qf[:, :, 64:96], QG.v(bc(QG.t[:, 64:96].unsqueeze(1), [128, NH, 32])), ALU.mult)
                    do_rope(R[:, :, :], q_[:, :, 64:96])
                k_ = kb[tcount % 2]
                v_ = vb[tcount % 2]
                K.cp(K.pool, v_[:, :, 0:VD], kvf[:, :, 64:128])
                rk64 = rh.v(bc(rh.t[:, NH:2 * NH].unsqueeze(2), [128, NH, 64]))
                rk32 = rh.v(bc(rh.t[:, NH:2 * NH].unsqueeze(2), [128, NH, 32]))
                K.tt(K.dve, sq[:, :, 0:64], kvf[:, :, 0:64], rk64, ALU.mult)
                K.tt(K.dve, k_[:, :, 0:64], sq[:, :, 0:64], KG.v(bc(KG.t[:, 0:64].unsqueeze(1), [128, NH, 64])), ALU.mult)
                K.tt(K.dve, R[:, :, :], krs.v(bc(krs.t[:, :].unsqueeze(1), [128, NH, 32])), rk32, ALU.mult)
                if isctx:
                    K.tt(K.dve, k_[:, :, 64:96], R[:, :, :], KG.v(bc(KG.t[:, 64:96].unsqueeze(1), [128, NH, 32])), ALU.mult)
                else:
                    K.tt(K.dve, R[:, :, :], R[:, :, :], KG.v(bc(KG.t[:, 64:96].unsqueeze(1), [128, NH, 32])), ALU.mult)
                    do_rope(R[:, :, :], k_[:, :, 64:96])
                if not isctx:
                    for h in range(NH):
                        K.tr(pTq[0:QK, h, :], q_[:, h, :], ident[:, :], sig=(h == NH - 1))
                    K.cp(K.act, qs[0:QK, :, cs], pTq[0:QK, :, :])
                for h in range(NH):
                    K.tr(pTk[0:QK, h, :], k_[:, h, :], ident[:, :], sig=(h == NH - 1))
                K.cp(K.act, ks[0:QK, :, cs], pTk[0:QK, :, :])
                key = b["key0"] + 128 * i
                K.dma(K.sp, vE[:, key // 128, :, :], v_[:, :, :])
                tcount += 1
            if not isctx:
                K.dma(K.sp, qT[:, :, b["t0"]:b["t0"] + 512].rearrange("h d t -> d h t"), qs[0:QK, :, :])
            K.dma(K.sp, kT[:, :, b["key0"]:b["key0"] + Wd].rearrange("h d t -> d h t"), ks[0:QK, :, 0:Wd])
        K.barrier()


def phase_attn(K, I, S, qT, kT, vE, attT):
    nc = K.nc
    NK = S + CTX
    NT = NK // 128
    with contextlib.ExitStack() as st:
        kTh = [K.sb(st, [128, NK], BF16, "kTh") for _ in range(2)]
        vEh = [K.sb(st, [128, NT, VD + 1], BF16, "vEh") for _ in range(2)]
        qTh = [K.sb(st, [128, 512], BF16, "qTh") for _ in range(2)]
        pt = [K.sb(st, [128, 1024], BF16, "pt") for _ in range(3)]
        osb = [K.sb(st, [128, 512], F32, "osb") for _ in range(2)]
        rd = [K.sb(st, [128, 512], F32, "rd") for _ in range(2)]
        atts = [K.sb(st, [128, 512], BF16, "atts") for _ in range(2)]
        ones = K.sb(st, [128, 64], F32, "ones")
        K.memset(K.dve, ones[:, :], 0.0)
        K.memset(K.dve, ones[64:65, :], 1.0)
        for r__ in rd:
            K.memset(K.dve, r__[:, :], 0.0)
        ps = [K.ps(st, [128, 1024], F32, "ps") for _ in range(2)]
        po = [K.ps(st, [128, 512], F32, "po") for _ in range(2)]
        pb = K.ps(st, [128, 512], F32, "pb")
        npair = NT // 2
        units = [(h, qbk) for h in range(NH) for qbk in range(S // 512)]
        jobs = [(u, pr) for u in range(len(units)) for pr in range(npair)]

        def load_head(h):
            K.dma(K.sp, kTh[h % 2][0:QK, :], kT[h, :, :])
            K.dma(K.sp, vEh[h % 2][:, :, :], vE[:, :, h, :])

        def load_q(u):
            h, qbk = units[u]
            K.dma(K.sp, qTh[u % 2][0:QK, :], qT[h, :, qbk * 512:(qbk + 1) * 512])

        def S_(g):
            u, pr = jobs[g]
            h, qbk = units[u]
            if pr == 0 and u + 1 < len(units):
                if units[u + 1][0] != h:
                    load_head(h + 1)
                load_q(u + 1)
            kk, qq, p_ = kTh[h % 2], qTh[u % 2], ps[g % 2]
            for j in range(2):
                kt = 2 * pr + j
                K.mm(p_[:, j * 512:(j + 1) * 512], kk[0:QK, kt * 128:(kt + 1) * 128], qq[0:QK, :],
                     start=True, stop=True, sig=(j == 1))

        def E_(g):
            K.actf(pt[g % 3][:, :], ps[g % 2][:, :], AF.Exp, scale=SM_SCALE)

        def PV_(g):
            u, pr = jobs[g]
            h, qbk = units[u]
            vv, e_, o_ = vEh[h % 2], pt[g % 3], po[u % 2]
            for j in range(2):
                kt = 2 * pr + j
                K.mm(o_[0:VD + 1, :], vv[:, kt, :], e_[:, j * 512:(j + 1) * 512],
                     start=(kt == 0), stop=(kt == NT - 1), sig=(j == 1))

        def norm1(u):
            ob, r_, o_ = osb[u % 2], rd[u % 2], po[u % 2]
            K.cp(K.dve, ob[0:VD + 1, :], o_[0:VD + 1, :])
            K.recip(r_[64:65, :], ob[64:65, :])

        def norm2(u):
            h, qbk = units[u]
            ob, r_, a_ = osb[u % 2], rd[u % 2], atts[u % 2]
            K.mm(pb[0:VD, :], ones[0:VD + 1, 0:VD], r_[0:VD + 1, :], start=True, stop=True)
            K.tt(K.dve, a_[0:VD, :], ob[0:VD, :], pb[0:VD, :], ALU.mult)
            K.dma(K.pool, attT[h * VD:(h + 1) * VD, qbk * 512:(qbk + 1) * 512], a_[0:VD, :])

        load_head(0)
        load_q(0)
        S_(0)
        pending = None
        for g in range(len(jobs)):
            if g + 1 < len(jobs):
                S_(g + 1)
            E_(g)
            PV_(g)
            if pending is not None:
                norm2(pending)
                pending = None
            if jobs[g][1] == npair - 1:
                norm1(jobs[g][0])
                pending = jobs[g][0]
        if pending is not None:
            norm2(pending)
        K.barrier()


def phase_l0c(K, I, S, modrow, uT, attT, x_in, x_out):
    nc = K.nc
    HALO = CONV_K // 2
    with contextlib.ExitStack() as st:
        wout = K.sb(st, [128, 8, D], BF16, "evwout")
        cw = K.sb(st, [128, 4, CONV_K], F32, "cw31")
        cb = K.sb(st, [128, 4], F32, "cb31")
        lg = K.sb(st, [128, 4], F32, "lng")
        lb = K.sb(st, [128, 4], F32, "lnb")
        onesF = K.sb(st, [128, 128], F32, "onesF")
        identb = K.sb(st, [128, 128], BF16, "identb")
        identF = K.sb(st, [128, 128], F32, "identF")
        diag = K.sb(st, [128, 4, CONV_K, 128], BF16, "diag")
        K.cload(identb[:, :], I["ident"])
        for kk in range(CONV_K):
            K.cload(cw.v(cw.t[:, :, kk:kk + 1]), colvec(I["ev_conv_w"][0, kk, :], 4), allow_slow_non_contiguous=True)
        K.cload(cb.v(cb.t[:, :].rearrange("p (k o) -> p k o", o=1)), colvec(I["ev_conv_b"][0, :], 4))
        K.cload(lg.v(lg.t[:, :].rearrange("p (k o) -> p k o", o=1)), colvec(I["ev_ln_g"][0, :], 4))
        K.cload(lb.v(lb.t[:, :].rearrange("p (k o) -> p k o", o=1)), colvec(I["ev_ln_b"][0, :], 4))
        with contextlib.ExitStack() as st2:
            G = K.sb(st2, [128, D], F32, "G")
            stg = [K.sb(st2, [128, 1024], F32, "wstg") for _ in range(3)]
            K.cload(G[:, :], modrow[2 * D:3 * D].partition_broadcast(128))
            K.setup_done()
            K.memset(K.dve, onesF[:, :], 1.0 / CONV_CH)
            K.ts(K.dve, cw[:, :, :], cw[:, :, :], 0.5, None, ALU.mult)
            K.cp(K.dve, identF[:, :], identb[:, :])
            for m in range(4):
                for kk in range(CONV_K):
                    K.ts(K.dve, diag[:, m, kk, :], identF[:, :], cw[:, m, kk:kk + 1], None, ALU.mult)
            load_cast(K, st2, wout, I["ev_w_out"][0], 8, D, colscale=G, stg=stg)
            K.barrier()
        uw = [K.sb(st, [128, 4, 512 + 2 * HALO], F32, "uw") for _ in range(2)]
        uwa = [K.sb(st, [128, 4, 544], BF16, "uwa") for _ in range(2)]
        uwb = [K.sb(st, [128, 4, 544], BF16, "uwb") for _ in range(2)]
        pcv = [K.ps(st, [128, 512], F32, "pcv") for _ in range(2)]
        accs = [K.sb(st, [128, 4, 512], F32, "acc31") for _ in range(2)]
        sqbs = [K.sb(st, [128, 4, 512], F32, "sq31") for _ in range(2)]
        mean = K.sb(st, [128, 512], F32, "mean")
        rs = K.sb(st, [128, 512], F32, "rs")
        aTs = [K.sb(st, [128, 4, 512], BF16, "aT") for _ in range(2)]
        at = [K.sb(st, [128, 4, 512], BF16, "attblk") for _ in range(2)]
        xr = [K.sb(st, [128, D], F32, "xr") for _ in range(2)]
        pm = K.ps(st, [128, 512], F32, "pmean")
        pq = K.ps(st, [128, 512], F32, "pmsq")
        po = [K.ps(st, [128, 512], F32, "po") for _ in range(2)]
        nblk = S // 512
        cnt = [0]

        def prep(j):
            t0 = 512 * j
            w_ = uw[j % 2]
            lo = max(t0 - HALO, 0)
            hi = min(t0 + 512 + HALO, S)
            if lo != t0 - HALO or hi != t0 + 512 + HALO:
                K.memset(K.pool, w_[:, :, :], 0.0)
            c0_ = lo - (t0 - HALO)
            K.dma(K.sp, w_[:, :, c0_:c0_ + (hi - lo)], uT[:, lo:hi].rearrange("(m p) t -> p m t", p=128))
            K.cp(K.act, uwa[j % 2][:, :, 0:542], w_[:, :, :])
            K.cp(K.dve, uwb[j % 2][:, :, 1:543], w_[:, :, :])

        def load_at(j):
            t0 = 512 * j
            K.dma(K.sp, at[j % 2][:, :, :], attT[:, t0:t0 + 512].rearrange("(m p) t -> p m t", p=128))

        def convmm(j):
            wa, wb, acc, sqb = uwa[j % 2], uwb[j % 2], accs[j % 2], sqbs[j % 2]
            for m in range(4):
                pc_ = pcv[m % 2]
                for kk in range(CONV_K):
                    mv = wa[:, m, kk:kk + 512] if kk % 2 == 0 else wb[:, m, kk + 1:kk + 513]
                    K.mm(pc_[:, :], diag[:, m, kk, :], mv, start=(kk == 0), stop=(kk == CONV_K - 1))
                K.ts(K.dve, acc[:, m, :], pc_[:, :], cb[:, m:m + 1], None, ALU.add)
                K.actf(sqb[:, m, :], acc[:, m, :], AF.Square)

        def stats_mm(j):
            acc, sqb = accs[j % 2], sqbs[j % 2]
            for m in range(4):
                K.mm(pm[:, :], onesF[:, :], acc[:, m, :], start=(m == 0), stop=(m == 3))
            for m in range(4):
                K.mm(pq[:, :], onesF[:, :], sqb[:, m, :], start=(m == 0), stop=(m == 3))

        def chain(j):
            acc, aT = accs[j % 2], aTs[j % 2]
            K.cp(K.act, mean[:, :], pm[:, :])
            K.tt(K.dve, rs[:, :], mean[:, :], mean[:, :], ALU.mult)
            K.tt(K.dve, rs[:, :], pq[:, :], rs[:, :], ALU.subtract)
            K.ts(K.dve, rs[:, :], rs[:, :], EPS, None, ALU.add)
            K.actf(rs[:, :], rs[:, :], AF.Sqrt)
            K.recip(rs[:, :], rs[:, :])
            for m in range(4):
                K.tt(K.dve, acc[:, m, :], acc[:, m, :], mean[:, :], ALU.subtract)
                K.tt(K.dve, acc[:, m, :], acc[:, m, :], rs[:, :], ALU.mult)
                K.actf(aT[:, m, :], acc[:, m, :], AF.Silu, scale=lg[:, m:m + 1], bias=lb[:, m:m + 1])

        def outp(j):
            t0 = 512 * j
            aT, a_ = aTs[j % 2], at[j % 2]
            for i in range(4):
                x = xr[cnt[0] % 2]
                cnt[0] += 1
                K.dma(K.sp, x[:, :], x_in[t0 + 128 * i:t0 + 128 * (i + 1), :])
                cs = slice(128 * i, 128 * (i + 1))
                for n_ in range(2):
                    for k in range(8):
                        l_ = aT[:, k, cs] if k < 4 else a_[:, k - 4, cs]
                        K.mm(po[n_][:, :], l_, wout[:, k, n_ * 512:(n_ + 1) * 512], start=(k == 0), stop=(k == 7))
                for n_ in range(2):
                    K.tt(K.dve, x[:, n_ * 512:(n_ + 1) * 512], x[:, n_ * 512:(n_ + 1) * 512], po[n_][:, :], ALU.add)
                K.dma(K.pool, x_out[t0 + 128 * i:t0 + 128 * (i + 1), :], x[:, :])

        prep(0)
        load_at(0)
        convmm(0)
        if nblk > 1:
            prep(1)
            load_at(1)
        for j in range(nblk):
            stats_mm(j)
            chain(j)
            if j + 1 < nblk:
                convmm(j + 1)
            if j + 2 < nblk:
                prep(j + 2)
            outp(j)
            if j + 2 < nblk:
                load_at(j + 2)
        K.barrier()


WEIGHT_NAMES = ["ada_w", "ada_b", "norm_mix_g", "norm_ffn_g", "ffn_w_up", "ffn_conv_w", "ffn_conv_b",
                "ffn_w_down", "ev_w_in", "ev_conv_w", "ev_conv_b", "ev_ln_g", "ev_ln_b", "ev_qa_norm_g", "ev_w_uq",
                "ev_kva_norm_g", "ev_w_ukv", "ev_q_norm_g", "ev_k_norm_g", "ev_w_out", "od_w_in", "od_conv_w",
                "od_conv_b", "od_w_out"]


def build(S, shapes, debug=False, phases=("mod", "a", "b", "c", "f0", "m1", "f1")):
    nc = bass.Bass("TRN2", target_bir_lowering=False)
    I = {}
    I["x"] = nc.dram_tensor("x", [S, D], F32, kind="ExternalInput").ap()
    I["c"] = nc.dram_tensor("c", [D], F32, kind="ExternalInput").ap()
    I["ctx"] = nc.dram_tensor("ctx", [CTX, D], F32, kind="ExternalInput").ap()
    I["c_ctx"] = nc.dram_tensor("c_ctx", [D], F32, kind="ExternalInput").ap()
    for n in WEIGHT_NAMES:
        I[n] = nc.dram_tensor(n, list(shapes[n]), F32, kind="ExternalInput").ap()
    I["ident"] = nc.dram_tensor("ident", [128, 128], BF16, kind="ExternalInput").ap()
    I["rope"] = nc.dram_tensor("rope", [S, 64], F32, kind="ExternalInput").ap()
    y = nc.dram_tensor("y", [S, D], F32, kind="ExternalOutput").ap()
    sk = "ExternalOutput" if debug else "Internal"
    NK = S + CTX
    modv = nc.dram_tensor("modv", [2, 6 * D], F32, kind=sk).ap()
    modc = nc.dram_tensor("modc", [1, 2 * D], F32, kind=sk).ap()
    uT = nc.dram_tensor("uT", [CONV_CH, S], F32, kind=sk).ap()
    qT = nc.dram_tensor("qT", [NH, QK, S], BF16, kind=sk).ap()
    kT = nc.dram_tensor("kT", [NH, QK, NK], BF16, kind=sk).ap()
    vE = nc.dram_tensor("vE", [128, NK // 128, NH, VD + 1], BF16, kind=sk).ap()
    attT = nc.dram_tensor("attT", [NH * VD, S], BF16, kind=sk).ap()
    xa = nc.dram_tensor("xa", [S, D], F32, kind=sk).ap()
    xb = nc.dram_tensor("xb", [S, D], F32, kind=sk).ap()
    K = Kern(nc)
    with K.es:
        if "mod" in phases:
            phase_mod(K, I, modv, modc)
        if "a" in phases:
            phase_l0a(K, I, S, modv[0, :], modc[0, :], uT, qT, kT, vE)
        if "b" in phases:
            phase_attn(K, I, S, qT, kT, vE, attT)
        if "c" in phases:
            phase_l0c(K, I, S, modv[0, :], uT, attT, I["x"], xa)
        if "f0" in phases:
            phase_ffn(K, I, 0, S, xa, xb, modv[0, :])
        if "m1" in phases:
            phase_sconv(K, I, S, xb, xa, modv[1, :])
        if "f1" in phases:
            phase_ffn(K, I, 1, S, xa, y, modv[1, :])
        K.barrier()
    return nc


def rope_table(S):
    t = np.arange(S)
    row = (t // 64).astype(np.float32)
    col = (t % 64).astype(np.float32)
    half = 16
    inv = (10000.0 ** (-np.arange(0, half, 2, dtype=np.float32) / half)).astype(np.float32)
    ar = row[:, None] * inv[None, :]
    ac = col[:, None] * inv[None, :]
    cr, sr, cc, sc = np.cos(ar), np.sin(ar), np.cos(ac), np.sin(ac)
    C = np.concatenate([cr, cr, cc, cc], axis=1)
    Sg = np.concatenate([-sr, sr, -sc, sc], axis=1)
    return np.ascontiguousarray(np.concatenate([C, Sg], axis=1).astype(np.float32))


def kernel(debug=False, phases=("mod", "a", "b", "c", "f0", "m1", "f1"), **inputs):
    x = np.asarray(inputs["x"], dtype=np.float32)
    B, S, _ = x.shape
    assert B == 8
    shapes = {n: np.asarray(inputs[n]).shape for n in WEIGHT_NAMES}
    nc = build(S, shapes, debug=debug, phases=phases)
    ident = np.eye(128, dtype=np.float32).astype(ml_dtypes.bfloat16)
    rope = rope_table(S)
    shared = {n: np.ascontiguousarray(np.asarray(inputs[n], dtype=np.float32)) for n in WEIGHT_NAMES}
    shared["c_ctx"] = np.ascontiguousarray(np.asarray(inputs["c_ctx"], dtype=np.float32))
    shared["ident"] = ident
    shared["rope"] = rope
    in_maps = []
    for b in range(B):
        m = dict(shared)
        m["x"] = np.ascontiguousarray(x[b])
        m["c"] = np.ascontiguousarray(np.asarray(inputs["c"], dtype=np.float32)[b])
        m["ctx"] = np.ascontiguousarray(np.asarray(inputs["ctx"], dtype=np.float32)[b])
        in_maps.append(m)
    res = run_bass_kernel_spmd(nc, in_maps, core_ids=list(range(B)))
    if debug:
        return res
    return np.stack([np.asarray(r["y"], dtype=np.float32) for r in res.results], axis=0)
```

```python
import contextlib
import math
import numpy as np
import ml_dtypes
import concourse.bass as bass
import concourse.mybir as mybir
from concourse.bass_utils import run_bass_kernel_spmd

F32 = mybir.dt.float32
BF16 = mybir.dt.bfloat16
AF = mybir.ActivationFunctionType
ALU = mybir.AluOpType
AX = mybir.AxisListType

D = 1024
CTX = 256
CONV_CH = 512
CONV_K = 31
NH = 8
QK = 96
VD = 64
QL = 384
KVL = 256
EVEN_IN = 1696
FFN = 2816
EPS = 1e-6
SM_SCALE = QK ** -0.5
WSTR = 510


class Eng:
    def __init__(self, name, h, sem):
        self.name = name
        self.h = h
        self.sem = sem
        self.n = 0
        self.seen = {}


class SemC:
    def __init__(self, sem):
        self.sem = sem
        self.n = 0


class View:
    __slots__ = ("buf", "ap")

    def __init__(self, buf, ap):
        self.buf = buf
        self.ap = ap


class Buf:
    def __init__(self, K, t):
        self.K = K
        self.t = t
        self.cw = {}
        self.cr = {}
        self.pw = {}
        self.pr = {}
        self.has_read = False
        self.ld = None
        self.st = None

    def __getitem__(self, key):
        return View(self, self.t[key])

    def v(self, ap):
        return View(self, ap)


class Kern:
    def __init__(self, nc):
        self.nc = nc
        self.es = contextlib.ExitStack()
        mk = lambda n, h: Eng(n, h, self.es.enter_context(nc.semaphore("sem_" + n)))
        self.pe = mk("pe", nc.tensor)
        self.act = mk("act", nc.scalar)
        self.dve = mk("dve", nc.vector)
        self.pool = mk("pool", nc.gpsimd)
        self.sp = mk("sp", nc.sync)
        self.engs = [self.pe, self.act, self.dve, self.pool, self.sp]
        self.free_sems = [SemC(self.es.enter_context(nc.semaphore("dsem%d" % i))) for i in range(88)]
        self.live = []
        self.setup_sem = self.free_sems.pop()
        self.uid = 0

    def sb(self, st, shape, dt, name=None):
        self.uid += 1
        t = st.enter_context(self.nc.sbuf_tensor("%s_%d" % (name or "sb", self.uid), list(shape), dt))
        return Buf(self, t)

    def ps(self, st, shape, dt, name=None):
        self.uid += 1
        t = st.enter_context(self.nc.psum_tensor("%s_%d" % (name or "ps", self.uid), list(shape), dt))
        return Buf(self, t)

    def _need(self, e, sem, val):
        key = id(sem)
        if e.seen.get(key, 0) >= val:
            return
        e.h.wait_ge(sem, val)
        e.seen[key] = val

    def _dep(self, e, b, tag, idx, raw):
        if tag == "LD":
            self._need(e, b.ld.sem, idx)
        elif tag == "ST":
            self._need(e, b.st.sem, idx)
        else:
            if tag is e and not raw:
                return
            self._need(e, tag.sem, idx + 1)

    def _rdeps(self, e, b):
        for tag, idx in b.cw.items():
            self._dep(e, b, tag, idx, True)

    def _wdeps(self, e, b, also_cur=False):
        if b.has_read:
            b.pr, b.pw, b.cr, b.cw, b.has_read = b.cr, b.cw, {}, {}, False
        for tag, idx in b.pr.items():
            self._dep(e, b, tag, idx, False)
        for tag, idx in b.pw.items():
            self._dep(e, b, tag, idx, False)
        if also_cur:
            for tag, idx in b.cw.items():
                self._dep(e, b, tag, idx, False)

    def issue(self, e, reads, writes, fn, sig=True, wait_cur=False):
        rb = []
        for v in reads:
            if isinstance(v, View) and v.buf not in rb:
                rb.append(v.buf)
        wb = []
        for v in writes:
            if v.buf not in wb:
                wb.append(v.buf)
        for b in rb:
            self._rdeps(e, b)
        for b in wb:
            self._wdeps(e, b, also_cur=wait_cur)
        ins = fn()
        idx = e.n
        if sig:
            ins.then_inc(e.sem, 1)
            e.n += 1
        for b in rb:
            if b not in wb:
                b.cr[e] = idx
                b.has_read = True
        for b in wb:
            b.cw[e] = idx
        return ins

    def _getsem(self, b, which):
        s = getattr(b, which)
        if s is None:
            s = self.free_sems.pop()
            setattr(b, which, s)
            if b not in self.live:
                self.live.append(b)
        return s

    def dma(self, q, out, in_, **kw):
        nc = self.nc
        if isinstance(out, View):
            b = out.buf
            self._wdeps(q, b, also_cur=True)
            s = self._getsem(b, "ld")
            q.h.dma_start(out=out.ap, in_=in_, **kw).then_inc(s.sem, 16)
            s.n += 16
            b.cw["LD"] = s.n
        else:
            b = in_.buf
            self._rdeps(q, b)
            s = self._getsem(b, "st")
            q.h.dma_start(out=out, in_=in_.ap, **kw).then_inc(s.sem, 16)
            s.n += 16
            b.cr["ST"] = s.n
            b.has_read = True

    def cload(self, out, in_, q=None, **kw):
        q = q or self.sp
        kw.setdefault("allow_slow_non_contiguous", True)
        s = self.setup_sem
        q.h.dma_start(out=out.ap, in_=in_, **kw).then_inc(s.sem, 16)
        s.n += 16

    def setup_done(self):
        for e in self.engs:
            self._need(e, self.setup_sem.sem, self.setup_sem.n)

    def barrier(self):
        for e in self.engs:
            for f in self.engs:
                if f is not e and f.n > 0:
                    self._need(e, f.sem, f.n)
            for b in self.live:
                for s in (b.ld, b.st):
                    if s is not None and s.n > 0:
                        self._need(e, s.sem, s.n)
            self._need(e, self.setup_sem.sem, self.setup_sem.n)
        for b in self.live:
            for w in ("ld", "st"):
                s = getattr(b, w)
                if s is not None:
                    self.free_sems.append(s)
                    setattr(b, w, None)
            b.cw, b.cr, b.pw, b.pr, b.has_read = {}, {}, {}, {}, False
        self.live = []

    @staticmethod
    def _a(v):
        return v.ap if isinstance(v, View) else v

    def mm(self, out, lhsT, rhs, start, stop, sig=None):
        if sig is None:
            sig = stop
        return self.issue(self.pe, [lhsT, rhs], [out],
                          lambda: self.nc.tensor.matmul(out.ap, lhsT.ap, rhs.ap, start=start, stop=stop), sig)

    def tr(self, out, in_, ident, sig=True):
        return self.issue(self.pe, [in_, ident], [out],
                          lambda: self.nc.tensor.transpose(out.ap, in_.ap, ident.ap), sig)

    def actf(self, out, in_, func, scale=None, bias=None, accum=None):
        kw = {}
        rd = [in_]
        wr = [out]
        if scale is not None:
            kw["scale"] = self._a(scale)
            rd.append(scale)
        if bias is not None:
            kw["bias"] = self._a(bias)
            rd.append(bias)
        if accum is not None:
            kw["accum_out"] = accum.ap
            wr.append(accum)
        return self.issue(self.act, rd, wr, lambda: self.nc.scalar.activation(out.ap, in_.ap, func, **kw))

    def _ve(self, e):
        return self.nc.vector if e is self.dve else self.nc.gpsimd

    def tt(self, e, out, in0, in1, op):
        return self.issue(e, [in0, in1], [out], lambda: self._ve(e).tensor_tensor(out.ap, in0.ap, in1.ap, op))

    def ts(self, e, out, in0, s1, s2, op0, op1=None):
        rd = [in0, s1, s2]
        if op1 is None:
            return self.issue(e, rd, [out], lambda: self._ve(e).tensor_scalar(out.ap, in0.ap, self._a(s1), None, op0))
        return self.issue(e, rd, [out],
                          lambda: self._ve(e).tensor_scalar(out.ap, in0.ap, self._a(s1), self._a(s2), op0, op1))

    def stt(self, out, in0, s, in1, op0, op1):
        return self.issue(self.dve, [in0, s, in1], [out],
                          lambda: self.nc.vector.scalar_tensor_tensor(out.ap, in0.ap, self._a(s), in1.ap, op0, op1))

    def cp(self, e, out, in_):
        if e is self.act:
            return self.issue(e, [in_], [out], lambda: self.nc.scalar.copy(out.ap, in_.ap))
        return self.issue(e, [in_], [out], lambda: self._ve(e).tensor_copy(out.ap, in_.ap))

    def recip(self, out, in_):
        return self.issue(self.dve, [in_], [out], lambda: self.nc.vector.reciprocal(out.ap, in_.ap))

    def red(self, out, in_, op=None):
        return self.issue(self.dve, [in_], [out],
                          lambda: self.nc.vector.tensor_reduce(out.ap, in_.ap, AX.X, op or ALU.add))

    def memset(self, e, out, val, wait_cur=False):
        return self.issue(e, [], [out], lambda: self._ve(e).memset(out.ap, val), wait_cur=wait_cur)

    def rsqrt(self, out, in_, mul, eps):
        self.ts(self.dve, out, in_, mul, eps, ALU.mult, ALU.add)
        self.actf(out, out, AF.Sqrt)
        self.recip(out, out)


def bc(ap, shape):
    return ap.broadcast_to(list(shape))


def load_cast(K, st, dst, src, nk, F, colscale=None, rowscale=None, stg=None):
    CH = 1024
    i = 0
    for k in range(nk):
        for c0 in range(0, F, CH):
            w = min(CH, F - c0)
            s = stg[i % len(stg)]
            K.dma(K.sp, s[:, 0:w], src[k * 128:(k + 1) * 128, c0:c0 + w])
            e = K.dve if i % 2 == 0 else K.pool
            d = dst[:, k, c0:c0 + w]
            if colscale is not None:
                K.tt(e, d, s[:, 0:w], colscale[:, c0:c0 + w], ALU.mult)
            elif rowscale is not None:
                K.ts(e, d, s[:, 0:w], rowscale[:, k:k + 1], None, ALU.mult)
            else:
                K.cp(e, d, s[:, 0:w])
            i += 1


def colvec(ap1d, nk):
    return ap1d.rearrange("(k p o) -> p k o", p=128, o=1)


def norm_block(K, A, x_rows, tiles, gs, sh, ident, hT, hb, xt, ss, rstd, pT, do_norm_now=True):
    nt = len(tiles)
    for (i, p_lo, p_hi, tok_lo, w) in tiles:
        x = xt[i % len(xt)]
        if p_lo > 0 or p_hi < 128:
            K.memset(K.dve, x[:, :], 0.0)
        K.dma(K.sp, x[p_lo:p_hi, :], x_rows[tok_lo:tok_lo + (p_hi - p_lo), :])
        K.actf(A["junk"][:, :], x[:, :], AF.Square, accum=ss[:, i:i + 1])
    K.rsqrt(rstd[:, 0:nt], ss[:, 0:nt], 1.0 / D, EPS)
    for (i, p_lo, p_hi, tok_lo, w) in tiles:
        x = xt[i % len(xt)]
        K.ts(K.dve, hb[i][:, :], x[:, :], rstd[:, i:i + 1], None, ALU.mult)


def transposes_block(K, tiles, gs, sh, ident, hT, hb, pT, zero_cols):
    for (i, p_lo, p_hi, tok_lo, w) in tiles:
        p = pT[i % len(pT)]
        for k in range(8):
            K.tr(p[:, k, :], hb[i][:, k * 128:(k + 1) * 128], ident[:, :], sig=(k == 7))
        for k in range(8):
            K.actf(hT[:, k, i * 128:i * 128 + w], p[:, k, 0:w], AF.Identity,
                   scale=gs[:, k:k + 1], bias=sh[:, k:k + 1])
    for c in zero_cols:
        K.memset(K.dve, hT[:, :, c:c + 1], 0.0, wait_cur=True)


def win_blocks(S):
    blocks = []
    j = 0
    while WSTR * j < S:
        nvalid = min(WSTR, S - WSTR * j)
        t0 = WSTR * j - 1
        Wd = nvalid + 2
        tiles = []
        for i in range((Wd + 127) // 128):
            a = t0 + 128 * i
            tok_lo = max(a, 0)
            tok_hi = min(a + 128, S, t0 + Wd)
            w = min(128, Wd - 128 * i)
            tiles.append((i, tok_lo - a, tok_hi - a, tok_lo, w))
        zero_cols = []
        if t0 < 0:
            zero_cols.append(0)
        if t0 + Wd - 1 >= S:
            zero_cols.append(Wd - 1)
        blocks.append(dict(j=j, t0=t0, Wd=Wd, nvalid=nvalid, tiles=tiles, zero_cols=zero_cols))
        j += 1
    return blocks


def load_mod_cols(K, st, modrow, sec_shift, sec_scale, normg, name):
    gs = K.sb(st, [128, 8], F32, name + "gs")
    sh = K.sb(st, [128, 8], F32, name + "sh")
    ng = K.sb(st, [128, 8], F32, name + "ng")
    K.cload(gs.v(gs.t[:, :].rearrange("p (k o) -> p k o", o=1)), colvec(modrow[sec_scale * D:(sec_scale + 1) * D], 8))
    K.cload(sh.v(sh.t[:, :].rearrange("p (k o) -> p k o", o=1)), colvec(modrow[sec_shift * D:(sec_shift + 1) * D], 8))
    K.cload(ng.v(ng.t[:, :].rearrange("p (k o) -> p k o", o=1)), colvec(normg, 8))
    K.setup_done()
    K.stt(gs[:, :], gs[:, :], 1.0, ng[:, :], ALU.add, ALU.mult)
    return gs, sh


def phase_mod(K, I, modv, modc):
    nc = K.nc
    with contextlib.ExitStack() as st:
        cT = K.sb(st, [128, 8], F32, "cT")
        ccT = K.sb(st, [128, 8], F32, "ccT")
        ab = K.sb(st, [1, 2 * 6 * D], F32, "ab")
        stg = [K.sb(st, [128, 8, 512], F32, "adastg") for _ in range(2)]
        row = [K.sb(st, [1, 512], F32, "modrow") for _ in range(2)]
        pm = [K.ps(st, [128, 512], F32, "pm") for _ in range(2)]
        K.cload(cT.v(cT.t[:, :].rearrange("p (k o) -> p k o", o=1)), colvec(I["c"], 8))
        K.cload(ccT.v(ccT.t[:, :].rearrange("p (k o) -> p k o", o=1)), colvec(I["c_ctx"], 8))
        K.cload(ab[:, :], I["ada_b"].rearrange("(o l) f -> o (l f)", o=1))
        K.setup_done()
        K.actf(cT[:, :], cT[:, :], AF.Silu)
        K.actf(ccT[:, :], ccT[:, :], AF.Silu)
        it = 0
        for layer in range(2):
            for g in range(12):
                s = stg[it % 2]
                K.dma(K.sp, s[:, :, :], I["ada_w"][layer, :, g * 512:(g + 1) * 512].rearrange("(k p) f -> p k f", p=128))
                srcs = [(cT, modv[layer:layer + 1, g * 512:(g + 1) * 512])]
                if layer == 0 and g < 4:
                    srcs.append((ccT, modc[0:1, g * 512:(g + 1) * 512]))
                for (vec, dst) in srcs:
                    p = pm[it % 2]
                    r = row[it % 2]
                    for k in range(8):
                        K.mm(p[0:1, :], vec[:, k:k + 1], s[:, k, :], start=(k == 0), stop=(k == 7))
                    K.tt(K.dve, r[:, :], p[0:1, :], ab[:, layer * 6 * D + g * 512: layer * 6 * D + (g + 1) * 512], ALU.add)
                    K.dma(K.pool, dst, r[:, :])
                    it += 1
        K.barrier()


def phase_ffn(K, I, layer, S, x_in, x_out, modrow):
    nc = K.nc
    NM = FFN // 128
    with contextlib.ExitStack() as st:
        wup = K.sb(st, [128, 8, 2 * FFN], BF16, "wup")
        wdn = K.sb(st, [128, NM, D], BF16, "wdn")
        ident = K.sb(st, [128, 128], BF16, "ident")
        cw = K.sb(st, [128, NM, 3], F32, "cw")
        cb = K.sb(st, [128, NM], F32, "cb")
        K.cload(ident[:, :], I["ident"])
        for kk in range(3):
            K.cload(cw.v(cw.t[:, :, kk:kk + 1]), colvec(I["ffn_conv_w"][layer, kk, :], NM), allow_slow_non_contiguous=True)
        K.cload(cb.v(cb.t[:, :].rearrange("p (k o) -> p k o", o=1)), colvec(I["ffn_conv_b"][layer, :], NM))
        gs, sh = load_mod_cols(K, st, modrow, 3, 4, I["norm_ffn_g"][layer, :], "f")
        with contextlib.ExitStack() as st2:
            G = K.sb(st2, [128, D], F32, "G")
            stg = [K.sb(st2, [128, 1024], F32, "wstg") for _ in range(3)]
            K.cload(G[:, :], modrow[5 * D:6 * D].partition_broadcast(128))
            K.setup_done()
            load_cast(K, st2, wup, I["ffn_w_up"][layer], 8, 2 * FFN, stg=stg)
            load_cast(K, st2, wdn, I["ffn_w_down"][layer], NM, D, colscale=G, stg=stg)
            K.barrier()
        hid = K.sb(st, [128, NM, 512], BF16, "hid")
        hT = K.sb(st, [128, 8, 512], BF16, "hT")
        hb = [K.sb(st, [128, D], BF16, "hb") for _ in range(4)]
        xt = [K.sb(st, [128, D], F32, "xt") for _ in range(4)]
        xr = [K.sb(st, [128, D], F32, "xr") for _ in range(2)]
        acc = [K.sb(st, [128, 512], F32, "acc") for _ in range(2)]
        sl = [K.sb(st, [128, 512], BF16, "sl") for _ in range(2)]
        A = {"junk": K.sb(st, [128, D], BF16, "junk")}
        ss = K.sb(st, [128, 4], F32, "ss")
        rstd = K.sb(st, [128, 4], F32, "rstd")
        pT = [K.ps(st, [128, 8, 128], BF16, "pT") for _ in range(2)]
        pg = [K.ps(st, [128, 512], F32, "pg") for _ in range(2)]
        pv = [K.ps(st, [128, 512], F32, "pv") for _ in range(2)]
        po = [K.ps(st, [128, 512], F32, "po") for _ in range(2)]
        blocks = win_blocks(S)

        def pre_norm(b):
            norm_block(K, A, x_in, b["tiles"], gs, sh, ident, hT, hb, xt, ss, rstd, pT)

        def pre_tr(b):
            transposes_block(K, b["tiles"], gs, sh, ident, hT, hb, pT, b["zero_cols"])

        def up(b):
            Wd = b["Wd"]
            n = Wd - 2
            for m in range(NM):
                g_, v_ = pg[m % 2], pv[m % 2]
                for k in range(8):
                    K.mm(g_[:, 0:Wd], wup[:, k, m * 128:(m + 1) * 128], hT[:, k, 0:Wd], start=(k == 0), stop=(k == 7))
                for k in range(8):
                    K.mm(v_[:, 0:Wd], wup[:, k, FFN + m * 128:FFN + (m + 1) * 128], hT[:, k, 0:Wd], start=(k == 0), stop=(k == 7))
                a = acc[m % 2]
                K.ts(K.dve, a[:, 0:n], g_[:, 0:n], cw[:, m, 0:1], None, ALU.mult)
                K.stt(a[:, 0:n], g_[:, 1:n + 1], cw[:, m, 1:2], a[:, 0:n], ALU.mult, ALU.add)
                K.stt(a[:, 0:n], g_[:, 2:n + 2], cw[:, m, 2:3], a[:, 0:n], ALU.mult, ALU.add)
                s_ = sl[m % 2]
                K.actf(s_[:, 0:n], a[:, 0:n], AF.Silu, bias=cb[:, m:m + 1])
                K.tt(K.dve, hid[:, m, 0:n], s_[:, 0:n], v_[:, 1:n + 1], ALU.mult)

        def down(b, cnt):
            nv = b["nvalid"]
            tok0 = WSTR * b["j"]
            for i2 in range((nv + 127) // 128):
                r = min(128, nv - 128 * i2)
                x = xr[cnt[0] % 2]
                cnt[0] += 1
                K.dma(K.sp, x[0:r, :], x_in[tok0 + 128 * i2: tok0 + 128 * i2 + r, :])
                for n_ in range(2):
                    for k in range(NM):
                        K.mm(po[n_][0:r, :], hid[:, k, 128 * i2:128 * i2 + r], wdn[:, k, n_ * 512:(n_ + 1) * 512],
                             start=(k == 0), stop=(k == NM - 1))
                for n_ in range(2):
                    K.tt(K.dve, x[0:r, n_ * 512:(n_ + 1) * 512], x[0:r, n_ * 512:(n_ + 1) * 512], po[n_][0:r, :], ALU.add)
                K.dma(K.pool, x_out[tok0 + 128 * i2: tok0 + 128 * i2 + r, :], x[0:r, :])

        cnt = [0]
        pre_norm(blocks[0])
        pre_tr(blocks[0])
        for bi, b in enumerate(blocks):
            nxt = blocks[bi + 1] if bi + 1 < len(blocks) else None
            if nxt:
                pre_norm(nxt)
            up(b)
            if nxt:
                pre_tr(nxt)
            down(b, cnt)
        K.barrier()


def phase_sconv(K, I, S, x_in, x_out, modrow):
    nc = K.nc
    with contextlib.ExitStack() as st:
        win = K.sb(st, [128, 8, 3 * D], BF16, "odwin")
        wout = K.sb(st, [128, 8, D], BF16, "odwout")
        ident = K.sb(st, [128, 128], BF16, "ident")
        cw = K.sb(st, [128, 8, 3], F32, "cw")
        cb = K.sb(st, [128, 8], F32, "cb")
        K.cload(ident[:, :], I["ident"])
        for kk in range(3):
            K.cload(cw.v(cw.t[:, :, kk:kk + 1]), colvec(I["od_conv_w"][0, kk, :], 8), allow_slow_non_contiguous=True)
        K.cload(cb.v(cb.t[:, :].rearrange("p (k o) -> p k o", o=1)), colvec(I["od_conv_b"][0, :], 8))
        gs, sh = load_mod_cols(K, st, modrow, 0, 1, I["norm_mix_g"][1, :], "m1")
        with contextlib.ExitStack() as st2:
            G = K.sb(st2, [128, D], F32, "G")
            stg = [K.sb(st2, [128, 1024], F32, "wstg") for _ in range(3)]
            K.cload(G[:, :], modrow[2 * D:3 * D].partition_broadcast(128))
            K.setup_done()
            load_cast(K, st2, win, I["od_w_in"][0], 8, 3 * D, stg=stg)
            load_cast(K, st2, wout, I["od_w_out"][0], 8, D, colscale=G, stg=stg)
            K.barrier()
        yT = K.sb(st, [128, 8, 512], BF16, "yT")
        hT = K.sb(st, [128, 8, 512], BF16, "hT")
        hb = [K.sb(st, [128, D], BF16, "hb") for _ in range(4)]
        xt = [K.sb(st, [128, D], F32, "xt") for _ in range(4)]
        xr = [K.sb(st, [128, D], F32, "xr") for _ in range(2)]
        acc = [K.sb(st, [128, 512], F32, "acc") for _ in range(2)]
        us = [K.sb(st, [128, 512], F32, "us") for _ in range(2)]
        A = {"junk": K.sb(st, [128, D], BF16, "junk")}
        ss = K.sb(st, [128, 4], F32, "ss")
        rstd = K.sb(st, [128, 4], F32, "rstd")
        pT = [K.ps(st, [128, 8, 128], BF16, "pT") for _ in range(1)]
        pb = [K.ps(st, [128, 512], F32, "pb") for _ in range(2)]
        pc = [K.ps(st, [128, 512], F32, "pc") for _ in range(2)]
        pu = [K.ps(st, [128, 512], F32, "pu") for _ in range(1)]
        po = [K.ps(st, [128, 512], F32, "po") for _ in range(2)]
        blocks = win_blocks(S)

        def mix(b):
            Wd = b["Wd"]
            n = Wd - 2
            for m in range(8):
                b_, c_, u_ = pb[m % 2], pc[m % 2], pu[0]
                for (dst, off) in ((u_, 2 * D), (c_, D), (b_, 0)):
                    for k in range(8):
                        K.mm(dst[:, 0:Wd], win[:, k, off + m * 128: off + (m + 1) * 128], hT[:, k, 0:Wd],
                             start=(k == 0), stop=(k == 7))
                u = us[m % 2]
                K.cp(K.act, u[:, 0:Wd], u_[:, 0:Wd])
                K.tt(K.dve, u[:, 0:Wd], u[:, 0:Wd], c_[:, 0:Wd], ALU.mult)
                a = acc[m % 2]
                K.ts(K.dve, a[:, 0:n], u[:, 0:n], cw[:, m, 0:1], cb[:, m:m + 1], ALU.mult, ALU.add)
                K.stt(a[:, 0:n], u[:, 1:n + 1], cw[:, m, 1:2], a[:, 0:n], ALU.mult, ALU.add)
                K.stt(a[:, 0:n], u[:, 2:n + 2], cw[:, m, 2:3], a[:, 0:n], ALU.mult, ALU.add)
                K.tt(K.dve, yT[:, m, 0:n], a[:, 0:n], b_[:, 1:n + 1], ALU.mult)

        def outp(b, cnt):
            nv = b["nvalid"]
            tok0 = WSTR * b["j"]
            for i2 in range((nv + 127) // 128):
                r = min(128, nv - 128 * i2)
                x = xr[cnt[0] % 2]
                cnt[0] += 1
                K.dma(K.sp, x[0:r, :], x_in[tok0 + 128 * i2: tok0 + 128 * i2 + r, :])
                for n_ in range(2):
                    for k in range(8):
                        K.mm(po[n_][0:r, :], yT[:, k, 128 * i2:128 * i2 + r], wout[:, k, n_ * 512:(n_ + 1) * 512],
                             start=(k == 0), stop=(k == 7))
                for n_ in range(2):
                    K.tt(K.dve, x[0:r, n_ * 512:(n_ + 1) * 512], x[0:r, n_ * 512:(n_ + 1) * 512], po[n_][0:r, :], ALU.add)
                K.dma(K.pool, x_out[tok0 + 128 * i2: tok0 + 128 * i2 + r, :], x[0:r, :])

        cnt = [0]
        norm_block(K, A, x_in, blocks[0]["tiles"], gs, sh, ident, hT, hb, xt, ss, rstd, pT)
        transposes_block(K, blocks[0]["tiles"], gs, sh, ident, hT, hb, pT, blocks[0]["zero_cols"])
        for bi, b in enumerate(blocks):
            nxt = blocks[bi + 1] if bi + 1 < len(blocks) else None
            if nxt:
                norm_block(K, A, x_in, nxt["tiles"], gs, sh, ident, hT, hb, xt, ss, rstd, pT)
            mix(b)
            if nxt:
                transposes_block(K, nxt["tiles"], gs, sh, ident, hT, hb, pT, nxt["zero_cols"])
            outp(b, cnt)
        K.barrier()


def phase_l0a(K, I, S, modrow, modc, uT, qT, kT, vE):
    nc = K.nc
    NK = S + CTX
    with contextlib.ExitStack() as st:
        win = K.sb(st, [128, 8, EVEN_IN], BF16, "evwin")
        wuq = K.sb(st, [128, 3, NH * QK], BF16, "wuq")
        wukv = K.sb(st, [128, 2, NH * 128], BF16, "wukv")
        ident = K.sb(st, [128, 128], BF16, "ident")
        QG = K.sb(st, [128, QK], F32, "QG")
        KG = K.sb(st, [128, QK], F32, "KG")
        rope = K.sb(st, [128, S // 128, 64], F32, "rope")
        qag = K.sb(st, [128, 3], F32, "qag")
        kvag = K.sb(st, [128, 2], F32, "kvag")
        K.cload(ident[:, :], I["ident"])
        K.cload(QG[:, :], I["ev_q_norm_g"][0, :].partition_broadcast(128))
        K.cload(KG[:, :], I["ev_k_norm_g"][0, :].partition_broadcast(128))
        K.cload(rope[:, :, :], I["rope"].rearrange("(i p) c -> p i c", p=128))
        K.cload(qag.v(qag.t[:, :].rearrange("p (k o) -> p k o", o=1)), colvec(I["ev_qa_norm_g"][0, :], 3))
        K.cload(kvag.v(kvag.t[:, :].rearrange("p (k o) -> p k o", o=1)), colvec(I["ev_kva_norm_g"][0, :], 2))
        gs, sh = load_mod_cols(K, st, modrow, 0, 1, I["norm_mix_g"][0, :], "m0")
        gsc, shc = load_mod_cols(K, st, modc, 0, 1, I["norm_mix_g"][0, :], "m0c")
        with contextlib.ExitStack() as st2:
            stg = [K.sb(st2, [128, 1024], F32, "wstg") for _ in range(3)]
            load_cast(K, st2, win, I["ev_w_in"][0], 8, EVEN_IN, stg=stg)
            load_cast(K, st2, wuq, I["ev_w_uq"][0], 3, NH * QK, rowscale=qag, stg=stg)
            load_cast(K, st2, wukv, I["ev_w_ukv"][0], 2, NH * 128, rowscale=kvag, stg=stg)
            K.barrier()
        hT = K.sb(st, [128, 8, 512], BF16, "hT")
        hb = [K.sb(st, [128, D], BF16, "hb") for _ in range(4)]
        xt = [K.sb(st, [128, D], F32, "xt") for _ in range(4)]
        A = {"junk": K.sb(st, [128, D], BF16, "junk")}
        ss = K.sb(st, [128, 4], F32, "ss")
        rstd = K.sb(st, [128, 4], F32, "rstd")
        th = [K.sb(st, [128, 512], F32, "th") for _ in range(2)]
        ust = [K.sb(st, [128, 512], F32, "ust") for _ in range(2)]
        ccT = K.sb(st, [128, 5, 512], BF16, "ccT")
        krs = [K.sb(st, [128, 32], F32, "krs") for _ in range(4)]
        st2t = K.sb(st, [128, 4, 4], F32, "st2")
        r2 = K.sb(st, [128, 4, 2], F32, "r2")
        qfs = [K.sb(st, [128, NH, QK], F32, "qf") for _ in range(4)]
        kvfs = [K.sb(st, [128, NH, 128], F32, "kvf") for _ in range(4)]
        sq = K.sb(st, [128, NH, QK], F32, "sq")
        ssh = K.sb(st, [128, 4, 2 * NH], F32, "ssh")
        rh = K.sb(st, [128, 4, 2 * NH], F32, "rh")
        R = K.sb(st, [128, NH, 32], F32, "R")
        T1 = K.sb(st, [128, NH, 32], F32, "T1")
        U = K.sb(st, [128, NH, 32], F32, "U")
        qb = [K.sb(st, [128, NH, QK], BF16, "qb") for _ in range(4)]
        kb = [K.sb(st, [128, NH, QK], BF16, "kb") for _ in range(4)]
        vb = [K.sb(st, [128, NH, VD + 1], BF16, "vb") for _ in range(4)]
        qTs = [K.sb(st, [128, NH, 512], BF16, "qTs") for _ in range(2)]
        kTs = [K.sb(st, [128, NH, 512], BF16, "kTs") for _ in range(2)]
        for v_ in vb:
            K.memset(K.dve, v_[:, :, VD:VD + 1], 1.0)
        pT = [K.ps(st, [128, 8, 128], BF16, "pT")]
        pA = [K.ps(st, [128, 512], F32, "pA") for _ in range(2)]
        pQ = [K.ps(st, [128, 512], F32, "pQ") for _ in range(2)]
        pTq = K.ps(st, [128, 8, 128], BF16, "pTq")
        pTk = K.ps(st, [128, 8, 128], BF16, "pTk")
        blks = [dict(ctx=True, x=I["ctx"], t0=0, nt=CTX // 128, key0=0)]
        for j in range(S // 512):
            blks.append(dict(ctx=False, x=I["x"], t0=512 * j, nt=4, key0=CTX + 512 * j))
        tcount = 0
        for bi, b in enumerate(blks):
            nt = b["nt"]
            Wd = nt * 128
            isctx = b["ctx"]
            g_, s_ = (gsc, shc) if isctx else (gs, sh)
            tiles = [(i, 0, 128, b["t0"] + 128 * i, 128) for i in range(nt)]
            if bi == 0:
                norm_block(K, A, b["x"], tiles, g_, s_, ident, hT, hb, xt, ss, rstd, pT)
            transposes_block(K, tiles, g_, s_, ident, hT, hb, pT, [])
            if bi + 1 < len(blks):
                nb_ = blks[bi + 1]
                ntiles = [(i, 0, 128, nb_["t0"] + 128 * i, 128) for i in range(nb_["nt"])]
                norm_block(K, A, nb_["x"], ntiles, None, None, ident, hT, hb, xt, ss, rstd, pT)
            pi = 0
            if not isctx:
                for m in range(4):
                    pv_, pg_ = pA[0], pA[1]
                    for k in range(8):
                        K.mm(pv_[:, 0:Wd], win[:, k, m * 128:(m + 1) * 128], hT[:, k, 0:Wd], start=(k == 0), stop=(k == 7))
                    for k in range(8):
                        K.mm(pg_[:, 0:Wd], win[:, k, 512 + m * 128:512 + (m + 1) * 128], hT[:, k, 0:Wd], start=(k == 0), stop=(k == 7))
                    t_ = th[m % 2]
                    K.actf(t_[:, :], pg_[:, :], AF.Tanh, scale=0.5)
                    u_ = ust[m % 2]
                    K.stt(u_[:, :], t_[:, :], 1.0, pv_[:, :], ALU.add, ALU.mult)
                    K.dma(K.pool, uT[m * 128:(m + 1) * 128, b["t0"]:b["t0"] + 512], u_[:, :])
            for m in range(5):
                if isctx and m < 3:
                    continue
                p_ = pA[m % 2]
                for k in range(8):
                    K.mm(p_[:, 0:Wd], win[:, k, 1024 + m * 128:1024 + (m + 1) * 128], hT[:, k, 0:Wd], start=(k == 0), stop=(k == 7))
                K.cp(K.dve, ccT[:, m, 0:Wd], p_[:, 0:Wd])
            qs, ks = qTs[bi % 2], kTs[bi % 2]
            J = A["junk"]
            tb = b["t0"] // 128
            for i in range(nt):
                cs = slice(i * 128, (i + 1) * 128)
                p1, p2 = pA[0], pA[1]
                for k in range(8):
                    K.mm(p1[:, :], hT[:, k, cs], win[:, k, 1024:1536], start=(k == 0), stop=(k == 7))
                for k in range(8):
                    K.mm(p2[:, 0:160], hT[:, k, cs], win[:, k, 1536:1696], start=(k == 0), stop=(k == 7))
                K.actf(J[:, 0:384], p1[:, 0:384], AF.Square, accum=st2t[:, i, 0:1])
                K.actf(J[:, 0:128], p1[:, 384:512], AF.Square, accum=st2t[:, i, 1:2])
                K.actf(J[:, 0:128], p2[:, 0:128], AF.Square, accum=st2t[:, i, 2:3])
                K.actf(krs[i][:, :], p2[:, 128:160], AF.Identity)
                K.actf(J[:, 0:32], p2[:, 128:160], AF.Square, accum=st2t[:, i, 3:4])
            K.tt(K.dve, st2t[:, 0:nt, 1], st2t[:, 0:nt, 1], st2t[:, 0:nt, 2], ALU.add)
            K.ts(K.dve, r2[:, 0:nt, 0], st2t[:, 0:nt, 0], 1.0 / QL, EPS, ALU.mult, ALU.add)
            K.ts(K.dve, r2[:, 0:nt, 1], st2t[:, 0:nt, 1], 1.0 / KVL, EPS, ALU.mult, ALU.add)
            K.actf(r2[:, 0:nt, :], r2[:, 0:nt, :], AF.Sqrt)
            K.recip(r2[:, 0:nt, :], r2[:, 0:nt, :])
            for i in range(nt):
                cs = slice(i * 128, (i + 1) * 128)
                qf, kvf = qfs[i], kvfs[i]
                qfl = qf.t[:, :, :].rearrange("p h d -> p (h d)")
                kvfl = kvf.t[:, :, :].rearrange("p h d -> p (h d)")
                if not isctx:
                    pq0, pq1 = pQ[0], pQ[1]
                    for k in range(3):
                        K.mm(pq0[:, :], ccT[:, k, cs], wuq[:, k, 0:512], start=(k == 0), stop=(k == 2))
                    for k in range(3):
                        K.mm(pq1[:, 0:256], ccT[:, k, cs], wuq[:, k, 512:768], start=(k == 0), stop=(k == 2))
                    K.actf(qf.v(qfl[:, 0:512]), pq0[:, :], AF.Identity, scale=r2[:, i, 0:1])
                    K.actf(qf.v(qfl[:, 512:768]), pq1[:, 0:256], AF.Identity, scale=r2[:, i, 0:1])
                    K.tt(K.dve, sq[:, :, :], qf[:, :, :], qf[:, :, :], ALU.mult)
                    K.red(ssh[:, i, 0:NH], sq[:, :, :])
                pk0, pk1 = pQ[0], pQ[1]
                for n_, pk in enumerate((pk0, pk1)):
                    for k in range(2):
                        K.mm(pk[:, :], ccT[:, 3 + k, cs], wukv[:, k, n_ * 512:(n_ + 1) * 512], start=(k == 0), stop=(k == 1))
                K.actf(kvf.v(kvfl[:, 0:512]), pk0[:, :], AF.Identity, scale=r2[:, i, 1:2])
                K.actf(kvf.v(kvfl[:, 512:1024]), pk1[:, :], AF.Identity, scale=r2[:, i, 1:2])
                K.tt(K.dve, sq[:, :, 0:64], kvf[:, :, 0:64], kvf[:, :, 0:64], ALU.mult)
                K.red(ssh[:, i, NH:2 * NH], sq[:, :, 0:64])
                K.ts(K.dve, ssh[:, i, NH:2 * NH], ssh[:, i, NH:2 * NH], st2t[:, i, 3:4], None, ALU.add)
            lo = NH if isctx else 0
            K.ts(K.dve, rh[:, 0:nt, lo:2 * NH], ssh[:, 0:nt, lo:2 * NH], 1.0 / QK, EPS, ALU.mult, ALU.add)
            K.actf(rh[:, 0:nt, lo:2 * NH], rh[:, 0:nt, lo:2 * NH], AF.Sqrt)
            K.recip(rh[:, 0:nt, lo:2 * NH], rh[:, 0:nt, lo:2 * NH])

            def do_rope(ti, Rv, outv):
                C = rope.v(bc(rope.t[:, ti, 0:32].unsqueeze(1), [128, NH, 32]))
                K.tt(K.dve, T1[:, :, :], Rv, C, ALU.mult)
                R5 = Rv.ap.rearrange("p h (a b c) -> p h a b c", a=2, b=2)
                U5 = U.t[:, :, :].rearrange("p h (a b c) -> p h a b c", a=2, b=2)
                S5 = rope.t[:, ti, 32:64].rearrange("p (a b c) -> p a b c", a=2, b=2)
                for hb_ in range(2):
                    K.tt(K.dve, U.v(U5[:, :, :, hb_, :]), View(Rv.buf, R5[:, :, :, 1 - hb_, :]),
                         rope.v(bc(S5[:, :, hb_, :].unsqueeze(1), [128, NH, 2, 8])), ALU.mult)
                K.tt(K.dve, outv, T1[:, :, :], U[:, :, :], ALU.add)

            for i in range(nt):
                qf, kvf = qfs[i], kvfs[i]
                q_, k_, v_ = qb[i], kb[i], vb[i]
                K.cp(K.pool, v_[:, :, 0:VD], kvf[:, :, 64:128])
                if not isctx:
                    rq = rh.v(bc(rh.t[:, i, 0:NH].unsqueeze(2), [128, NH, QK]))
                    K.tt(K.dve, qf[:, :, :], qf[:, :, :], rq, ALU.mult)
                    K.tt(K.dve, q_[:, :, 0:64], qf[:, :, 0:64], QG.v(bc(QG.t[:, 0:64].unsqueeze(1), [128, NH, 64])), ALU.mult)
                    K.tt(K.dve, R[:, :, :], qf[:, :, 64:96], QG.v(bc(QG.t[:, 64:96].unsqueeze(1), [128, NH, 32])), ALU.mult)
                    do_rope(tb + i, R[:, :, :], q_[:, :, 64:96])
                rk64 = rh.v(bc(rh.t[:, i, NH:2 * NH].unsqueeze(2), [128, NH, 64]))
                rk32 = rh.v(bc(rh.t[:, i, NH:2 * NH].unsqueeze(2), [128, NH, 32]))
                K.tt(K.dve, sq[:, :, 0:64], kvf[:, :, 0:64], rk64, ALU.mult)
                K.tt(K.dve, k_[:, :, 0:64], sq[:, :, 0:64], KG.v(bc(KG.t[:, 0:64].unsqueeze(1), [128, NH, 64])), ALU.mult)
                K.tt(K.dve, R[:, :, :], krs[i].v(bc(krs[i].t[:, :].unsqueeze(1), [128, NH, 32])), rk32, ALU.mult)
                if isctx:
                    K.tt(K.dve, k_[:, :, 64:96], R[:, :, :], KG.v(bc(KG.t[:, 64:96].unsqueeze(1), [128, NH, 32])), ALU.mult)
                else:
                    K.tt(K.dve, R[:, :, :], R[:, :, :], KG.v(bc(KG.t[:, 64:96].unsqueeze(1), [128, NH, 32])), ALU.mult)
                    do_rope(tb + i, R[:, :, :], k_[:, :, 64:96])
            for i in range(nt):
                cs = slice(i * 128, (i + 1) * 128)
                q_, k_, v_ = qb[i], kb[i], vb[i]
                if not isctx:
                    for h in range(NH):
                        K.tr(pTq[0:QK, h, :], q_[:, h, :], ident[:, :], sig=(h == NH - 1))
                    K.cp(K.act, qs[0:QK, :, cs], pTq[0:QK, :, :])
                for h in range(NH):
                    K.tr(pTk[0:QK, h, :], k_[:, h, :], ident[:, :], sig=(h == NH - 1))
                K.cp(K.act, ks[0:QK, :, cs], pTk[0:QK, :, :])
                key = b["key0"] + 128 * i
                K.dma(K.sp, vE[:, key // 128, :, :], v_[:, :, :])
            if not isctx:
                K.dma(K.sp, qT[:, :, b["t0"]:b["t0"] + 512].rearrange("h d t -> d h t"), qs[0:QK, :, :])
            K.dma(K.sp, kT[:, :, b["key0"]:b["key0"] + Wd].rearrange("h d t -> d h t"), ks[0:QK, :, 0:Wd])
        K.barrier()


def phase_attn(K, I, S, qT, kT, vE, attT):
    nc = K.nc
    NK = S + CTX
    NT = NK // 128
    with contextlib.ExitStack() as st:
        kTh = [K.sb(st, [128, NK], BF16, "kTh") for _ in range(2)]
        vEh = [K.sb(st, [128, NT, VD + 1], BF16, "vEh") for _ in range(2)]
        qTh = [K.sb(st, [128, 512], BF16, "qTh") for _ in range(2)]
        pt = [K.sb(st, [128, 1024], BF16, "pt") for _ in range(3)]
        osb = [K.sb(st, [128, 512], F32, "osb") for _ in range(2)]
        rd = [K.sb(st, [128, 512], F32, "rd") for _ in range(2)]
        atts = [K.sb(st, [128, 512], BF16, "atts") for _ in range(2)]
        ones = K.sb(st, [128, 64], F32, "ones")
        K.memset(K.dve, ones[:, :], 0.0)
        K.memset(K.dve, ones[64:65, :], 1.0)
        for r__ in rd:
            K.memset(K.dve, r__[:, :], 0.0)
        ps = [K.ps(st, [128, 1024], F32, "ps") for _ in range(2)]
        po = [K.ps(st, [128, 512], F32, "po") for _ in range(2)]
        pb = K.ps(st, [128, 512], F32, "pb")
        npair = NT // 2
        units = [(h, qbk) for h in range(NH) for qbk in range(S // 512)]
        jobs = [(u, pr) for u in range(len(units)) for pr in range(npair)]

        def load_head(h):
            K.dma(K.sp, kTh[h % 2][0:QK, :], kT[h, :, :])
            K.dma(K.sp, vEh[h % 2][:, :, :], vE[:, :, h, :])

        def load_q(u):
            h, qbk = units[u]
            K.dma(K.sp, qTh[u % 2][0:QK, :], qT[h, :, qbk * 512:(qbk + 1) * 512])

        def S_(g):
            u, pr = jobs[g]
            h, qbk = units[u]
            if pr == 0 and u + 1 < len(units):
                if units[u + 1][0] != h:
                    load_head(h + 1)
                load_q(u + 1)
            kk, qq, p_ = kTh[h % 2], qTh[u % 2], ps[g % 2]
            for j in range(2):
                kt = 2 * pr + j
                K.mm(p_[:, j * 512:(j + 1) * 512], kk[0:QK, kt * 128:(kt + 1) * 128], qq[0:QK, :],
                     start=True, stop=True, sig=(j == 1))

        def E_(g):
            K.actf(pt[g % 3][:, :], ps[g % 2][:, :], AF.Exp, scale=SM_SCALE)

        def PV_(g):
            u, pr = jobs[g]
            h, qbk = units[u]
            vv, e_, o_ = vEh[h % 2], pt[g % 3], po[u % 2]
            for j in range(2):
                kt = 2 * pr + j
                K.mm(o_[0:VD + 1, :], vv[:, kt, :], e_[:, j * 512:(j + 1) * 512],
                     start=(kt == 0), stop=(kt == NT - 1), sig=(j == 1))

        def norm1(u):
            ob, r_, o_ = osb[u % 2], rd[u % 2], po[u % 2]
            K.cp(K.dve, ob[0:VD + 1, :], o_[0:VD + 1, :])
            K.recip(r_[64:65, :], ob[64:65, :])

        def norm2(u):
            h, qbk = units[u]
            ob, r_, a_ = osb[u % 2], rd[u % 2], atts[u % 2]
            K.mm(pb[0:VD, :], ones[0:VD + 1, 0:VD], r_[0:VD + 1, :], start=True, stop=True)
            K.tt(K.dve, a_[0:VD, :], ob[0:VD, :], pb[0:VD, :], ALU.mult)
            K.dma(K.pool, attT[h * VD:(h + 1) * VD, qbk * 512:(qbk + 1) * 512], a_[0:VD, :])

        load_head(0)
        load_q(0)
        S_(0)
        pending = None
        for g in range(len(jobs)):
            if g + 1 < len(jobs):
                S_(g + 1)
            E_(g)
            PV_(g)
            if pending is not None:
                norm2(pending)
                pending = None
            if jobs[g][1] == npair - 1:
                norm1(jobs[g][0])
                pending = jobs[g][0]
        if pending is not None:
            norm2(pending)
        K.barrier()


def phase_l0c(K, I, S, modrow, uT, attT, x_in, x_out):
    nc = K.nc
    HALO = CONV_K // 2
    with contextlib.ExitStack() as st:
        wout = K.sb(st, [128, 8, D], BF16, "evwout")
        cw = K.sb(st, [128, 4, CONV_K], F32, "cw31")
        cb = K.sb(st, [128, 4], F32, "cb31")
        lg = K.sb(st, [128, 4], F32, "lng")
        lb = K.sb(st, [128, 4], F32, "lnb")
        onesF = K.sb(st, [128, 128], F32, "onesF")
        identb = K.sb(st, [128, 128], BF16, "identb")
        identF = K.sb(st, [128, 128], F32, "identF")
        diag = K.sb(st, [128, 4, CONV_K, 128], BF16, "diag")
        K.cload(identb[:, :], I["ident"])
        for kk in range(CONV_K):
            K.cload(cw.v(cw.t[:, :, kk:kk + 1]), colvec(I["ev_conv_w"][0, kk, :], 4), allow_slow_non_contiguous=True)
        K.cload(cb.v(cb.t[:, :].rearrange("p (k o) -> p k o", o=1)), colvec(I["ev_conv_b"][0, :], 4))
        K.cload(lg.v(lg.t[:, :].rearrange("p (k o) -> p k o", o=1)), colvec(I["ev_ln_g"][0, :], 4))
        K.cload(lb.v(lb.t[:, :].rearrange("p (k o) -> p k o", o=1)), colvec(I["ev_ln_b"][0, :], 4))
        with contextlib.ExitStack() as st2:
            G = K.sb(st2, [128, D], F32, "G")
            stg = [K.sb(st2, [128, 1024], F32, "wstg") for _ in range(3)]
            K.cload(G[:, :], modrow[2 * D:3 * D].partition_broadcast(128))
            K.setup_done()
            K.memset(K.dve, onesF[:, :], 1.0 / CONV_CH)
            K.ts(K.dve, cw[:, :, :], cw[:, :, :], 0.5, None, ALU.mult)
            K.cp(K.dve, identF[:, :], identb[:, :])
            for m in range(4):
                for kk in range(CONV_K):
                    K.ts(K.dve, diag[:, m, kk, :], identF[:, :], cw[:, m, kk:kk + 1], None, ALU.mult)
            load_cast(K, st2, wout, I["ev_w_out"][0], 8, D, colscale=G, stg=stg)
            K.barrier()
        uw = [K.sb(st, [128, 4, 512 + 2 * HALO], F32, "uw") for _ in range(2)]
        uwa = [K.sb(st, [128, 4, 544], BF16, "uwa") for _ in range(2)]
        uwb = [K.sb(st, [128, 4, 544], BF16, "uwb") for _ in range(2)]
        pcv = [K.ps(st, [128, 512], F32, "pcv") for _ in range(2)]
        accs = [K.sb(st, [128, 4, 512], F32, "acc31") for _ in range(2)]
        sqbs = [K.sb(st, [128, 4, 512], F32, "sq31") for _ in range(2)]
        mean = K.sb(st, [128, 512], F32, "mean")
        rs = K.sb(st, [128, 512], F32, "rs")
        aTs = [K.sb(st, [128, 4, 512], BF16, "aT") for _ in range(2)]
        at = [K.sb(st, [128, 4, 512], BF16, "attblk") for _ in range(2)]
        xr = [K.sb(st, [128, D], F32, "xr") for _ in range(2)]
        pm = K.ps(st, [128, 512], F32, "pmean")
        pq = K.ps(st, [128, 512], F32, "pmsq")
        po = [K.ps(st, [128, 512], F32, "po") for _ in range(2)]
        nblk = S // 512
        cnt = [0]

        def prep(j):
            t0 = 512 * j
            w_ = uw[j % 2]
            lo = max(t0 - HALO, 0)
            hi = min(t0 + 512 + HALO, S)
            if lo != t0 - HALO or hi != t0 + 512 + HALO:
                K.memset(K.pool, w_[:, :, :], 0.0)
            c0_ = lo - (t0 - HALO)
            K.dma(K.sp, w_[:, :, c0_:c0_ + (hi - lo)], uT[:, lo:hi].rearrange("(m p) t -> p m t", p=128))
            K.cp(K.act, uwa[j % 2][:, :, 0:542], w_[:, :, :])
            K.cp(K.dve, uwb[j % 2][:, :, 1:543], w_[:, :, :])

        def load_at(j):
            t0 = 512 * j
            K.dma(K.sp, at[j % 2][:, :, :], attT[:, t0:t0 + 512].rearrange("(m p) t -> p m t", p=128))

        def convmm(j):
            wa, wb, acc, sqb = uwa[j % 2], uwb[j % 2], accs[j % 2], sqbs[j % 2]
            for m in range(4):
                pc_ = pcv[m % 2]
                for kk in range(CONV_K):
                    mv = wa[:, m, kk:kk + 512] if kk % 2 == 0 else wb[:, m, kk + 1:kk + 513]
                    K.mm(pc_[:, :], diag[:, m, kk, :], mv, start=(kk == 0), stop=(kk == CONV_K - 1))
                K.ts(K.dve, acc[:, m, :], pc_[:, :], cb[:, m:m + 1], None, ALU.add)
                K.actf(sqb[:, m, :], acc[:, m, :], AF.Square)

        def stats_mm(j):
            acc, sqb = accs[j % 2], sqbs[j % 2]
            for m in range(4):
                K.mm(pm[:, :], onesF[:, :], acc[:, m, :], start=(m == 0), stop=(m == 3))
            for m in range(4):
                K.mm(pq[:, :], onesF[:, :], sqb[:, m, :], start=(m == 0), stop=(m == 3))

        def chain(j):
            acc, aT = accs[j % 2], aTs[j % 2]
            K.cp(K.act, mean[:, :], pm[:, :])
            K.tt(K.dve, rs[:, :], mean[:, :], mean[:, :], ALU.mult)
            K.tt(K.dve, rs[:, :], pq[:, :], rs[:, :], ALU.subtract)
            K.ts(K.dve, rs[:, :], rs[:, :], EPS, None, ALU.add)
            K.actf(rs[:, :], rs[:, :], AF.Sqrt)
            K.recip(rs[:, :], rs[:, :])
            for m in range(4):
                K.tt(K.dve, acc[:, m, :], acc[:, m, :], mean[:, :], ALU.subtract)
                K.tt(K.dve, acc[:, m, :], acc[:, m, :], rs[:, :], ALU.mult)
                K.actf(aT[:, m, :], acc[:, m, :], AF.Silu, scale=lg[:, m:m + 1], bias=lb[:, m:m + 1])

        def outp(j):
            t0 = 512 * j
            aT, a_ = aTs[j % 2], at[j % 2]
            for i in range(4):
                x = xr[cnt[0] % 2]
                cnt[0] += 1
                K.dma(K.sp, x[:, :], x_in[t0 + 128 * i:t0 + 128 * (i + 1), :])
                cs = slice(128 * i, 128 * (i + 1))
                for n_ in range(2):
                    for k in range(8):
                        l_ = aT[:, k, cs] if k < 4 else a_[:, k - 4, cs]
                        K.mm(po[n_][:, :], l_, wout[:, k, n_ * 512:(n_ + 1) * 512], start=(k == 0), stop=(k == 7))
                for n_ in range(2):
                    K.tt(K.dve, x[:, n_ * 512:(n_ + 1) * 512], x[:, n_ * 512:(n_ + 1) * 512], po[n_][:, :], ALU.add)
                K.dma(K.pool, x_out[t0 + 128 * i:t0 + 128 * (i + 1), :], x[:, :])

        prep(0)
        load_at(0)
        convmm(0)
        if nblk > 1:
            prep(1)
            load_at(1)
        for j in range(nblk):
            stats_mm(j)
            chain(j)
            if j + 1 < nblk:
                convmm(j + 1)
            if j + 2 < nblk:
                prep(j + 2)
            outp(j)
            if j + 2 < nblk:
                load_at(j + 2)
        K.barrier()


WEIGHT_NAMES = ["ada_w", "ada_b", "norm_mix_g", "norm_ffn_g", "ffn_w_up", "ffn_conv_w", "ffn_conv_b",
                "ffn_w_down", "ev_w_in", "ev_conv_w", "ev_conv_b", "ev_ln_g", "ev_ln_b", "ev_qa_norm_g", "ev_w_uq",
                "ev_kva_norm_g", "ev_w_ukv", "ev_q_norm_g", "ev_k_norm_g", "ev_w_out", "od_w_in", "od_conv_w",
                "od_conv_b", "od_w_out"]


def build(S, shapes, debug=False, phases=("mod", "a", "b", "c", "f0", "m1", "f1")):
    nc = bass.Bass("TRN2", target_bir_lowering=False)
    I = {}
    I["x"] = nc.dram_tensor("x", [S, D], F32, kind="ExternalInput").ap()
    I["c"] = nc.dram_tensor("c", [D], F32, kind="ExternalInput").ap()
    I["ctx"] = nc.dram_tensor("ctx", [CTX, D], F32, kind="ExternalInput").ap()
    I["c_ctx"] = nc.dram_tensor("c_ctx", [D], F32, kind="ExternalInput").ap()
    for n in WEIGHT_NAMES:
        I[n] = nc.dram_tensor(n, list(shapes[n]), F32, kind="ExternalInput").ap()
    I["ident"] = nc.dram_tensor("ident", [128, 128], BF16, kind="ExternalInput").ap()
    I["rope"] = nc.dram_tensor("rope", [S, 64], F32, kind="ExternalInput").ap()
    y = nc.dram_tensor("y", [S, D], F32, kind="ExternalOutput").ap()
    sk = "ExternalOutput" if debug else "Internal"
    NK = S + CTX
    modv = nc.dram_tensor("modv", [2, 6 * D], F32, kind=sk).ap()
    modc = nc.dram_tensor("modc", [1, 2 * D], F32, kind=sk).ap()
    uT = nc.dram_tensor("uT", [CONV_CH, S], F32, kind=sk).ap()
    qT = nc.dram_tensor("qT", [NH, QK, S], BF16, kind=sk).ap()
    kT = nc.dram_tensor("kT", [NH, QK, NK], BF16, kind=sk).ap()
    vE = nc.dram_tensor("vE", [128, NK // 128, NH, VD + 1], BF16, kind=sk).ap()
    attT = nc.dram_tensor("attT", [NH * VD, S], BF16, kind=sk).ap()
    xa = nc.dram_tensor("xa", [S, D], F32, kind=sk).ap()
    xb = nc.dram_tensor("xb", [S, D], F32, kind=sk).ap()
    K = Kern(nc)
    with K.es:
        if "mod" in phases:
            phase_mod(K, I, modv, modc)
        if "a" in phases:
            phase_l0a(K, I, S, modv[0, :], modc[0, :], uT, qT, kT, vE)
        if "b" in phases:
            phase_attn(K, I, S, qT, kT, vE, attT)
        if "c" in phases:
            phase_l0c(K, I, S, modv[0, :], uT, attT, I["x"], xa)
        if "f0" in phases:
            phase_ffn(K, I, 0, S, xa, xb, modv[0, :])
        if "m1" in phases:
            phase_sconv(K, I, S, xb, xa, modv[1, :])
        if "f1" in phases:
            phase_ffn(K, I, 1, S, xa, y, modv[1, :])
        K.barrier()
    return nc


def rope_table(S):
    t = np.arange(S)
    row = (t // 64).astype(np.float32)
    col = (t % 64).astype(np.float32)
    half = 16
    inv = (10000.0 ** (-np.arange(0, half, 2, dtype=np.float32) / half)).astype(np.float32)
    ar = row[:, None] * inv[None, :]
    ac = col[:, None] * inv[None, :]
    cr, sr, cc, sc = np.cos(ar), np.sin(ar), np.cos(ac), np.sin(ac)
    C = np.concatenate([cr, cr, cc, cc], axis=1)
    Sg = np.concatenate([-sr, sr, -sc, sc], axis=1)
    return np.ascontiguousarray(np.concatenate([C, Sg], axis=1).astype(np.float32))


def kernel(debug=False, phases=("mod", "a", "b", "c", "f0", "m1", "f1"), **inputs):
    x = np.asarray(inputs["x"], dtype=np.float32)
    B, S, _ = x.shape
    assert B == 8
    shapes = {n: np.asarray(inputs[n]).shape for n in WEIGHT_NAMES}
    nc = build(S, shapes, debug=debug, phases=phases)
    ident = np.eye(128, dtype=np.float32).astype(ml_dtypes.bfloat16)
    rope = rope_table(S)
    shared = {n: np.ascontiguousarray(np.asarray(inputs[n], dtype=np.float32)) for n in WEIGHT_NAMES}
    shared["c_ctx"] = np.ascontiguousarray(np.asarray(inputs["c_ctx"], dtype=np.float32))
    shared["ident"] = ident
    shared["rope"] = rope
    in_maps = []
    for b in range(B):
        m = dict(shared)
        m["x"] = np.ascontiguousarray(x[b])
        m["c"] = np.ascontiguousarray(np.asarray(inputs["c"], dtype=np.float32)[b])
        m["ctx"] = np.ascontiguousarray(np.asarray(inputs["ctx"], dtype=np.float32)[b])
        in_maps.append(m)
    res = run_bass_kernel_spmd(nc, in_maps, core_ids=list(range(B)))
    if debug:
        return res
    return np.stack([np.asarray(r["y"], dtype=np.float32) for r in res.results], axis=0)
```
